# Optimizing a Trainium2 kernel written in Bass

```python
import jax
import jax.numpy as jnp
from jax import lax
import numpy as np

D_MODEL = 1024
BATCH = 8
SEQ = 4096
DEPTH = 4

N_EVEN = (DEPTH + 1) // 2
N_ODD = DEPTH // 2
D_FF = 4 * D_MODEL
RMS_EPS = 1e-6
NEG_BIG = -1e30
F_MIN = 1e-30
A_WIDTH = D_MODEL // 2
A_HEAD_DIM = 128
A_HEADS = A_WIDTH // A_HEAD_DIM
A_CHUNK = 64
B_WIDTH = D_MODEL // 2
B_BLOCKS = 8
B_BLOCK_DIM = B_WIDTH // B_BLOCKS
B_CONV = 4
RG_C = 8.0
C_HEAD_DIM = 64
C_HEADS = D_MODEL // C_HEAD_DIM
C_KV_HEADS = 4
C_GROUP = C_HEADS // C_KV_HEADS
WINDOW = 128
ROPE_THETA = 10000.0
EVEN_IN = 4 * A_WIDTH + 2 * B_WIDTH
EVEN_SPLITS = (A_WIDTH, 2 * A_WIDTH, 3 * A_WIDTH, 4 * A_WIDTH, 4 * A_WIDTH + B_WIDTH)
ODD_IN = (C_HEADS + 2 * C_KV_HEADS) * C_HEAD_DIM
ODD_SPLITS = (C_HEADS * C_HEAD_DIM, (C_HEADS + C_KV_HEADS) * C_HEAD_DIM)

kernel_name = 'hybrid_hgrn2_rglru_swa_sinks_trunk'


def _rmsnorm(x, gain):
    xf = x.astype(jnp.float32)
    y = xf * lax.rsqrt(jnp.mean(xf * xf, axis=-1, keepdims=True) + RMS_EPS)
    return (y * gain.astype(jnp.float32)).astype(x.dtype)


def _headnorm(x, gain):
    xf = x.astype(jnp.float32)
    y = xf * lax.rsqrt(jnp.mean(xf * xf, axis=-1, keepdims=True) + RMS_EPS)
    return y * gain.astype(jnp.float32)


def _hgrn2(q, fx, v, lb):
    bsz, s = q.shape[0], q.shape[1]
    lb = lb.astype(jnp.float32)
    fx = fx.astype(jnp.float32)
    f = lb + (1.0 - lb) * jax.nn.sigmoid(fx)
    log_f = jnp.log(jnp.maximum(f, F_MIN))
    k = (1.0 - lb) * jax.nn.sigmoid(-fx)
    nc = s // A_CHUNK

    def to_chunks(t):
        return t.astype(jnp.float32).reshape(bsz, nc, A_CHUNK, A_HEADS, A_HEAD_DIM).transpose(1, 0, 3, 2, 4)

    causal = jnp.tril(jnp.ones((A_CHUNK, A_CHUNK), dtype=bool))[:, :, None]

    def step(state, inp):
        qc, kc, gc, vc = inp
        b = jnp.cumsum(gc, axis=2)
        o_inter = jnp.einsum('bhtd,bhde->bhte', qc * jnp.exp(b), state)
        diff = b[:, :, :, None, :] - b[:, :, None, :, :]
        decay = jnp.exp(jnp.where(causal, diff, NEG_BIG))
        scores = jnp.einsum('bhtsd,bhsd->bhts', qc[:, :, :, None, :] * decay, kc)
        o = o_inter + jnp.einsum('bhts,bhse->bhte', scores, vc)
        b_last = b[:, :, -1, :]
        state = jnp.exp(b_last)[..., None] * state + jnp.einsum(
            'bhsd,bhse->bhde', kc * jnp.exp(b_last[:, :, None, :] - b), vc)
        return state, o

    s0 = jnp.zeros((bsz, A_HEADS, A_HEAD_DIM, A_HEAD_DIM), jnp.float32)
    _, o = lax.scan(step, s0, (to_chunks(q), to_chunks(k), to_chunks(log_f), to_chunks(v)))
    return o.transpose(1, 0, 3, 2, 4).reshape(bsz, s, A_HEADS, A_HEAD_DIM)


def _rglru(xb, conv_w, conv_b, wa, ba, wx, bx, lam):
    bsz, s, _ = xb.shape
    xc = lax.conv_general_dilated(
        xb, conv_w[:, None, :].astype(xb.dtype), window_strides=(1,),
        padding=[(B_CONV - 1, 0)], dimension_numbers=('NWC', 'WIO', 'NWC'),
        feature_group_count=B_WIDTH) + conv_b.astype(xb.dtype)
    xf = xc.astype(jnp.float32)
    xblk = xf.reshape(bsz, s, B_BLOCKS, B_BLOCK_DIM)
    r = jax.nn.sigmoid(jnp.einsum('bsni,nij->bsnj', xblk, wa.astype(jnp.float32)).reshape(bsz, s, B_WIDTH)
                       + ba.astype(jnp.float32))
    i = jax.nn.sigmoid(jnp.einsum('bsni,nij->bsnj', xblk, wx.astype(jnp.float32)).reshape(bsz, s, B_WIDTH)
                       + bx.astype(jnp.float32))
    log_a = -RG_C * jax.nn.softplus(-lam.astype(jnp.float32)) * r
    a = jnp.exp(log_a)
    u = jnp.sqrt(jnp.maximum(-jnp.expm1(2.0 * log_a), 0.0)) * (i * xf)

    def combine(left, right):
        a1, u1 = left
        a2, u2 = right
        return a1 * a2, a2 * u1 + u2

    _, h = lax.associative_scan(combine, (a, u), axis=1)
    return h


def _rope(x, cos, sin):
    x1, x2 = jnp.split(x, 2, axis=-1)
    return jnp.concatenate([x1 * cos - x2 * sin, x2 * cos + x1 * sin], axis=-1)


def _swa_sinks(q, k, v, sinks):
    bsz, s = q.shape[0], q.shape[1]
    nb = s // WINDOW
    qb = q.reshape(bsz, nb, WINDOW, C_KV_HEADS, C_GROUP, C_HEAD_DIM)

    def band(t):
        tb = t.reshape(bsz, nb, WINDOW, C_KV_HEADS, C_HEAD_DIM)
        prev = jnp.pad(tb[:, :-1], ((0, 0), (1, 0), (0, 0), (0, 0), (0, 0)))
        return jnp.concatenate([prev, tb], axis=2)

    kb, vb = band(k), band(v)
    scores = jnp.einsum('bnqhgd,bnkhd->bnhgqk', qb, kb) * (C_HEAD_DIM ** -0.5)
    qi = jnp.arange(WINDOW)[:, None]
    ki = jnp.arange(2 * WINDOW)[None, :]
    rel = qi + WINDOW - ki
    in_band = (rel >= 0) & (rel < WINDOW)
    valid = (jnp.arange(nb) > 0)[:, None, None] | (ki >= WINDOW)[None]
    mask = in_band[None] & valid
    scores = jnp.where(mask[None, :, None, None], scores, NEG_BIG)
    sink = sinks.astype(jnp.float32).reshape(1, 1, C_KV_HEADS, C_GROUP, 1, 1)
    m = jnp.maximum(jnp.max(scores, axis=-1, keepdims=True), sink)
    p = jnp.exp(scores - m)
    denom = jnp.sum(p, axis=-1, keepdims=True) + jnp.exp(sink - m)
    out = jnp.einsum('bnhgqk,bnkhd->bnqhgd', p / denom, vb)
    return out.reshape(bsz, s, C_HEADS * C_HEAD_DIM)


def setup_inputs(seed: int = 0) -> dict:
    key = jax.random.key(seed)
    ks = jax.random.split(key, 24)
    f32 = jnp.float32

    def w(k, shape, fan_in):
        return jax.random.normal(k, shape, f32) * (fan_in ** -0.5)

    def gain(k, shape):
        return 1.0 + 0.02 * jax.random.normal(k, shape, f32)

    def bias(k, shape):
        return 0.01 * jax.random.normal(k, shape, f32)

    x = jax.random.normal(ks[0], (BATCH, SEQ, D_MODEL), f32)
    offsets = jax.random.randint(ks[1], (BATCH, 1), 0, 4096, dtype=jnp.int32)
    positions = offsets + jnp.arange(SEQ, dtype=jnp.int32)[None, :]
    a_init = jax.random.uniform(ks[17], (N_EVEN, B_WIDTH), f32, minval=0.9, maxval=0.999)
    return {
        'x': x,
        'positions': positions,
        'norm_mix': gain(ks[2], (DEPTH, D_MODEL)),
        'norm_mlp': gain(ks[3], (DEPTH, D_MODEL)),
        'w_mlp_in': w(ks[4], (DEPTH, D_MODEL, D_FF), D_MODEL),
        'w_mlp_out': w(ks[5], (DEPTH, D_FF, D_MODEL), D_FF),
        'w_in_even': w(ks[6], (N_EVEN, D_MODEL, EVEN_IN), D_MODEL),
        'w_out_even': w(ks[7], (N_EVEN, A_WIDTH + B_WIDTH, D_MODEL), A_WIDTH + B_WIDTH),
        'hgrn_lb_logits': 0.1 * jax.random.normal(ks[8], (N_EVEN, A_WIDTH), f32),
        'hgrn_out_norm': gain(ks[9], (N_EVEN, A_WIDTH)),
        'conv_w': w(ks[10], (N_EVEN, B_CONV, B_WIDTH), B_CONV),
        'conv_b': bias(ks[11], (N_EVEN, B_WIDTH)),
        'rg_wa': w(ks[12], (N_EVEN, B_BLOCKS, B_BLOCK_DIM, B_BLOCK_DIM), B_BLOCK_DIM),
        'rg_ba': bias(ks[13], (N_EVEN, B_WIDTH)),
        'rg_wx': w(ks[14], (N_EVEN, B_BLOCKS, B_BLOCK_DIM, B_BLOCK_DIM), B_BLOCK_DIM),
        'rg_bx': bias(ks[15], (N_EVEN, B_WIDTH)),
        'rg_lambda': jnp.log(a_init) - jnp.log1p(-a_init),
        'w_in_odd': w(ks[16], (N_ODD, D_MODEL, ODD_IN), D_MODEL),
        'w_out_odd': w(ks[18], (N_ODD, C_HEADS * C_HEAD_DIM, D_MODEL), C_HEADS * C_HEAD_DIM),
        'q_norm': gain(ks[19], (N_ODD, C_HEAD_DIM)),
        'k_norm': gain(ks[20], (N_ODD, C_HEAD_DIM)),
        'sinks': 0.5 * jax.random.normal(ks[21], (N_ODD, C_HEADS), f32),
    }


def reference(x, positions, norm_mix, norm_mlp, w_mlp_in, w_mlp_out, w_in_even, w_out_even,
              hgrn_lb_logits, hgrn_out_norm, conv_w, conv_b, rg_wa, rg_ba, rg_wx, rg_bx, rg_lambda,
              w_in_odd, w_out_odd, q_norm, k_norm, sinks):
    bsz, s, _ = x.shape
    sm = jax.nn.softmax(hgrn_lb_logits.astype(jnp.float32), axis=0)
    lower_bounds = jnp.cumsum(sm, axis=0) - sm[0]
    inv_freq = ROPE_THETA ** (-jnp.arange(0, C_HEAD_DIM, 2, dtype=jnp.float32) / C_HEAD_DIM)
    ang = positions.astype(jnp.float32)[..., None] * inv_freq
    cos = jnp.cos(ang)[:, :, None, :]
    sin = jnp.sin(ang)[:, :, None, :]

    for layer in range(DEPTH):
        h = _rmsnorm(x, norm_mix[layer])
        if layer % 2 == 0:
            e = layer // 2
            z = h @ w_in_even[e]
            qa, fa, va, ga, gate_b, xb = jnp.split(z, EVEN_SPLITS, axis=-1)
            heads = lambda t: t.reshape(bsz, s, A_HEADS, A_HEAD_DIM)
            oa = _hgrn2(heads(qa), heads(fa), heads(va), lower_bounds[e].reshape(A_HEADS, A_HEAD_DIM))
            oa = _headnorm(oa, hgrn_out_norm[e].reshape(A_HEADS, A_HEAD_DIM)).reshape(bsz, s, A_WIDTH)
            ya = (oa * jax.nn.silu(ga.astype(jnp.float32))).astype(x.dtype)
            hb = _rglru(xb, conv_w[e], conv_b[e], rg_wa[e], rg_ba[e], rg_wx[e], rg_bx[e], rg_lambda[e])
            yb = (hb * jax.nn.gelu(gate_b.astype(jnp.float32))).astype(x.dtype)
            mix = jnp.concatenate([ya, yb], axis=-1) @ w_out_even[e]
        else:
            o = layer // 2
            z = h @ w_in_odd[o]
            q, k, v = jnp.split(z, ODD_SPLITS, axis=-1)
            q = _rope(_headnorm(q.reshape(bsz, s, C_HEADS, C_HEAD_DIM), q_norm[o]), cos, sin)
            k = _rope(_headnorm(k.reshape(bsz, s, C_KV_HEADS, C_HEAD_DIM), k_norm[o]), cos, sin)
            v = v.reshape(bsz, s, C_KV_HEADS, C_HEAD_DIM).astype(jnp.float32)
            attn = _swa_sinks(q, k, v, sinks[o])
            mix = attn.astype(x.dtype) @ w_out_odd[o]
        x = x + mix
        h = _rmsnorm(x, norm_mlp[layer])
        x = x + jnp.square(jax.nn.relu(h @ w_mlp_in[layer])) @ w_mlp_out[layer]
    return x
```

```python
import math
from contextlib import ExitStack

import numpy as np
import concourse.bass as bass
import concourse.mybir as mybir
from concourse.bass_utils import run_bass_kernel_spmd

F32 = mybir.dt.float32
BF16 = mybir.dt.bfloat16
I32 = mybir.dt.int32
AF = mybir.ActivationFunctionType
ALU = mybir.AluOpType
AX = mybir.AxisListType

GRAN = 256
ENGS = ("pe", "act", "dve", "pool", "sp")


class View:
    __slots__ = ("ap", "toks")

    def __init__(self, ap, toks):
        self.ap = ap
        self.toks = toks

    def __getitem__(self, key):
        return View(self.ap[key], self.toks)


class Arena:
    def __init__(self, tensor, space, nbytes):
        self.t = tensor
        self.space = space
        self.nbytes = nbytes

    def view(self, off, shape, dtype, p0=0, p1=128):
        esz = mybir.dt.size(dtype)
        n = 1
        for s in shape:
            n *= s
        nb = n * esz
        assert off % 4 == 0 and nb % 4 == 0, (off, nb)
        assert off + nb <= self.nbytes, (off, nb, self.nbytes)
        ap = self.t[p0:p1, off // 4:(off + nb) // 4]
        if dtype != F32:
            ap = ap.bitcast(dtype)
        if len(shape) == 2:
            ap = ap.rearrange("p (a b) -> p a b", a=shape[0])
        elif len(shape) == 3:
            ap = ap.rearrange("p (a b c) -> p a b c", a=shape[0], b=shape[1])
        toks = tuple((self.space, g) for g in range(off // GRAN, (off + nb - 1) // GRAN + 1))
        return View(ap, toks)


class Op:
    __slots__ = ("eng", "fn", "reads", "writes", "dma", "idx", "waits", "sig", "sigval", "dsem", "dval", "prewait")

    def __init__(self, eng, fn, reads, writes, dma):
        self.eng = eng
        self.fn = fn
        self.reads = reads
        self.writes = writes
        self.dma = dma
        self.waits = []
        self.sig = False
        self.sigval = 0
        self.dsem = None
        self.dval = 0
        self.prewait = None


def _toks(xs):
    out = []
    for x in xs:
        if x is None:
            continue
        if isinstance(x, View):
            out.extend(x.toks)
        else:
            out.append(x)
    return out


class Prog:
    def __init__(self, nc, n_dma_sems=12):
        self.nc = nc
        self.ops = []
        self.n_dma_sems = n_dma_sems

    def op(self, eng, fn, reads=(), writes=(), dma=False):
        o = Op(eng, fn, _toks(reads), _toks(writes), dma)
        o.idx = len(self.ops)
        self.ops.append(o)
        return o

    def analyze(self):
        last_w = {}
        readers = {}
        per_eng = {e: [] for e in ENGS}
        need = []
        ops = self.ops
        for o in ops:
            deps = set()
            raw = set()
            for t in o.reads:
                w = last_w.get(t)
                if w is not None:
                    deps.add(w)
                    raw.add(w)
            for t in o.writes:
                w = last_w.get(t)
                if w is not None:
                    deps.add(w)
                r = readers.get(t)
                if r:
                    deps.update(r)
            for t in o.writes:
                last_w[t] = o.idx
                readers[t] = []
            for t in o.reads:
                readers.setdefault(t, []).append(o.idx)
            deps.discard(o.idx)
            per_eng[o.eng].append(o)
            nd = []
            for j in deps:
                p = ops[j]
                if p.dma or p.eng != o.eng or o.dma:
                    nd.append(j)
                elif o.eng == "pool" or (o.eng != "pe" and j in raw):
                    nd.append(j)
            for j in nd:
                if not ops[j].dma:
                    ops[j].sig = True
            need.append(nd)
        self.per_eng = per_eng
        self.dma_count = {}
        for e in ENGS:
            c = 0
            n = 0
            for o in per_eng[e]:
                if o.dma:
                    o.dsem = (e, n % self.n_dma_sems)
                    o.dval = 16 * (n // self.n_dma_sems + 1)
                    if n >= self.n_dma_sems:
                        o.prewait = (o.dsem, o.dval - 16)
                    n += 1
                elif o.sig:
                    c += 1
                    o.sigval = c
            self.dma_count[e] = n
        known = {e: {} for e in ENGS}
        for o, nd in zip(ops, need):
            k = known[o.eng]
            w = {}
            if o.prewait is not None:
                s, v = o.prewait
                if k.get(s, 0) < v:
                    w[s] = v
            for j in nd:
                p = ops[j]
                if p.dma:
                    s, v = p.dsem, p.dval
                else:
                    s, v = ("c", p.eng), p.sigval
                if k.get(s, 0) < v and w.get(s, 0) < v:
                    w[s] = v
            for s, v in w.items():
                k[s] = v
            o.waits = list(w.items())

    def emit(self, final_waits_eng="sp"):
        nc = self.nc
        self.analyze()
        with ExitStack() as es:
            sems = {}
            for e in ENGS:
                sems[("c", e)] = es.enter_context(nc.semaphore("c_" + e))
                for i in range(min(self.n_dma_sems, self.dma_count[e])):
                    sems[(e, i)] = es.enter_context(nc.semaphore("d_%s_%d" % (e, i)))
            block = es.enter_context(nc.Block())

            def run(e):
                def body(eng):
                    for o in self.per_eng[e]:
                        for s, v in o.waits:
                            eng.wait_ge(sems[s], v)
                        ins = o.fn(eng)
                        if o.dma:
                            ins.then_inc(sems[o.dsem], 16)
                        elif o.sig:
                            ins.then_inc(sems[("c", e)], 1)
                    if e == final_waits_eng:
                        for e2 in ENGS:
                            n = self.dma_count[e2]
                            for i in range(min(self.n_dma_sems, n)):
                                cnt = (n - i + self.n_dma_sems - 1) // self.n_dma_sems
                                eng.wait_ge(sems[(e2, i)], 16 * cnt)
                return body

            block.tensor(run("pe"))
            block.scalar(run("act"))
            block.vector(run("dve"))
            block.gpsimd(run("pool"))
            block.sync(run("sp"))


D = 1024
SEQ = 4096
DEPTH = 4
DFF = 4096
T = 512
NT = SEQ // T
NSUB = T // 128
EPS = 1e-6
TWO_PI = 2.0 * math.pi

SM_LBL = 0
SM_CONVW = 8
SM_CONVB = 40
SM_BA = 48
SM_BX = 56
SM_LAM = 64
NSM = 72


def build_nc(n_tiles=NT, layers=(0, 1, 2, 3), do_mixer=True, do_mlp=True):
    n_layers = len(layers)
    nc = bass.Bass("TRN2", target_bir_lowering=False)
    dt_in = lambda name, shape, dt=F32: nc.dram_tensor(name, shape, dt, kind="ExternalInput").ap()
    x_d = dt_in("x", [SEQ, D])
    pos_d = dt_in("pos", [128, SEQ // 128], I32)
    nmix_d = dt_in("norm_mix", [DEPTH, D])
    nmlp_d = dt_in("norm_mlp", [DEPTH, D])
    wmi_d = dt_in("w_mlp_in", [DEPTH, D, DFF])
    wmo_d = dt_in("w_mlp_out", [DEPTH, DFF, D])
    wie_d = dt_in("w_in_even", [2, D, 3072])
    woe_d = dt_in("w_out_even", [2, D, D])
    wio_d = dt_in("w_in_odd", [2, D, 1536])
    woo_d = dt_in("w_out_odd", [2, D, D])
    small_d = dt_in("small", [128, NSM])
    hon_d = dt_in("hgrn_out_norm", [2, 512])
    rgwa_d = dt_in("rg_wa", [2, 8, 64, 64])
    rgwx_d = dt_in("rg_wx", [2, 8, 64, 64])
    qn_d = dt_in("q_norm", [2, 64])
    kn_d = dt_in("k_norm", [2, 64])
    sinks_d = dt_in("sinks", [2, 16])
    out_d = nc.dram_tensor("out", [SEQ, D], F32, kind="ExternalOutput").ap()

    def scratch(name, shape):
        return nc.dram_tensor(name, shape, BF16, kind="Internal").ap()

    wmi_b = scratch("wmi_b", [DEPTH, D, DFF])
    wmo_b = scratch("wmo_b", [DEPTH, DFF, D])
    wie_b = scratch("wie_b", [2, D, 3072])
    woe_b = scratch("woe_b", [2, D, D])
    wio_b = scratch("wio_b", [2, D, 1536])
    woo_b = scratch("woo_b", [2, D, D])

    P = Prog(nc)
    with ExitStack() as es:
        SB_BYTES = 206 * 1024
        sb_t = es.enter_context(nc.sbuf_tensor("arena", [128, SB_BYTES // 4], F32))
        ps_t = es.enter_context(nc.psum_tensor("psarena", [128, 4096], F32))
        SB = Arena(sb_t, "S", SB_BYTES)
        PS = Arena(ps_t, "P", 16384)
        cur = [0]

        def alloc(shape, dt, p0=0, p1=128):
            n = 1
            for s_ in shape:
                n *= s_
            nb = (n * mybir.dt.size(dt) + GRAN - 1) // GRAN * GRAN
            v = SB.view(cur[0], shape, dt, p0, p1)
            cur[0] += nb
            return v

        def bank(i, shape=(512,), dt=F32, off=0):
            return PS.view(i * 2048 + off, shape, dt)

        def mm(out, lhsT, rhs, start, stop):
            P.op("pe", lambda e: e.matmul(out.ap, lhsT=lhsT.ap, rhs=rhs.ap, start=start, stop=stop),
                 reads=[lhsT, rhs], writes=[out])

        def tr(out, in_):
            P.op("pe", lambda e: e.transpose(out=out.ap, in_=in_.ap, identity=ident.ap),
                 reads=[in_, ident], writes=[out])

        def act(out, in_, func, bias=None, scale=None, accum=None, rd=()):
            kw = {}
            rds = [in_] + list(rd)
            if bias is not None:
                if isinstance(bias, View):
                    kw["bias"] = bias.ap
                    rds.append(bias)
                else:
                    kw["bias"] = bias
            if scale is not None:
                if isinstance(scale, View):
                    kw["scale"] = scale.ap
                    rds.append(scale)
                else:
                    kw["scale"] = scale
            wr = [out]
            if accum is not None:
                kw["accum_out"] = accum.ap
                wr.append(accum)
            P.op("act", lambda e: e.activation(out=out.ap, in_=in_.ap, func=func, **kw), reads=rds, writes=wr)

        def tt(eng, out, in0, in1, op):
            P.op(eng, lambda e: e.tensor_tensor(out=out.ap, in0=in0.ap, in1=in1.ap, op=op), reads=[in0, in1], writes=[out])

        def ts(eng, out, in0, s1, s2, op0, op1=None):
            rds = [in0]
            a1 = s1
            a2 = s2
            if isinstance(s1, View):
                rds.append(s1)
                a1 = s1.ap
            if isinstance(s2, View):
                rds.append(s2)
                a2 = s2.ap
            if op1 is None:
                P.op(eng, lambda e: e.tensor_scalar(out=out.ap, in0=in0.ap, scalar1=a1, scalar2=None, op0=op0), reads=rds, writes=[out])
            else:
                P.op(eng, lambda e: e.tensor_scalar(out=out.ap, in0=in0.ap, scalar1=a1, scalar2=a2, op0=op0, op1=op1), reads=rds, writes=[out])

        def stt(eng, out, in0, sc, in1, op0, op1):
            rds = [in0, in1]
            a = sc
            if isinstance(sc, View):
                rds.append(sc)
                a = sc.ap
            P.op(eng, lambda e: e.scalar_tensor_tensor(out=out.ap, in0=in0.ap, scalar=a, in1=in1.ap, op0=op0, op1=op1), reads=rds, writes=[out])

        def cp(eng, out, in_):
            if eng == "act":
                P.op("act", lambda e: e.copy(out=out.ap, in_=in_.ap), reads=[in_], writes=[out])
            else:
                P.op(eng, lambda e: e.tensor_copy(out=out.ap, in_=in_.ap), reads=[in_], writes=[out])

        def memset(eng, out, val):
            P.op(eng, lambda e: e.memset(out.ap, val), writes=[out])

        def recip(out, in_):
            P.op("dve", lambda e: e.reciprocal(out=out.ap, in_=in_.ap), reads=[in_], writes=[out])

        def dma(q, out_ap, in_ap, reads, writes):
            P.op(q, lambda e: e.dma_start(out=out_ap, in_=in_ap), reads=reads, writes=writes, dma=True)

        def sigm(out, in_, bias=None, scale=None, neg_bias=None):
            nscale = -1.0 if scale is None else -scale
            act(out, in_, AF.Exp, bias=neg_bias, scale=nscale)
            ts("dve", out, out, 1.0, None, ALU.add)
            recip(out, out)

        def rstd_of(out, tmp, ss, scale):
            act(tmp, ss, AF.Ln, bias=EPS, scale=scale)
            act(out, tmp, AF.Exp, scale=-0.5)

        def bc(v, shape):
            return View(v.ap.to_broadcast(shape), v.toks)

        def bcm(v, shape):
            return View(v.ap.unsqueeze(1).to_broadcast(shape), v.toks)

        xs = [alloc((D,), F32) for _ in range(NSUB)]
        hT = alloc((8, T), BF16)
        yT = alloc((8, T), BF16)
        NRING = 3
        ring = [alloc((8, 1024), BF16) for _ in range(NRING)]
        gbc = [alloc((D,), F32) for _ in range(1)]
        ident = alloc((128,), BF16)
        identf = alloc((128,), F32)
        small = alloc((NSM,), F32)
        stat = alloc((16,), F32)
        hb = [alloc((D,), BF16) for _ in range(2)]

        memset("pool", identf, 0.0)
        P.op("pool", lambda e: e.affine_select(out=identf.ap, in_=identf.ap, compare_op=ALU.not_equal, fill=1.0,
                                               base=0, pattern=[[-1, 128]], channel_multiplier=1),
             reads=[identf], writes=[identf])
        cp("dve", ident, identf)
        dma("sp", small.ap, small_d, [], [small])

        CW = 1024
        NSTG = 3
        stg = [alloc((CW,), F32) for _ in range(NSTG)]
        stgb = [alloc((CW,), BF16) for _ in range(NSTG)]
        conv_state = {"n": 0}

        def conv_chunks(src, dst, name, idx, R, C):
            out = []
            cw = C if C <= CW else (CW if C % CW == 0 else 512)
            for r0 in range(0, R, 128):
                for c0 in range(0, C, cw):
                    def f(r0=r0, c0=c0):
                        k = conv_state["n"] % NSTG
                        ce = ("act", "dve")[conv_state["n"] % 2]
                        conv_state["n"] += 1
                        s_, b_ = stg[k], stgb[k]
                        dma("sp", s_.ap[:, 0:cw], src[idx, r0:r0 + 128, c0:c0 + cw], [], [s_])
                        cp(ce, b_[:, 0:cw], s_[:, 0:cw])
                        dma("sp", dst[idx, r0:r0 + 128, c0:c0 + cw], b_.ap[:, 0:cw], [b_], [("D", name, idx, r0 // 128)])
                    out.append(f)
            return out

        conv_groups = []
        for l in layers:
            g = []
            if do_mixer:
                if l % 2 == 0:
                    g += conv_chunks(wie_d, wie_b, "wie", l // 2, D, 3072)
                    g += conv_chunks(woe_d, woe_b, "woe", l // 2, D, D)
                else:
                    g += conv_chunks(wio_d, wio_b, "wio", l // 2, D, 1536)
                    g += conv_chunks(woo_d, woo_b, "woo", l // 2, D, D)
            conv_groups.append(g)
            g = []
            if do_mlp:
                ci = conv_chunks(wmi_d, wmi_b, "wmi", l, D, DFF)
                co = conv_chunks(wmo_d, wmo_b, "wmo", l, DFF, D)
                g += ci + co
            conv_groups.append(g)
        conv_flat = [f for g in conv_groups for f in g]
        conv_pos = [0]

        def pump(upto_group=None, n=None):
            if upto_group is not None:
                tgt = sum(len(g) for g in conv_groups[:upto_group + 1])
            else:
                tgt = min(len(conv_flat), conv_pos[0] + n)
            while conv_pos[0] < tgt:
                conv_flat[conv_pos[0]]()
                conv_pos[0] += 1

        def wtoks(name, idx, rows):
            return [("D", name, idx, r) for r in rows]

        ring_n = [0]

        def wload(src_b, name, idx, c0, cw, rows=None):
            slot = ring[ring_n[0] % NRING]
            ring_n[0] += 1
            v = slot if cw == 1024 else slot[:, :, 0:cw]
            dma("sp", v.ap, src_b[idx].rearrange("(c p) f -> p c f", p=128)[:, :, c0:c0 + cw],
                wtoks(name, idx, range(8)), [slot])
            return v

        def wload_rows(src_b, name, idx, r0):
            slot = ring[ring_n[0] % NRING]
            ring_n[0] += 1
            dma("sp", slot.ap, src_b[idx, r0 * 128:(r0 + 8) * 128, :].rearrange("(j p) d -> p j d", p=128),
                wtoks(name, idx, range(r0, r0 + 8)), [slot])
            return slot

        gb_n = [0]

        def norm_to_hT(gain_row_ap):
            g = gbc[0]
            gb_n[0] += 1
            dma("sp", g.ap, gain_row_ap.partition_broadcast(128), [], [g])
            for s in range(NSUB):
                ss = stat[:, s:s + 1]
                rs = stat[:, 4 + s:5 + s]
                rr = stat[:, 8 + s:9 + s]
                h_ = hb[s % 2]
                act(h_, xs[s], AF.Square, scale=1.0 / 32.0, accum=ss)
                rstd_of(rr, rs, ss, 1.0)
                stt("dve", h_, xs[s], rr, g, ALU.mult, ALU.mult)
                pT = bank(7, (8, 128), BF16)
                for c in range(8):
                    tr(pT[:, c, :], h_[:, c * 128:(c + 1) * 128])
                cp("act", hT[:, :, s * 128:(s + 1) * 128], pT)

        def v3(v, a):
            return View(v.ap.rearrange("p (a b) -> p a b", a=a), v.toks)

        NBLK = SEQ // 128
        ctab = alloc((NBLK, 32), F32)
        stab = alloc((NBLK, 32), F32)
        gqk = [[alloc((64,), F32) for _ in range(2)] for _ in range(2)]
        esink = [alloc((16,), F32) for _ in range(2)]
        swam = alloc((2, 128), BF16)
        Vb = [[alloc((4, 65), BF16) for _ in range(2)] for _ in range(2)]
        KTr = [[[alloc((4, 128), BF16) for _ in range(2)] for _ in range(2)] for _ in range(2)]
        Sp = [alloc((4, 128), F32) for _ in range(2)]
        gainA = [alloc((512,), F32) for _ in range(2)]
        WBD = alloc((16, 128), BF16)
        onesT = alloc((T,), F32)
        bmask = alloc((128,), BF16)
        ecst = alloc((64,), F32)
        Lcar = alloc((8,), F32)
        hcar = alloc((8,), F32)
        halo = alloc((8, 3), F32)
        SCR = cur[0]

        has_even = do_mixer and any(l % 2 == 0 for l in layers)
        if has_even:
            LB, OML, NOML, CL, CL2, NBA, NBX = 0, 8, 16, 24, 32, 40, 48
            tmpc = alloc((64,), F32)
            l0 = small[:, SM_LBL:SM_LBL + 4]
            l1 = small[:, SM_LBL + 4:SM_LBL + 8]
            mx, e0, e1, sd, sm0, sm1 = (tmpc[:, 4 * i:4 * i + 4] for i in range(6))
            tt("dve", mx, l0, l1, ALU.max)
            tt("dve", e0, l0, mx, ALU.subtract)
            tt("dve", e1, l1, mx, ALU.subtract)
            act(e0, e0, AF.Exp)
            act(e1, e1, AF.Exp)
            tt("dve", sd, e0, e1, ALU.add)
            recip(sd, sd)
            tt("dve", sm0, e0, sd, ALU.mult)
            tt("dve", sm1, e1, sd, ALU.mult)
            tt("dve", ecst[:, LB:LB + 4], sm0, sm0, ALU.subtract)
            tt("dve", mx, sm0, sm1, ALU.add)
            tt("dve", ecst[:, LB + 4:LB + 8], mx, sm0, ALU.subtract)
            ts("dve", ecst[:, OML:OML + 8], ecst[:, LB:LB + 8], -1.0, 1.0, ALU.mult, ALU.add)
            ts("dve", ecst[:, NOML:NOML + 8], ecst[:, LB:LB + 8], -1.0, None, ALU.add)
            sigm(tmpc[:, 32:40], small[:, SM_LAM:SM_LAM + 8])
            act(tmpc[:, 32:40], tmpc[:, 32:40], AF.Ln)
            ts("dve", ecst[:, NBA:NBA + 8], small[:, SM_BA:SM_BA + 8], -1.0, None, ALU.mult)
            ts("dve", ecst[:, NBX:NBX + 8], small[:, SM_BX:SM_BX + 8], -1.0, None, ALU.mult)
            ts("dve", ecst[:, CL:CL + 8], tmpc[:, 32:40], 8.0, None, ALU.mult)
            ts("dve", ecst[:, CL2:CL2 + 8], tmpc[:, 32:40], 16.0, None, ALU.mult)
            wst = alloc((16, 128), F32)
            memset("pool", wst, 0.0)
            for e in range(2):
                for gi_, wsrc in enumerate((rgwa_d, rgwx_d)):
                    for m in range(4):
                        k_ = e * 8 + gi_ * 4 + m
                        dma("sp", wst.ap[0:64, k_, 0:64], wsrc[e, 2 * m], [], [wst])
                        dma("sp", wst.ap[64:128, k_, 64:128], wsrc[e, 2 * m + 1], [], [wst])
            cp("pool", WBD, wst)
            for e in range(2):
                dma("sp", gainA[e].ap, hon_d[e:e + 1, :].partition_broadcast(128), [], [gainA[e]])
                memset("pool", Sp[e], 0.0)
            memset("pool", onesT, 1.0)
            memset("pool", Lcar, 0.0)
            memset("pool", hcar, 0.0)
            memset("pool", halo, 0.0)
            bmf = alloc((128,), F32)
            memset("pool", bmf, 1.0)
            P.op("pool", lambda e: e.affine_select(out=bmf.ap, in_=bmf.ap, compare_op=ALU.is_ge, fill=0.0,
                                                   base=0, pattern=[[1, 128]], channel_multiplier=-1),
                 reads=[bmf], writes=[bmf])
            memset("pool", bmf[0:64, 64:128], 0.0)
            cp("dve", bmask, bmf)
        cur[0] = SCR

        has_odd = do_mixer and any(l % 2 == 1 for l in layers)
        if has_odd:
            posi = alloc((NBLK,), I32)
            posf = alloc((NBLK,), F32)
            invf = alloc((32,), F32)
            ang = alloc((NBLK, 32), F32)
            ang2 = alloc((NBLK, 32), F32)
            ki = alloc((NBLK, 32), I32)
            kf = alloc((NBLK, 32), F32)
            dma("sp", posi.ap, pos_d, [], [posi])
            cp("dve", posf, posi)
            for i in range(32):
                memset("pool", invf[:, i:i + 1], float(10000.0 ** (-i / 32.0)))
            tt("dve", ang, View(posf.ap.unsqueeze(2).to_broadcast([128, NBLK, 32]), posf.toks), bcm(invf, [128, NBLK, 32]), ALU.mult)

            def sin_of(dst, src, shift):
                ts("dve", ang2, src, shift, None, ALU.add)
                ts("dve", ki, ang2, 1.0 / TWO_PI, None, ALU.mult)
                cp("dve", kf, ki)
                stt("dve", ang2, kf, -TWO_PI, ang2, ALU.mult, ALU.add)
                ts("dve", kf, ang2, math.pi, -TWO_PI, ALU.is_gt, ALU.mult)
                tt("dve", ang2, ang2, kf, ALU.add)
                ts("dve", kf, ang2, -math.pi, TWO_PI, ALU.is_lt, ALU.mult)
                tt("dve", ang2, ang2, kf, ALU.add)
                act(dst, ang2, AF.Sin)

            sin_of(stab, ang, 0.0)
            sin_of(ctab, ang, math.pi / 2.0)
            onesf = alloc((128,), F32)
            mtmp = alloc((2, 128), F32)
            memset("pool", onesf, 1.0)
            P.op("pool", lambda e: e.affine_select(out=mtmp.ap[:, 0, :], in_=onesf.ap, compare_op=ALU.is_gt, fill=0.0,
                                                   base=0, pattern=[[-1, 128]], channel_multiplier=1),
                 reads=[onesf], writes=[mtmp])
            P.op("pool", lambda e: e.affine_select(out=mtmp.ap[:, 1, :], in_=onesf.ap, compare_op=ALU.is_ge, fill=0.0,
                                                   base=0, pattern=[[1, 128]], channel_multiplier=-1),
                 reads=[onesf], writes=[mtmp])
            cp("dve", swam, mtmp)
            for o in range(2):
                dma("sp", gqk[o][0].ap, qn_d[o:o + 1, :].partition_broadcast(128), [], [gqk[o][0]])
                dma("sp", gqk[o][1].ap, kn_d[o:o + 1, :].partition_broadcast(128), [], [gqk[o][1]])
                dma("sp", esink[o].ap, sinks_d[o:o + 1, :].partition_broadcast(128), [], [esink[o]])
                act(esink[o], esink[o], AF.Exp)
                for k_ in range(2):
                    memset("pool", Vb[o][k_], 1.0)
                    memset("pool", KTr[o][k_][0], 0.0)
                    memset("pool", KTr[o][k_][1], 0.0)

        cur[0] = SCR
        aT = [alloc((8, T), BF16) for _ in range(2)]
        rl = [alloc((T,), F32) for _ in range(2)]
        mlp_n = [0]
        mlp_r = [0]

        def mlp(l, first_tile):
            norm_to_hT(nmlp_d[l:l + 1, :])
            for r in range(4):
                Wi = wload(wmi_b, "wmi", l, r * 1024, 1024)
                Wo = wload_rows(wmo_b, "wmo", l, r * 8)
                a_ = aT[mlp_r[0] % 2]
                mlp_r[0] += 1
                for j in range(8):
                    pu = bank(mlp_n[0] % 2)
                    r_ = rl[mlp_n[0] % 2]
                    mlp_n[0] += 1
                    for c in range(8):
                        mm(pu, Wi[:, c, j * 128:(j + 1) * 128], hT[:, c, :], c == 0, c == 7)
                    act(r_, pu, AF.Relu)
                    tt("pool", a_[:, j, :], r_, r_, ALU.mult)
                    if first_tile:
                        pump(n=1)
                for s in range(NSUB):
                    py = [bank(2 + 2 * (s % 2)), bank(3 + 2 * (s % 2))]
                    for h in range(2):
                        for j in range(8):
                            mm(py[h], a_[:, j, s * 128:(s + 1) * 128], Wo[:, j, h * 512:(h + 1) * 512], j == 0, j == 7)
                    for h in range(2):
                        tt("dve", xs[s][:, h * 512:(h + 1) * 512], xs[s][:, h * 512:(h + 1) * 512], py[h], ALU.add)

        cur[0] = SCR
        sqb = alloc((1280,), F32)
        zn = alloc((1280,), F32)
        rt = [alloc((20, 32), F32) for _ in range(4)]
        qkr = alloc((1280,), BF16)
        kdup = alloc((512,), BF16)
        QT = alloc((8, 128), BF16)
        PT = [alloc((4, 2, 128), BF16) for _ in range(2)]
        atok = alloc((1024,), BF16)
        ost = alloc((64,), F32)
        pqk = PS.view(0, (1280,), F32)

        import os as _os
        DBG = float(_os.environ.get("KDBG", "99"))

        def odd_layer(l, it):
            o = l // 2
            norm_to_hT(nmix_d[l:l + 1, :])
            WA = wload(wio_b, "wio", o, 0, 1024)
            WB = wload(wio_b, "wio", o, 1024, 512)
            WO = wload(woo_b, "woo", o, 0, 1024)
            for s in range(NSUB):
                gb = it * NSUB + s
                sl = slice(s * 128, (s + 1) * 128)
                own, prv = gb % 2, (gb + 1) % 2
                if it == 0:
                    pump(n=16)
                for n in range(2):
                    for c in range(8):
                        mm(bank(n), hT[:, c, sl], WA[:, c, n * 512:(n + 1) * 512], c == 0, c == 7)
                for c in range(8):
                    mm(bank(2), hT[:, c, sl], WB[:, c, :], c == 0, c == 7)
                cp("act", Vb[o][own][:, :, 0:64], PS.view(2 * 2048 + 1024, (4, 64), F32))
                if DBG <= 1:
                    continue
                for (b_, h0, h1) in ((0, 0, 8), (1, 8, 16), (2, 16, 20)):
                    act(sqb[:, h0 * 64:h1 * 64], PS.view(b_ * 2048, ((h1 - h0) * 64,), F32), AF.Square)
                ssq = ost[:, 0:20]
                rq_ = ost[:, 20:40]
                rq = ost[:, 40:60]
                P.op("dve", lambda e: e.tensor_reduce(out=ssq.ap, in_=v3(sqb, 20).ap, axis=AX.X, op=ALU.add), reads=[sqb], writes=[ssq])
                rstd_of(rq, rq_, ssq, 1.0 / 64.0)
                zn3 = v3(zn, 20)
                for (b_, h0, h1) in ((0, 0, 8), (1, 8, 16), (2, 16, 20)):
                    tt("dve", zn3[:, h0:h1, :], PS.view(b_ * 2048, (h1 - h0, 64), F32),
                       View(rq.ap[:, h0:h1].unsqueeze(2).to_broadcast([128, h1 - h0, 64]), rq.toks), ALU.mult)
                tt("pool", zn3[:, 0:16, :], zn3[:, 0:16, :], bcm(gqk[o][0], [128, 16, 64]), ALU.mult)
                tt("pool", zn3[:, 16:20, :], zn3[:, 16:20, :], bcm(gqk[o][1], [128, 4, 64]), ALU.mult)
                if DBG <= 2:
                    continue
                cosb = bcm(ctab[:, gb, :], [128, 20, 32])
                sinb = bcm(stab[:, gb, :], [128, 20, 32])
                x1 = zn3[:, :, 0:32]
                x2 = zn3[:, :, 32:64]
                q3 = v3(qkr, 20)
                tt("pool", rt[0], x1, cosb, ALU.mult)
                tt("dve", rt[1], x2, sinb, ALU.mult)
                tt("pool", q3[:, :, 0:32], rt[0], rt[1], ALU.subtract)
                tt("pool", rt[2], x2, cosb, ALU.mult)
                tt("dve", rt[3], x1, sinb, ALU.mult)
                tt("dve", q3[:, :, 32:64], rt[2], rt[3], ALU.add)
                if DBG <= 3:
                    continue
                pT = bank(7, (8, 128), BF16)
                for c in range(8):
                    tr(pT[:, c, :], qkr[:, c * 128:(c + 1) * 128])
                cp("act", QT, pT)
                kd4 = View(kdup.ap.rearrange("p (g r d) -> p g r d", g=4, r=2), kdup.toks)
                ksrc = View(q3.ap[:, 16:20, :].unsqueeze(2).to_broadcast([128, 4, 2, 64]), qkr.toks)
                cp("pool", kd4, ksrc)
                pTk = bank(6, (4, 128), BF16)
                for g in range(4):
                    tr(pTk[:, g, :], kdup[:, g * 128:(g + 1) * 128])
                cp("dve", KTr[o][own][0][0:64], pTk[0:64])
                cp("dve", KTr[o][own][1][64:128], pTk[64:128])
                a3 = v3(atok, 16)
                if DBG <= 4:
                    continue
                for g in range(4):
                    pS = PS.view(3 * 2048, (4, 2, 128), F32)
                    kbs = [1] if gb == 0 else [0, 1]
                    for hh in range(4):
                        h = 4 * g + hh
                        c, half = h // 2, h % 2
                        for kb in kbs:
                            slot = prv if kb == 0 else own
                            mm(pS[:, hh, kb, :], KTr[o][slot][half][:, g, :], QT[:, c, :], True, True)
                    pt_ = PT[g % 2]
                    if DBG <= 4.5:
                        continue
                    if gb == 0:
                        act(pt_[:, :, 1, :], pS[:, :, 1, :], AF.Exp, scale=0.125)
                        tt("pool", pt_[:, :, 1, :], pt_[:, :, 1, :], bcm(swam[:, 1, :], [128, 4, 128]), ALU.mult)
                    else:
                        act(pt_, pS, AF.Exp, scale=0.125)
                        tt("pool", pt_, pt_, View(swam.ap.unsqueeze(1).to_broadcast([128, 4, 2, 128]), swam.toks), ALU.mult)
                    if DBG <= 5:
                        continue
                    pO = bank(5 + (g % 2), (4, 65), F32)
                    for hh in range(4):
                        for kb in kbs:
                            slot = prv if kb == 0 else own
                            mm(pO[:, hh, :], pt_[:, hh, kb, :], Vb[o][slot][:, g, :], kb == kbs[0], kb == kbs[-1])
                    dn = ost[:, 60:64]
                    tt("dve", dn, pO[:, :, 64], esink[o][:, 4 * g:4 * g + 4], ALU.add)
                    recip(dn, dn)
                    tt("dve", a3[:, 4 * g:4 * g + 4, :], pO[:, :, 0:64],
                       View(dn.ap.unsqueeze(2).to_broadcast([128, 4, 64]), dn.toks), ALU.mult)
                if DBG <= 6:
                    continue
                pT = bank(7, (8, 128), BF16)
                for c in range(8):
                    tr(pT[:, c, :], atok[:, c * 128:(c + 1) * 128])
                cp("act", yT[:, :, sl], pT)
                for hf in range(2):
                    for c in range(8):
                        mm(bank(hf), yT[:, c, sl], WO[:, c, hf * 512:(hf + 1) * 512], c == 0, c == 7)
                for hf in range(2):
                    tt("dve", xs[s][:, hf * 512:(hf + 1) * 512], xs[s][:, hf * 512:(hf + 1) * 512], bank(hf), ALU.add)

        cur[0] = SCR
        hA, hB, hK, hC, hD = (alloc((T,), F32) for _ in range(5))
        Qz = [alloc((4, T), BF16) for _ in range(2)]
        Kz = [alloc((4, T), BF16) for _ in range(2)]
        Ktok = [alloc((NSUB, 512), BF16) for _ in range(2)]
        Vtok = alloc((NSUB, 512), BF16)
        Gs = alloc((NSUB, 512), BF16)
        ATb = [alloc((4, 128), BF16) for _ in range(2)]
        T2s = alloc((4, 128), F32)
        Spb = [alloc((4, 128), BF16) for _ in range(2)]
        sqe = alloc((4, 128), F32)
        tno = alloc((4, 128), F32)
        yat = alloc((512,), BF16)
        etab = alloc((4, 4, 8), F32)
        est = alloc((16,), F32)
        XB = alloc((516,), F32)
        sgt = alloc((512,), F32)
        xcb = alloc((T,), BF16)

        def even_layer(l, it):
            e = l // 2
            LB, OML, NOML, CL, CL2, NBA, NBX = 0, 8, 16, 24, 32, 40, 48
            norm_to_hT(nmix_d[l:l + 1, :])
            W1 = wload(wie_b, "wie", e, 0, 1024)
            W2 = wload(wie_b, "wie", e, 1024, 1024)
            W3 = wload(wie_b, "wie", e, 2048, 1024)
            for s in range(NSUB):
                sl = slice(s * 128, (s + 1) * 128)
                for n in range(2):
                    for c in range(8):
                        mm(bank(4 + n), hT[:, c, sl], W2[:, c, n * 512:(n + 1) * 512], c == 0, c == 7)
                cp("act", Vtok[:, s, :], bank(4))
                sigm(sgt, bank(5))
                tt("dve", Gs[:, s, :], sgt, bank(5), ALU.mult)
            if DBG <= 11:
                return
            for j in range(2):
                for z_ in (Qz[j], Kz[j]):
                    zv = View(z_.ap.rearrange("p h (s j t) -> p h s j t", s=4, j=2)[:, :, :, 1 - j, :], z_.toks)
                    memset("pool", zv, 0.0)
            Lprev, Emu, Elast, Gt = (etab[:, i] for i in range(4))
            for h in range(4):
                pq = bank(2 * (h % 2))
                pf = bank(2 * (h % 2) + 1)
                for c in range(8):
                    mm(pq, W1[:, c, h * 128:(h + 1) * 128], hT[:, c, :], c == 0, c == 7)
                for c in range(8):
                    mm(pf, W1[:, c, 512 + h * 128:512 + (h + 1) * 128], hT[:, c, :], c == 0, c == 7)
                col = e * 4 + h
                if it == 0:
                    pump(n=8)
                sigm(hA, pf)
                if DBG <= 12:
                    continue
                ts("dve", hB, hA, ecst[:, OML + col:OML + col + 1], ecst[:, LB + col:LB + col + 1], ALU.mult, ALU.add)
                ts("pool", hK, hA, ecst[:, NOML + col:NOML + col + 1], ecst[:, OML + col:OML + col + 1], ALU.mult, ALU.add)
                ts("pool", hB, hB, 1e-30, None, ALU.max)
                if DBG <= 13:
                    continue
                act(hB, hB, AF.Ln)
                P.op("dve", lambda en, h=h: en.tensor_tensor_scan(out=hC.ap, data0=onesT.ap, data1=hB.ap,
                                                                 initial=Lcar.ap[:, e * 4 + h:e * 4 + h + 1],
                                                                 op0=ALU.mult, op1=ALU.add),
                     reads=[onesT, hB, Lcar], writes=[hC])
                if DBG <= 14:
                    continue
                C3 = v3(hC, 8)
                D3 = v3(hD, 8)
                cp("pool", Lprev[:, h, 0:1], Lcar[:, col:col + 1])
                cp("pool", Lprev[:, h, 1:8], C3[:, 0:7, 63])
                cp("pool", Lcar[:, col:col + 1], hC[:, T - 1:T])
                tt("dve", D3, C3, View(C3.ap[:, :, 31:32].to_broadcast([128, 8, 64]), hC.toks), ALU.subtract)
                tt("dve", Emu[:, h, :], C3[:, :, 31], Lprev[:, h, :], ALU.subtract)
                act(Emu[:, h, :], Emu[:, h, :], AF.Exp)
                act(Elast[:, h, :], D3[:, :, 63], AF.Exp)
                tt("pool", Gt[:, h, 0:7], Elast[:, h, 0:7], Emu[:, h, 1:8], ALU.mult)
                cp("pool", Gt[:, h, 7:8], Elast[:, h, 7:8])
                if DBG <= 15:
                    continue
                act(hA, hD, AF.Exp)
                act(hB, hD, AF.Exp, scale=-1.0)
                for j in range(2):
                    def par(v, j=j):
                        return View(v.ap.rearrange("p (s j t) -> p s j t", s=4, j=2)[:, :, j, :], v.toks)
                    tt("dve", par(Qz[j][:, h, :]), par(pq), par(hA), ALU.mult)
                    tt("pool", par(Kz[j][:, h, :]), par(hK), par(hB), ALU.mult)
            if DBG <= 16:
                return
            for s in range(NSUB):
                sl = slice(s * 128, (s + 1) * 128)
                for j in range(2):
                    pT = bank(7, (4, 128), BF16)
                    for h in range(4):
                        tr(pT[:, h, :], Kz[j][:, h, sl])
                    cp("act" if j == 0 else "dve", Ktok[j][:, s, :], View(pT.ap.rearrange("p h d -> p (h d)"), pT.toks))
            if DBG <= 17:
                return
            Spe = Sp[e]
            tt("dve", Spe, Spe, View(Emu.ap[:, :, 0:1].to_broadcast([128, 4, 128]), etab.toks), ALU.mult)
            cp("act", Spb[0], Spe)
            for s in range(NSUB):
                sl = slice(s * 128, (s + 1) * 128)
                pS = bank(6, (4, 128), F32)
                if it == 0:
                    pump(n=8)
                for h in range(4):
                    for j in range(2):
                        mm(pS[:, h, :], Kz[j][:, h, sl], Qz[j][:, h, sl], j == 0, j == 1)
                at_ = ATb[s % 2]
                tt("dve", at_, pS, bcm(bmask, [128, 4, 128]), ALU.mult)
                pkv = [bank(0, (4, 128), F32), bank(1, (4, 128), F32)]
                for j in range(2):
                    for h in range(4):
                        mm(pkv[j][:, h, :], Ktok[j][:, s, h * 128:(h + 1) * 128], Vtok[:, s, h * 128:(h + 1) * 128], True, True)
                for j in range(2):
                    cch = 2 * s + j
                    tt("dve", T2s, pkv[j], Spe, ALU.add)
                    tt("pool", Spe, T2s, View(Gt.ap[:, :, cch:cch + 1].to_broadcast([128, 4, 128]), etab.toks), ALU.mult)
                    if j == 0:
                        cp("act", Spb[1], Spe)
                if DBG <= 18:
                    continue
                po = bank(2 + (s % 2), (4, 128), F32)
                for h in range(4):
                    mm(po[:, h, :], at_[:, h, :], Vtok[:, s, h * 128:(h + 1) * 128], True, False)
                    mm(po[:, h, :], Qz[0][:, h, sl], Spb[0][:, h, :], False, False)
                    mm(po[:, h, :], Qz[1][:, h, sl], Spb[1][:, h, :], False, True)
                if s < NSUB - 1:
                    cp("act", Spb[0], Spe)
                if DBG <= 19:
                    continue
                act(sqe, po, AF.Square)
                P.op("dve", lambda en: en.tensor_reduce(out=est.ap[:, 0:4], in_=sqe.ap, axis=AX.X, op=ALU.add), reads=[sqe], writes=[est])
                rstd_of(est[:, 8:12], est[:, 4:8], est[:, 0:4], 1.0 / 128.0)
                tt("dve", tno, po, View(est.ap[:, 8:12].unsqueeze(2).to_broadcast([128, 4, 128]), est.toks), ALU.mult)
                tno2 = View(tno.ap.rearrange("p h d -> p (h d)"), tno.toks)
                tt("pool", tno2, tno2, gainA[e], ALU.mult)
                tt("pool", yat, tno2, Gs[:, s, :], ALU.mult)
                pT = bank(7, (4, 128), BF16)
                for h in range(4):
                    tr(pT[:, h, :], yat[:, h * 128:(h + 1) * 128])
                cp("act", yT[:, 0:4, sl], pT)
            R_, I_, T1, Gg, T2r = hA, hB, hK, hC, hD
            if DBG <= 20:
                return
            for m in range(4):
                col = e * 4 + m
                pg, px = bank(4), bank(5)
                for c in range(8):
                    mm(pg, W3[:, c, m * 128:(m + 1) * 128], hT[:, c, :], c == 0, c == 7)
                for c in range(8):
                    mm(px, W3[:, c, 512 + m * 128:512 + (m + 1) * 128], hT[:, c, :], c == 0, c == 7)
                cp("pool", XB[:, 0:3], halo[:, col, :])
                cp("act", XB[:, 3:515], px)
                xc = T2r

                def cwc(j):
                    k_ = SM_CONVW + e * 16 + j * 4 + m
                    return small[:, k_:k_ + 1]
                ts("dve", xc, XB[:, 3:515], cwc(3), small[:, SM_CONVB + col:SM_CONVB + col + 1], ALU.mult, ALU.add)
                for k_ in (1, 2, 3):
                    stt("dve", xc, XB[:, 3 - k_:515 - k_], cwc(3 - k_), xc, ALU.mult, ALU.add)
                cp("pool", halo[:, col, :], XB[:, 512:515])
                cp("pool", xcb, xc)
                if DBG <= 21:
                    continue
                pr, pi = bank(6), bank(7)
                mm(pr, WBD[:, e * 8 + m, :], xcb, True, True)
                mm(pi, WBD[:, e * 8 + 4 + m, :], xcb, True, True)
                sigm(R_, pr, neg_bias=ecst[:, NBA + col:NBA + col + 1])
                sigm(I_, pi, neg_bias=ecst[:, NBX + col:NBX + col + 1])
                if DBG <= 22:
                    continue
                act(T1, R_, AF.Exp, scale=ecst[:, CL2 + col:CL2 + col + 1])
                ts("dve", T1, T1, -1.0, 1.0, ALU.mult, ALU.add)
                ts("dve", T1, T1, 1e-30, None, ALU.max)
                act(T1, T1, AF.Ln)
                act(T1, T1, AF.Exp, scale=0.5)
                act(R_, R_, AF.Exp, scale=ecst[:, CL + col:CL + col + 1])
                tt("pool", I_, I_, xc, ALU.mult)
                tt("pool", I_, I_, T1, ALU.mult)
                P.op("dve", lambda en, col=col: en.tensor_tensor_scan(out=T1.ap, data0=R_.ap, data1=I_.ap,
                                                                     initial=hcar.ap[:, col:col + 1],
                                                                     op0=ALU.mult, op1=ALU.add),
                     reads=[R_, I_, hcar], writes=[T1])
                cp("pool", hcar[:, col:col + 1], T1[:, T - 1:T])
                if DBG <= 23:
                    continue
                cp("act", Gg, pg)
                act(T2r, pg, AF.Square)
                ts("dve", T2r, T2r, 0.044715, 1.0, ALU.mult, ALU.add)
                tt("pool", T2r, T2r, Gg, ALU.mult)
                sigm(T2r, T2r, scale=2.0 * math.sqrt(2.0 / math.pi))
                tt("pool", Gg, Gg, T2r, ALU.mult)
                tt("pool", yT[:, 4 + m, :], T1, Gg, ALU.mult)
            WO = wload(woe_b, "woe", e, 0, 1024)
            for s in range(NSUB):
                sl = slice(s * 128, (s + 1) * 128)
                for hf in range(2):
                    for c in range(8):
                        mm(bank(hf), yT[:, c, sl], WO[:, c, hf * 512:(hf + 1) * 512], c == 0, c == 7)
                for hf in range(2):
                    tt("dve", xs[s][:, hf * 512:(hf + 1) * 512], xs[s][:, hf * 512:(hf + 1) * 512], bank(hf), ALU.add)

        print("SBUF bytes used", cur[0], "of", SB_BYTES)
        if n_layers > 0:
            pump(upto_group=0)
        for it in range(n_tiles):
            for s in range(NSUB):
                r0 = it * T + s * 128
                dma("sp", xs[s].ap, x_d[r0:r0 + 128, :], [], [xs[s]])
            for li, l in enumerate(layers):
                gi = 2 * li
                if it == 0:
                    pump(upto_group=gi)
                if do_mixer:
                    if l % 2 == 1:
                        odd_layer(l, it)
                    else:
                        even_layer(l, it)
                if do_mlp:
                    if it == 0:
                        pump(upto_group=gi + 1)
                    mlp(l, it == 0)
            for s in range(NSUB):
                r0 = it * T + s * 128
                dma("sp", out_d[r0:r0 + 128, :], xs[s].ap, [xs[s]], [("D", "out", it, s)])
        P.emit()
    return nc


def _small_pack(inp):
    sm = np.zeros((128, NSM), np.float32)

    def fm(v):
        return np.ascontiguousarray(v.reshape(4, 128).T)

    for e in range(2):
        sm[:, SM_LBL + e * 4:SM_LBL + e * 4 + 4] = fm(inp["hgrn_lb_logits"][e])
        for j in range(4):
            sm[:, SM_CONVW + e * 16 + j * 4:SM_CONVW + e * 16 + j * 4 + 4] = fm(inp["conv_w"][e, j])
        sm[:, SM_CONVB + e * 4:SM_CONVB + e * 4 + 4] = fm(inp["conv_b"][e])
        sm[:, SM_BA + e * 4:SM_BA + e * 4 + 4] = fm(inp["rg_ba"][e])
        sm[:, SM_BX + e * 4:SM_BX + e * 4 + 4] = fm(inp["rg_bx"][e])
        sm[:, SM_LAM + e * 4:SM_LAM + e * 4 + 4] = fm(inp["rg_lambda"][e])
    return sm


_NC_CACHE = {}


def make_in_maps(inputs, n_cores=8):
    inp = {k: np.asarray(v) for k, v in inputs.items()}
    shared = {
        "norm_mix": inp["norm_mix"], "norm_mlp": inp["norm_mlp"],
        "w_mlp_in": inp["w_mlp_in"], "w_mlp_out": inp["w_mlp_out"],
        "w_in_even": inp["w_in_even"], "w_out_even": inp["w_out_even"],
        "w_in_odd": inp["w_in_odd"], "w_out_odd": inp["w_out_odd"],
        "small": _small_pack(inp), "hgrn_out_norm": inp["hgrn_out_norm"],
        "rg_wa": inp["rg_wa"], "rg_wx": inp["rg_wx"],
        "q_norm": inp["q_norm"], "k_norm": inp["k_norm"], "sinks": inp["sinks"],
    }
    shared = {k: np.ascontiguousarray(v, dtype=np.float32) for k, v in shared.items()}
    maps = []
    for b in range(n_cores):
        m = dict(shared)
        m["x"] = np.ascontiguousarray(inp["x"][b], dtype=np.float32)
        m["pos"] = np.ascontiguousarray(inp["positions"][b].reshape(SEQ // 128, 128).T.astype(np.int32))
        maps.append(m)
    return maps


def kernel(**inputs):
    key = "full"
    if key not in _NC_CACHE:
        _NC_CACHE[key] = build_nc()
    nc = _NC_CACHE[key]
    in_maps = make_in_maps(inputs, 8)
    res = run_bass_kernel_spmd(nc, in_maps, core_ids=list(range(8)))
    out = np.stack([np.asarray(r["out"]).reshape(SEQ, D) for r in res.results], axis=0)
    return out.astype(np.float32)
```

```python
import math
from contextlib import ExitStack

import numpy as np
import concourse.bass as bass
import concourse.mybir as mybir
from concourse.bass_utils import run_bass_kernel_spmd

F32 = mybir.dt.float32
BF16 = mybir.dt.bfloat16
I32 = mybir.dt.int32
AF = mybir.ActivationFunctionType
ALU = mybir.AluOpType
AX = mybir.AxisListType

GRAN = 256
ENGS = ("pe", "act", "dve", "pool", "sp")


class View:
    __slots__ = ("ap", "toks")

    def __init__(self, ap, toks):
        self.ap = ap
        self.toks = toks

    def __getitem__(self, key):
        return View(self.ap[key], self.toks)


class Arena:
    def __init__(self, tensor, space, nbytes):
        self.t = tensor
        self.space = space
        self.nbytes = nbytes

    def view(self, off, shape, dtype, p0=0, p1=128):
        esz = mybir.dt.size(dtype)
        n = 1
        for s in shape:
            n *= s
        nb = n * esz
        assert off % 4 == 0 and nb % 4 == 0, (off, nb)
        assert off + nb <= self.nbytes, (off, nb, self.nbytes)
        ap = self.t[p0:p1, off // 4:(off + nb) // 4]
        if dtype != F32:
            ap = ap.bitcast(dtype)
        if len(shape) == 2:
            ap = ap.rearrange("p (a b) -> p a b", a=shape[0])
        elif len(shape) == 3:
            ap = ap.rearrange("p (a b c) -> p a b c", a=shape[0], b=shape[1])
        toks = tuple((self.space, g) for g in range(off // GRAN, (off + nb - 1) // GRAN + 1))
        return View(ap, toks)


class Op:
    __slots__ = ("eng", "fn", "reads", "writes", "dma", "idx", "waits", "sig", "sigval", "dsem", "dval", "prewait")

    def __init__(self, eng, fn, reads, writes, dma):
        self.eng = eng
        self.fn = fn
        self.reads = reads
        self.writes = writes
        self.dma = dma
        self.waits = []
        self.sig = False
        self.sigval = 0
        self.dsem = None
        self.dval = 0
        self.prewait = None


def _toks(xs):
    out = []
    for x in xs:
        if x is None:
            continue
        if isinstance(x, View):
            out.extend(x.toks)
        else:
            out.append(x)
    return out


class Prog:
    def __init__(self, nc, n_dma_sems=12):
        self.nc = nc
        self.ops = []
        self.n_dma_sems = n_dma_sems

    def op(self, eng, fn, reads=(), writes=(), dma=False):
        o = Op(eng, fn, _toks(reads), _toks(writes), dma)
        o.idx = len(self.ops)
        self.ops.append(o)
        return o

    def analyze(self):
        last_w = {}
        readers = {}
        per_eng = {e: [] for e in ENGS}
        need = []
        ops = self.ops
        for o in ops:
            deps = set()
            raw = set()
            for t in o.reads:
                w = last_w.get(t)
                if w is not None:
                    deps.add(w)
                    raw.add(w)
            for t in o.writes:
                w = last_w.get(t)
                if w is not None:
                    deps.add(w)
                r = readers.get(t)
                if r:
                    deps.update(r)
            for t in o.writes:
                last_w[t] = o.idx
                readers[t] = []
            for t in o.reads:
                readers.setdefault(t, []).append(o.idx)
            deps.discard(o.idx)
            per_eng[o.eng].append(o)
            nd = []
            for j in deps:
                p = ops[j]
                if p.dma or p.eng != o.eng or o.dma:
                    nd.append(j)
                elif o.eng == "pool" or (o.eng != "pe" and j in raw):
                    nd.append(j)
            for j in nd:
                if not ops[j].dma:
                    ops[j].sig = True
            need.append(nd)
        self.per_eng = per_eng
        self.dma_count = {}
        for e in ENGS:
            c = 0
            n = 0
            for o in per_eng[e]:
                if o.dma:
                    o.dsem = (e, n % self.n_dma_sems)
                    o.dval = 16 * (n // self.n_dma_sems + 1)
                    if n >= self.n_dma_sems:
                        o.prewait = (o.dsem, o.dval - 16)
                    n += 1
                elif o.sig:
                    c += 1
                    o.sigval = c
            self.dma_count[e] = n
        known = {e: {} for e in ENGS}
        for o, nd in zip(ops, need):
            k = known[o.eng]
            w = {}
            if o.prewait is not None:
                s, v = o.prewait
                if k.get(s, 0) < v:
                    w[s] = v
            for j in nd:
                p = ops[j]
                if p.dma:
                    s, v = p.dsem, p.dval
                else:
                    s, v = ("c", p.eng), p.sigval
                if k.get(s, 0) < v and w.get(s, 0) < v:
                    w[s] = v
            for s, v in w.items():
                k[s] = v
            o.waits = list(w.items())

    def emit(self, final_waits_eng="sp"):
        nc = self.nc
        self.analyze()
        with ExitStack() as es:
            sems = {}
            for e in ENGS:
                sems[("c", e)] = es.enter_context(nc.semaphore("c_" + e))
                for i in range(min(self.n_dma_sems, self.dma_count[e])):
                    sems[(e, i)] = es.enter_context(nc.semaphore("d_%s_%d" % (e, i)))
            block = es.enter_context(nc.Block())

            def run(e):
                def body(eng):
                    for o in self.per_eng[e]:
                        for s, v in o.waits:
                            eng.wait_ge(sems[s], v)
                        ins = o.fn(eng)
                        if o.dma:
                            ins.then_inc(sems[o.dsem], 16)
                        elif o.sig:
                            ins.then_inc(sems[("c", e)], 1)
                    if e == final_waits_eng:
                        for e2 in ENGS:
                            n = self.dma_count[e2]
                            for i in range(min(self.n_dma_sems, n)):
                                cnt = (n - i + self.n_dma_sems - 1) // self.n_dma_sems
                                eng.wait_ge(sems[(e2, i)], 16 * cnt)
                return body

            block.tensor(run("pe"))
            block.scalar(run("act"))
            block.vector(run("dve"))
            block.gpsimd(run("pool"))
            block.sync(run("sp"))


D = 1024
SEQ = 4096
DEPTH = 4
DFF = 4096
T = 512
NT = SEQ // T
NSUB = T // 128
EPS = 1e-6
TWO_PI = 2.0 * math.pi

SM_LBL = 0
SM_CONVW = 8
SM_CONVB = 40
SM_BA = 48
SM_BX = 56
SM_LAM = 64
NSM = 72


def build_nc(n_tiles=NT, layers=(0, 1, 2, 3), do_mixer=True, do_mlp=True):
    n_layers = len(layers)
    nc = bass.Bass("TRN2", target_bir_lowering=False)
    dt_in = lambda name, shape, dt=F32: nc.dram_tensor(name, shape, dt, kind="ExternalInput").ap()
    x_d = dt_in("x", [SEQ, D])
    pos_d = dt_in("pos", [128, SEQ // 128], I32)
    nmix_d = dt_in("norm_mix", [DEPTH, D])
    nmlp_d = dt_in("norm_mlp", [DEPTH, D])
    wmi_d = dt_in("w_mlp_in", [DEPTH, D, DFF])
    wmo_d = dt_in("w_mlp_out", [DEPTH, DFF, D])
    wie_d = dt_in("w_in_even", [2, D, 3072])
    woe_d = dt_in("w_out_even", [2, D, D])
    wio_d = dt_in("w_in_odd", [2, D, 1536])
    woo_d = dt_in("w_out_odd", [2, D, D])
    small_d = dt_in("small", [128, NSM])
    hon_d = dt_in("hgrn_out_norm", [2, 512])
    rgwa_d = dt_in("rg_wa", [2, 8, 64, 64])
    rgwx_d = dt_in("rg_wx", [2, 8, 64, 64])
    qn_d = dt_in("q_norm", [2, 64])
    kn_d = dt_in("k_norm", [2, 64])
    sinks_d = dt_in("sinks", [2, 16])
    out_d = nc.dram_tensor("out", [SEQ, D], F32, kind="ExternalOutput").ap()

    def scratch(name, shape):
        return nc.dram_tensor(name, shape, BF16, kind="Internal").ap()

    wmi_b = scratch("wmi_b", [DEPTH, D, DFF])
    wmo_b = scratch("wmo_b", [DEPTH, DFF, D])
    wie_b = scratch("wie_b", [2, D, 3072])
    woe_b = scratch("woe_b", [2, D, D])
    wio_b = scratch("wio_b", [2, D, 1536])
    woo_b = scratch("woo_b", [2, D, D])

    P = Prog(nc)
    with ExitStack() as es:
        SB_BYTES = 206 * 1024
        sb_t = es.enter_context(nc.sbuf_tensor("arena", [128, SB_BYTES // 4], F32))
        ps_t = es.enter_context(nc.psum_tensor("psarena", [128, 4096], F32))
        SB = Arena(sb_t, "S", SB_BYTES)
        PS = Arena(ps_t, "P", 16384)
        cur = [0]

        def alloc(shape, dt, p0=0, p1=128):
            n = 1
            for s_ in shape:
                n *= s_
            nb = (n * mybir.dt.size(dt) + GRAN - 1) // GRAN * GRAN
            v = SB.view(cur[0], shape, dt, p0, p1)
            cur[0] += nb
            return v

        def bank(i, shape=(512,), dt=F32, off=0):
            return PS.view(i * 2048 + off, shape, dt)

        def mm(out, lhsT, rhs, start, stop):
            P.op("pe", lambda e: e.matmul(out.ap, lhsT=lhsT.ap, rhs=rhs.ap, start=start, stop=stop),
                 reads=[lhsT, rhs], writes=[out])

        def tr(out, in_):
            P.op("pe", lambda e: e.transpose(out=out.ap, in_=in_.ap, identity=ident.ap),
                 reads=[in_, ident], writes=[out])

        def act(out, in_, func, bias=None, scale=None, accum=None, rd=()):
            kw = {}
            rds = [in_] + list(rd)
            if bias is not None:
                if isinstance(bias, View):
                    kw["bias"] = bias.ap
                    rds.append(bias)
                else:
                    kw["bias"] = bias
            if scale is not None:
                if isinstance(scale, View):
                    kw["scale"] = scale.ap
                    rds.append(scale)
                else:
                    kw["scale"] = scale
            wr = [out]
            if accum is not None:
                kw["accum_out"] = accum.ap
                wr.append(accum)
            P.op("act", lambda e: e.activation(out=out.ap, in_=in_.ap, func=func, **kw), reads=rds, writes=wr)

        def tt(eng, out, in0, in1, op):
            P.op(eng, lambda e: e.tensor_tensor(out=out.ap, in0=in0.ap, in1=in1.ap, op=op), reads=[in0, in1], writes=[out])

        def ts(eng, out, in0, s1, s2, op0, op1=None):
            rds = [in0]
            a1 = s1
            a2 = s2
            if isinstance(s1, View):
                rds.append(s1)
                a1 = s1.ap
            if isinstance(s2, View):
                rds.append(s2)
                a2 = s2.ap
            if op1 is None:
                P.op(eng, lambda e: e.tensor_scalar(out=out.ap, in0=in0.ap, scalar1=a1, scalar2=None, op0=op0), reads=rds, writes=[out])
            else:
                P.op(eng, lambda e: e.tensor_scalar(out=out.ap, in0=in0.ap, scalar1=a1, scalar2=a2, op0=op0, op1=op1), reads=rds, writes=[out])

        def stt(eng, out, in0, sc, in1, op0, op1):
            rds = [in0, in1]
            a = sc
            if isinstance(sc, View):
                rds.append(sc)
                a = sc.ap
            P.op(eng, lambda e: e.scalar_tensor_tensor(out=out.ap, in0=in0.ap, scalar=a, in1=in1.ap, op0=op0, op1=op1), reads=rds, writes=[out])

        def cp(eng, out, in_):
            if eng == "act":
                P.op("act", lambda e: e.copy(out=out.ap, in_=in_.ap), reads=[in_], writes=[out])
            else:
                P.op(eng, lambda e: e.tensor_copy(out=out.ap, in_=in_.ap), reads=[in_], writes=[out])

        def memset(eng, out, val):
            P.op(eng, lambda e: e.memset(out.ap, val), writes=[out])

        def recip(out, in_):
            P.op("dve", lambda e: e.reciprocal(out=out.ap, in_=in_.ap), reads=[in_], writes=[out])

        def dma(q, out_ap, in_ap, reads, writes):
            P.op(q, lambda e: e.dma_start(out=out_ap, in_=in_ap), reads=reads, writes=writes, dma=True)

        def sigm(out, in_, bias=None, scale=None, neg_bias=None):
            nscale = -1.0 if scale is None else -scale
            act(out, in_, AF.Exp, bias=neg_bias, scale=nscale)
            act(out, out, AF.Ln, bias=1.0)
            act(out, out, AF.Exp, scale=-1.0)

        def rstd_of(out, tmp, ss, scale):
            act(tmp, ss, AF.Ln, bias=EPS, scale=scale)
            act(out, tmp, AF.Exp, scale=-0.5)

        def bc(v, shape):
            return View(v.ap.to_broadcast(shape), v.toks)

        def bcm(v, shape):
            return View(v.ap.unsqueeze(1).to_broadcast(shape), v.toks)

        xs = [alloc((D,), F32) for _ in range(NSUB)]
        hT = alloc((8, T), BF16)
        yT = alloc((8, T), BF16)
        NRING = 3
        ring = [alloc((8, 1024), BF16) for _ in range(NRING)]
        gbc = [alloc((D,), F32) for _ in range(1)]
        ident = alloc((128,), BF16)
        identf = alloc((128,), F32)
        small = alloc((NSM,), F32)
        stat = alloc((16,), F32)
        hb = [alloc((D,), BF16) for _ in range(1)]

        memset("pool", identf, 0.0)
        P.op("pool", lambda e: e.affine_select(out=identf.ap, in_=identf.ap, compare_op=ALU.not_equal, fill=1.0,
                                               base=0, pattern=[[-1, 128]], channel_multiplier=1),
             reads=[identf], writes=[identf])
        cp("dve", ident, identf)
        dma("sp", small.ap, small_d, [], [small])

        CW = 1024
        NSTG = 2
        stg = [alloc((CW,), F32) for _ in range(NSTG)]
        stgb = [alloc((CW,), BF16) for _ in range(NSTG)]
        conv_state = {"n": 0}

        def conv_chunks(src, dst, name, idx, R, C):
            out = []
            cw = C if C <= CW else (CW if C % CW == 0 else 512)
            for r0 in range(0, R, 128):
                for c0 in range(0, C, cw):
                    def f(r0=r0, c0=c0):
                        k = conv_state["n"] % NSTG
                        ce = ("act", "dve")[conv_state["n"] % 2]
                        conv_state["n"] += 1
                        s_, b_ = stg[k], stgb[k]
                        dma("sp", s_.ap[:, 0:cw], src[idx, r0:r0 + 128, c0:c0 + cw], [], [s_])
                        cp(ce, b_[:, 0:cw], s_[:, 0:cw])
                        dma("sp", dst[idx, r0:r0 + 128, c0:c0 + cw], b_.ap[:, 0:cw], [b_], [("D", name, idx, r0 // 128)])
                    out.append(f)
            return out

        conv_groups = []
        for l in layers:
            g = []
            if do_mixer:
                if l % 2 == 0:
                    g += conv_chunks(wie_d, wie_b, "wie", l // 2, D, 3072)
                    g += conv_chunks(woe_d, woe_b, "woe", l // 2, D, D)
                else:
                    g += conv_chunks(wio_d, wio_b, "wio", l // 2, D, 1536)
                    g += conv_chunks(woo_d, woo_b, "woo", l // 2, D, D)
            conv_groups.append(g)
            g = []
            if do_mlp:
                ci = conv_chunks(wmi_d, wmi_b, "wmi", l, D, DFF)
                co = conv_chunks(wmo_d, wmo_b, "wmo", l, DFF, D)
                g += ci + co
            conv_groups.append(g)
        conv_flat = [f for g in conv_groups for f in g]
        conv_pos = [0]

        def pump(upto_group=None, n=None):
            if upto_group is not None:
                tgt = sum(len(g) for g in conv_groups[:upto_group + 1])
            else:
                tgt = min(len(conv_flat), conv_pos[0] + n)
            while conv_pos[0] < tgt:
                conv_flat[conv_pos[0]]()
                conv_pos[0] += 1

        def wtoks(name, idx, rows):
            return [("D", name, idx, r) for r in rows]

        ring_n = [0]

        def wload(src_b, name, idx, c0, cw, rows=None):
            slot = ring[ring_n[0] % NRING]
            ring_n[0] += 1
            v = slot if cw == 1024 else slot[:, :, 0:cw]
            dma("sp", v.ap, src_b[idx].rearrange("(c p) f -> p c f", p=128)[:, :, c0:c0 + cw],
                wtoks(name, idx, range(8)), [slot])
            return v

        def wload_rows(src_b, name, idx, r0):
            slot = ring[ring_n[0] % NRING]
            ring_n[0] += 1
            dma("sp", slot.ap, src_b[idx, r0 * 128:(r0 + 8) * 128, :].rearrange("(j p) d -> p j d", p=128),
                wtoks(name, idx, range(r0, r0 + 8)), [slot])
            return slot

        gb_n = [0]

        def norm_to_hT(gain_row_ap):
            g = gbc[0]
            gb_n[0] += 1
            dma("sp", g.ap, gain_row_ap.partition_broadcast(128), [], [g])
            for s in range(NSUB):
                ss = stat[:, s:s + 1]
                rs = stat[:, 4 + s:5 + s]
                rr = stat[:, 8 + s:9 + s]
                h_ = hb[0]
                act(h_, xs[s], AF.Square, scale=1.0 / 32.0, accum=ss)
                rstd_of(rr, rs, ss, 1.0)
                stt("dve", h_, xs[s], rr, g, ALU.mult, ALU.mult)
                pT = bank(7, (8, 128), BF16)
                for c in range(8):
                    tr(pT[:, c, :], h_[:, c * 128:(c + 1) * 128])
                cp("act", hT[:, :, s * 128:(s + 1) * 128], pT)

        def v3(v, a):
            return View(v.ap.rearrange("p (a b) -> p a b", a=a), v.toks)

        NBLK = SEQ // 128
        ctab = alloc((NBLK, 32), F32)
        stab = alloc((NBLK, 32), F32)
        gqk = [[alloc((64,), F32) for _ in range(2)] for _ in range(2)]
        esink = [alloc((16,), F32) for _ in range(2)]
        swam = alloc((2, 128), BF16)
        Vb = [[alloc((4, 65), BF16) for _ in range(2)] for _ in range(2)]
        KTr = [[[alloc((4, 128), BF16) for _ in range(2)] for _ in range(2)] for _ in range(2)]
        Sp = [alloc((4, 128), F32) for _ in range(2)]
        gainA = [alloc((512,), F32) for _ in range(2)]
        WBD = alloc((16, 128), BF16)
        onesT = alloc((T,), F32)
        bmask = alloc((128,), BF16)
        ecst = alloc((64,), F32)
        Lcar = alloc((8,), F32)
        hcar = alloc((8,), F32)
        halo = alloc((8, 3), F32)
        SCR = cur[0]

        has_even = do_mixer and any(l % 2 == 0 for l in layers)
        if has_even:
            LB, OML, NOML, CL, CL2, NBA, NBX = 0, 8, 16, 24, 32, 40, 48
            tmpc = alloc((64,), F32)
            l0 = small[:, SM_LBL:SM_LBL + 4]
            l1 = small[:, SM_LBL + 4:SM_LBL + 8]
            mx, e0, e1, sd, sm0, sm1 = (tmpc[:, 4 * i:4 * i + 4] for i in range(6))
            tt("dve", mx, l0, l1, ALU.max)
            tt("dve", e0, l0, mx, ALU.subtract)
            tt("dve", e1, l1, mx, ALU.subtract)
            act(e0, e0, AF.Exp)
            act(e1, e1, AF.Exp)
            tt("dve", sd, e0, e1, ALU.add)
            recip(sd, sd)
            tt("dve", sm0, e0, sd, ALU.mult)
            tt("dve", sm1, e1, sd, ALU.mult)
            tt("dve", ecst[:, LB:LB + 4], sm0, sm0, ALU.subtract)
            tt("dve", mx, sm0, sm1, ALU.add)
            tt("dve", ecst[:, LB + 4:LB + 8], mx, sm0, ALU.subtract)
            ts("dve", ecst[:, OML:OML + 8], ecst[:, LB:LB + 8], -1.0, 1.0, ALU.mult, ALU.add)
            ts("dve", ecst[:, NOML:NOML + 8], ecst[:, LB:LB + 8], -1.0, None, ALU.add)
            sigm(tmpc[:, 32:40], small[:, SM_LAM:SM_LAM + 8])
            act(tmpc[:, 32:40], tmpc[:, 32:40], AF.Ln)
            ts("dve", ecst[:, NBA:NBA + 8], small[:, SM_BA:SM_BA + 8], -1.0, None, ALU.mult)
            ts("dve", ecst[:, NBX:NBX + 8], small[:, SM_BX:SM_BX + 8], -1.0, None, ALU.mult)
            ts("dve", ecst[:, CL:CL + 8], tmpc[:, 32:40], 8.0, None, ALU.mult)
            ts("dve", ecst[:, CL2:CL2 + 8], tmpc[:, 32:40], 16.0, None, ALU.mult)
            wst = alloc((16, 128), F32)
            memset("pool", wst, 0.0)
            for e in range(2):
                for gi_, wsrc in enumerate((rgwa_d, rgwx_d)):
                    for m in range(4):
                        k_ = e * 8 + gi_ * 4 + m
                        dma("sp", wst.ap[0:64, k_, 0:64], wsrc[e, 2 * m], [], [wst])
                        dma("sp", wst.ap[64:128, k_, 64:128], wsrc[e, 2 * m + 1], [], [wst])
            cp("pool", WBD, wst)
            for e in range(2):
                dma("sp", gainA[e].ap, hon_d[e:e + 1, :].partition_broadcast(128), [], [gainA[e]])
                memset("pool", Sp[e], 0.0)
            memset("pool", onesT, 1.0)
            memset("pool", Lcar, 0.0)
            memset("pool", hcar, 0.0)
            memset("pool", halo, 0.0)
            bmf = alloc((128,), F32)
            memset("pool", bmf, 1.0)
            P.op("pool", lambda e: e.affine_select(out=bmf.ap, in_=bmf.ap, compare_op=ALU.is_ge, fill=0.0,
                                                   base=0, pattern=[[1, 128]], channel_multiplier=-1),
                 reads=[bmf], writes=[bmf])
            memset("pool", bmf[0:64, 64:128], 0.0)
            cp("dve", bmask, bmf)
        cur[0] = SCR

        has_odd = do_mixer and any(l % 2 == 1 for l in layers)
        if has_odd:
            posi = alloc((NBLK,), I32)
            posf = alloc((NBLK,), F32)
            invf = alloc((32,), F32)
            ang = alloc((NBLK, 32), F32)
            ang2 = alloc((NBLK, 32), F32)
            ki = alloc((NBLK, 32), I32)
            kf = alloc((NBLK, 32), F32)
            dma("sp", posi.ap, pos_d, [], [posi])
            cp("dve", posf, posi)
            for i in range(32):
                memset("pool", invf[:, i:i + 1], float(10000.0 ** (-i / 32.0)))
            tt("dve", ang, View(posf.ap.unsqueeze(2).to_broadcast([128, NBLK, 32]), posf.toks), bcm(invf, [128, NBLK, 32]), ALU.mult)

            def sin_of(dst, src, shift):
                ts("dve", ang2, src, shift, None, ALU.add)
                ts("dve", ki, ang2, 1.0 / TWO_PI, None, ALU.mult)
                cp("dve", kf, ki)
                stt("dve", ang2, kf, -TWO_PI, ang2, ALU.mult, ALU.add)
                ts("dve", kf, ang2, math.pi, -TWO_PI, ALU.is_gt, ALU.mult)
                tt("dve", ang2, ang2, kf, ALU.add)
                ts("dve", kf, ang2, -math.pi, TWO_PI, ALU.is_lt, ALU.mult)
                tt("dve", ang2, ang2, kf, ALU.add)
                act(dst, ang2, AF.Sin)

            sin_of(stab, ang, 0.0)
            sin_of(ctab, ang, math.pi / 2.0)
            onesf = alloc((128,), F32)
            mtmp = alloc((2, 128), F32)
            memset("pool", onesf, 1.0)
            P.op("pool", lambda e: e.affine_select(out=mtmp.ap[:, 0, :], in_=onesf.ap, compare_op=ALU.is_gt, fill=0.0,
                                                   base=0, pattern=[[-1, 128]], channel_multiplier=1),
                 reads=[onesf], writes=[mtmp])
            P.op("pool", lambda e: e.affine_select(out=mtmp.ap[:, 1, :], in_=onesf.ap, compare_op=ALU.is_ge, fill=0.0,
                                                   base=0, pattern=[[1, 128]], channel_multiplier=-1),
                 reads=[onesf], writes=[mtmp])
            cp("dve", swam, mtmp)
            for o in range(2):
                dma("sp", gqk[o][0].ap, qn_d[o:o + 1, :].partition_broadcast(128), [], [gqk[o][0]])
                dma("sp", gqk[o][1].ap, kn_d[o:o + 1, :].partition_broadcast(128), [], [gqk[o][1]])
                dma("sp", esink[o].ap, sinks_d[o:o + 1, :].partition_broadcast(128), [], [esink[o]])
                act(esink[o], esink[o], AF.Exp)
                for k_ in range(2):
                    memset("pool", Vb[o][k_], 1.0)
                    memset("pool", KTr[o][k_][0], 0.0)
                    memset("pool", KTr[o][k_][1], 0.0)

        cur[0] = SCR
        aT = [alloc((8, T), BF16) for _ in range(2)]
        rl = [alloc((T,), F32) for _ in range(2)]
        mlp_n = [0]
        mlp_r = [0]

        def mlp(l, first_tile):
            norm_to_hT(nmlp_d[l:l + 1, :])
            for r in range(4):
                Wi = wload(wmi_b, "wmi", l, r * 1024, 1024)
                Wo = wload_rows(wmo_b, "wmo", l, r * 8)
                a_ = aT[mlp_r[0] % 2]
                mlp_r[0] += 1
                for j in range(8):
                    pu = bank(mlp_n[0] % 2)
                    r_ = rl[mlp_n[0] % 2]
                    mlp_n[0] += 1
                    for c in range(8):
                        mm(pu, Wi[:, c, j * 128:(j + 1) * 128], hT[:, c, :], c == 0, c == 7)
                    act(r_, pu, AF.Relu)
                    tt("pool", a_[:, j, :], r_, r_, ALU.mult)
                    if first_tile:
                        pump(n=1)
                for s in range(NSUB):
                    py = [bank(2 + 2 * (s % 2)), bank(3 + 2 * (s % 2))]
                    for h in range(2):
                        for j in range(8):
                            mm(py[h], a_[:, j, s * 128:(s + 1) * 128], Wo[:, j, h * 512:(h + 1) * 512], j == 0, j == 7)
                    for h in range(2):
                        tt("dve", xs[s][:, h * 512:(h + 1) * 512], xs[s][:, h * 512:(h + 1) * 512], py[h], ALU.add)

        def interleave(*gens):
            gens = [g for g in gens if g is not None]
            while gens:
                for g in list(gens):
                    try:
                        next(g)
                    except StopIteration:
                        gens.remove(g)

        cur[0] = SCR
        sqb = alloc((1280,), F32)
        zn = alloc((1280,), F32)
        rt = [alloc((20, 32), F32) for _ in range(4)]
        qkr = alloc((1280,), BF16)
        kdup = alloc((512,), BF16)
        QT = [alloc((8, 128), BF16) for _ in range(2)]
        PT = [alloc((4, 2, 128), BF16) for _ in range(2)]
        atok = alloc((1024,), BF16)
        ostA = alloc((64,), F32)
        ostB = alloc((64,), F32)
        kst = alloc((4, 128), BF16)
        vst = alloc((4, 64), BF16)

        def odd_A(o, it, s, WA, WB):
            gb = it * NSUB + s
            sl = slice(s * 128, (s + 1) * 128)
            own = gb % 2
            for n in range(2):
                for c in range(8):
                    mm(bank(n), hT[:, c, sl], WA[:, c, n * 512:(n + 1) * 512], c == 0, c == 7)
                yield
            for c in range(8):
                mm(bank(2), hT[:, c, sl], WB[:, c, :], c == 0, c == 7)
            yield
            cp("act", vst, PS.view(2 * 2048 + 1024, (4, 64), F32))
            for (b_, h0, h1) in ((0, 0, 8), (1, 8, 16), (2, 16, 20)):
                act(sqb[:, h0 * 64:h1 * 64], PS.view(b_ * 2048, ((h1 - h0) * 64,), F32), AF.Square)
                yield
            ssq = ostA[:, 0:20]
            rq_ = ostA[:, 20:40]
            rq = ostA[:, 40:60]
            P.op("dve", lambda e: e.tensor_reduce(out=ssq.ap, in_=v3(sqb, 20).ap, axis=AX.X, op=ALU.add), reads=[sqb], writes=[ssq])
            yield
            rstd_of(rq, rq_, ssq, 1.0 / 64.0)
            yield
            zn3 = v3(zn, 20)
            for (b_, h0, h1) in ((0, 0, 8), (1, 8, 16), (2, 16, 20)):
                tt("dve", zn3[:, h0:h1, :], PS.view(b_ * 2048, (h1 - h0, 64), F32),
                   View(rq.ap[:, h0:h1].unsqueeze(2).to_broadcast([128, h1 - h0, 64]), rq.toks), ALU.mult)
                yield
            tt("pool", zn3[:, 0:16, :], zn3[:, 0:16, :], bcm(gqk[o][0], [128, 16, 64]), ALU.mult)
            tt("dve", zn3[:, 16:20, :], zn3[:, 16:20, :], bcm(gqk[o][1], [128, 4, 64]), ALU.mult)
            yield
            cosb = bcm(ctab[:, gb, :], [128, 20, 32])
            sinb = bcm(stab[:, gb, :], [128, 20, 32])
            x1 = zn3[:, :, 0:32]
            x2 = zn3[:, :, 32:64]
            q3 = v3(qkr, 20)
            tt("pool", rt[0], x1, cosb, ALU.mult)
            tt("dve", rt[1], x2, sinb, ALU.mult)
            yield
            tt("pool", rt[2], x2, cosb, ALU.mult)
            tt("dve", rt[3], x1, sinb, ALU.mult)
            yield
            tt("pool", q3[:, :, 0:32], rt[0], rt[1], ALU.subtract)
            tt("dve", q3[:, :, 32:64], rt[2], rt[3], ALU.add)
            yield
            pT = bank(7, (8, 128), BF16)
            for c in range(8):
                tr(pT[:, c, :], qkr[:, c * 128:(c + 1) * 128])
            yield
            cp("act", QT[s % 2], pT)
            kd4 = View(kdup.ap.rearrange("p (g r d) -> p g r d", g=4, r=2), kdup.toks)
            ksrc = View(q3.ap[:, 16:20, :].unsqueeze(2).to_broadcast([128, 4, 2, 64]), qkr.toks)
            cp("pool", kd4, ksrc)
            yield
            pTk = bank(6, (4, 128), BF16)
            for g in range(4):
                tr(pTk[:, g, :], kdup[:, g * 128:(g + 1) * 128])
            cp("dve", kst, pTk)
            yield

        def odd_commit(o, it, s):
            own = (it * NSUB + s) % 2
            cp("dve", KTr[o][own][0][0:64], kst[0:64])
            cp("pool", KTr[o][own][1][64:128], kst[64:128])
            cp("dve", Vb[o][own][:, :, 0:64], vst)

        def odd_B(o, it, s, WO):
            gb = it * NSUB + s
            sl = slice(s * 128, (s + 1) * 128)
            own, prv = gb % 2, (gb + 1) % 2
            qt_ = QT[s % 2]
            a3 = v3(atok, 16)
            kbs = [1] if gb == 0 else [0, 1]
            for g in range(4):
                pS = PS.view(3 * 2048, (4, 2, 128), F32)
                for hh in range(4):
                    h = 4 * g + hh
                    c, half = h // 2, h % 2
                    for kb in kbs:
                        slot = prv if kb == 0 else own
                        mm(pS[:, hh, kb, :], KTr[o][slot][half][:, g, :], qt_[:, c, :], True, True)
                yield
                pt_ = PT[g % 2]
                if gb == 0:
                    act(pt_[:, :, 1, :], pS[:, :, 1, :], AF.Exp, scale=0.125)
                    yield
                    tt("pool", pt_[:, :, 1, :], pt_[:, :, 1, :], bcm(swam[:, 1, :], [128, 4, 128]), ALU.mult)
                else:
                    act(pt_, pS, AF.Exp, scale=0.125)
                    yield
                    tt("pool", pt_, pt_, View(swam.ap.unsqueeze(1).to_broadcast([128, 4, 2, 128]), swam.toks), ALU.mult)
                yield
                pO = bank(5 + (g % 2), (4, 65), F32)
                for hh in range(4):
                    for kb in kbs:
                        slot = prv if kb == 0 else own
                        mm(pO[:, hh, :], pt_[:, hh, kb, :], Vb[o][slot][:, g, :], kb == kbs[0], kb == kbs[-1])
                dn = ostB[:, 4 * g:4 * g + 4]
                tt("dve", dn, pO[:, :, 64], esink[o][:, 4 * g:4 * g + 4], ALU.add)
                recip(dn, dn)
                tt("dve", a3[:, 4 * g:4 * g + 4, :], pO[:, :, 0:64],
                   View(dn.ap.unsqueeze(2).to_broadcast([128, 4, 64]), dn.toks), ALU.mult)
                yield
            pT = bank(5, (8, 128), BF16)
            for c in range(8):
                tr(pT[:, c, :], atok[:, c * 128:(c + 1) * 128])
            yield
            cp("act", yT[:, :, sl], pT)
            yield
            for hf in range(2):
                for c in range(8):
                    mm(bank(3 + hf), yT[:, c, sl], WO[:, c, hf * 512:(hf + 1) * 512], c == 0, c == 7)
                yield
            for hf in range(2):
                tt("dve", xs[s][:, hf * 512:(hf + 1) * 512], xs[s][:, hf * 512:(hf + 1) * 512], bank(3 + hf), ALU.add)
            yield

        def odd_layer(l, it):
            o = l // 2
            norm_to_hT(nmix_d[l:l + 1, :])
            WA = wload(wio_b, "wio", o, 0, 1024)
            WB = wload(wio_b, "wio", o, 1024, 512)
            WO = wload(woo_b, "woo", o, 0, 1024)
            if it == 0:
                pump(n=16)
            interleave(odd_A(o, it, 0, WA, WB))
            odd_commit(o, it, 0)
            for s in range(NSUB):
                if it == 0:
                    pump(n=16)
                interleave(odd_B(o, it, s, WO), odd_A(o, it, s + 1, WA, WB) if s + 1 < NSUB else None)
                if s + 1 < NSUB:
                    odd_commit(o, it, s + 1)

        cur[0] = SCR
        hA, hB, hK, hC, hD = (alloc((T,), F32) for _ in range(5))
        rA, rB, rC, rD, rE = (alloc((T,), F32) for _ in range(5))
        Qz = [alloc((4, T), BF16) for _ in range(2)]
        Kz = [alloc((4, T), BF16) for _ in range(2)]
        Ktok = [alloc((NSUB, 512), BF16) for _ in range(2)]
        Vtok = alloc((NSUB, 512), BF16)
        Gs = alloc((NSUB, 512), BF16)
        ATb = [alloc((4, 128), BF16) for _ in range(2)]
        T2s = alloc((4, 128), F32)
        Spb = [[alloc((4, 128), BF16) for _ in range(2)] for _ in range(2)]
        sqe = alloc((4, 128), F32)
        tno = alloc((4, 128), F32)
        yat = alloc((512,), BF16)
        etab = alloc((4, 4, 8), F32)
        est = alloc((16,), F32)
        XB = alloc((516,), F32)
        xcb = alloc((T,), BF16)
        LB, OML, NOML, CL, CL2, NBA, NBX = 0, 8, 16, 24, 32, 40, 48

        def ev_gates(e, h, W1):
            Lprev, Emu, Elast, Gt = (etab[:, i] for i in range(4))
            pq = bank(0)
            pf = bank(1)
            for c in range(8):
                mm(pf, W1[:, c, 512 + h * 128:512 + (h + 1) * 128], hT[:, c, :], c == 0, c == 7)
            yield
            for c in range(8):
                mm(pq, W1[:, c, h * 128:(h + 1) * 128], hT[:, c, :], c == 0, c == 7)
            yield
            col = e * 4 + h
            act(hA, pf, AF.Exp, scale=-1.0)
            yield
            act(hA, hA, AF.Ln, bias=1.0)
            yield
            act(hA, hA, AF.Exp, scale=-1.0)
            yield
            ts("dve", hB, hA, ecst[:, OML + col:OML + col + 1], ecst[:, LB + col:LB + col + 1], ALU.mult, ALU.add)
            ts("pool", hK, hA, ecst[:, NOML + col:NOML + col + 1], ecst[:, OML + col:OML + col + 1], ALU.mult, ALU.add)
            yield
            ts("dve", hB, hB, 1e-30, None, ALU.max)
            yield
            act(hB, hB, AF.Ln)
            yield
            P.op("dve", lambda en: en.tensor_tensor_scan(out=hC.ap, data0=onesT.ap, data1=hB.ap,
                                                         initial=Lcar.ap[:, col:col + 1],
                                                         op0=ALU.mult, op1=ALU.add),
                 reads=[onesT, hB, Lcar], writes=[hC])
            yield
            C3 = v3(hC, 8)
            D3 = v3(hD, 8)
            cp("pool", Lprev[:, h, 0:1], Lcar[:, col:col + 1])
            cp("pool", Lprev[:, h, 1:8], C3[:, 0:7, 63])
            cp("pool", Lcar[:, col:col + 1], hC[:, T - 1:T])
            tt("dve", D3, C3, View(C3.ap[:, :, 31:32].to_broadcast([128, 8, 64]), hC.toks), ALU.subtract)
            yield
            act(hA, hD, AF.Exp)
            yield
            act(hB, hD, AF.Exp, scale=-1.0)
            tt("dve", Emu[:, h, :], C3[:, :, 31], Lprev[:, h, :], ALU.subtract)
            yield
            for j in range(2):
                def par(v, j=j):
                    return View(v.ap.rearrange("p (s j t) -> p s j t", s=4, j=2)[:, :, j, :], v.toks)
                tt("dve", par(Qz[j][:, h, :]), par(pq), par(hA), ALU.mult)
                tt("pool", par(Kz[j][:, h, :]), par(hK), par(hB), ALU.mult)
                yield
            act(Emu[:, h, :], Emu[:, h, :], AF.Exp)
            act(Elast[:, h, :], D3[:, :, 63], AF.Exp)
            yield
            tt("dve", Gt[:, h, 0:7], Elast[:, h, 0:7], Emu[:, h, 1:8], ALU.mult)
            cp("dve", Gt[:, h, 7:8], Elast[:, h, 7:8])
            yield
            for j in range(2):
                pT = bank(2 + j, (4, 128), BF16)
                for s in range(NSUB):
                    tr(pT[:, s, :], Kz[j][:, h, s * 128:(s + 1) * 128])
                yield
                cp("act" if j == 0 else "dve", Ktok[j][:, :, h * 128:(h + 1) * 128], pT)
                yield

        def ev_rglru(e, m, W3):
            R_, I_, T1, Gg, T2r = rA, rB, rC, rD, rE
            col = e * 4 + m
            pg, px = bank(4), bank(5)
            for c in range(8):
                mm(px, W3[:, c, 512 + m * 128:512 + (m + 1) * 128], hT[:, c, :], c == 0, c == 7)
            yield
            for c in range(8):
                mm(pg, W3[:, c, m * 128:(m + 1) * 128], hT[:, c, :], c == 0, c == 7)
            yield
            cp("pool", XB[:, 0:3], halo[:, col, :])
            cp("act", XB[:, 3:515], px)
            yield
            xc = T2r

            def cwc(j):
                k_ = SM_CONVW + e * 16 + j * 4 + m
                return small[:, k_:k_ + 1]
            ts("dve", xc, XB[:, 3:515], cwc(3), small[:, SM_CONVB + col:SM_CONVB + col + 1], ALU.mult, ALU.add)
            yield
            for k_ in (1, 2, 3):
                stt("dve", xc, XB[:, 3 - k_:515 - k_], cwc(3 - k_), xc, ALU.mult, ALU.add)
                yield
            cp("pool", halo[:, col, :], XB[:, 512:515])
            cp("act", xcb, xc)
            yield
            pr, pi = bank(6), bank(7)
            mm(pr, WBD[:, e * 8 + m, :], xcb, True, True)
            mm(pi, WBD[:, e * 8 + 4 + m, :], xcb, True, True)
            yield
            act(R_, pr, AF.Exp, bias=ecst[:, NBA + col:NBA + col + 1], scale=-1.0)
            yield
            act(I_, pi, AF.Exp, bias=ecst[:, NBX + col:NBX + col + 1], scale=-1.0)
            yield
            act(R_, R_, AF.Ln, bias=1.0)
            yield
            act(I_, I_, AF.Ln, bias=1.0)
            yield
            act(R_, R_, AF.Exp, scale=-1.0)
            yield
            act(I_, I_, AF.Exp, scale=-1.0)
            yield
            act(T1, R_, AF.Exp, scale=ecst[:, CL2 + col:CL2 + col + 1])
            tt("pool", I_, I_, xc, ALU.mult)
            yield
            ts("dve", T1, T1, -1.0, 1.0, ALU.mult, ALU.add)
            yield
            ts("dve", T1, T1, 1e-30, None, ALU.max)
            act(R_, R_, AF.Exp, scale=ecst[:, CL + col:CL + col + 1])
            yield
            act(T1, T1, AF.Ln)
            yield
            act(T1, T1, AF.Exp, scale=0.5)
            yield
            tt("dve", I_, I_, T1, ALU.mult)
            yield
            P.op("dve", lambda en: en.tensor_tensor_scan(out=T1.ap, data0=R_.ap, data1=I_.ap,
                                                         initial=hcar.ap[:, col:col + 1],
                                                         op0=ALU.mult, op1=ALU.add),
                 reads=[R_, I_, hcar], writes=[T1])
            yield
            cp("pool", hcar[:, col:col + 1], T1[:, T - 1:T])
            cp("act", Gg, pg)
            yield
            act(T2r, pg, AF.Square)
            yield
            ts("dve", T2r, T2r, 0.044715, 1.0, ALU.mult, ALU.add)
            yield
            tt("dve", T2r, T2r, Gg, ALU.mult)
            yield
            act(T2r, T2r, AF.Exp, scale=-2.0 * math.sqrt(2.0 / math.pi))
            yield
            act(T2r, T2r, AF.Ln, bias=1.0)
            yield
            act(T2r, T2r, AF.Exp, scale=-1.0)
            tt("pool", Gg, Gg, T1, ALU.mult)
            yield
            tt("dve", yT[:, 4 + m, :], Gg, T2r, ALU.mult)
            yield

        def ev_X(e, s):
            Lprev, Emu, Elast, Gt = (etab[:, i] for i in range(4))
            sl = slice(s * 128, (s + 1) * 128)
            Spe = Sp[e]
            pS = bank(6, (4, 128), F32)
            for h in range(4):
                for j in range(2):
                    mm(pS[:, h, :], Kz[j][:, h, sl], Qz[j][:, h, sl], j == 0, j == 1)
            yield
            pkv = [bank(0, (4, 128), F32), bank(1, (4, 128), F32)]
            for j in range(2):
                for h in range(4):
                    mm(pkv[j][:, h, :], Ktok[j][:, s, h * 128:(h + 1) * 128], Vtok[:, s, h * 128:(h + 1) * 128], True, True)
                yield
            tt("dve", ATb[s % 2], pS, bcm(bmask, [128, 4, 128]), ALU.mult)
            yield
            if s == 0:
                tt("dve", Spe, Spe, View(Emu.ap[:, :, 0:1].to_broadcast([128, 4, 128]), etab.toks), ALU.mult)
                yield
            cp("act", Spb[s % 2][0], Spe)
            yield
            for j in range(2):
                cch = 2 * s + j
                tt("dve", T2s, pkv[j], Spe, ALU.add)
                yield
                tt("dve", Spe, T2s, View(Gt.ap[:, :, cch:cch + 1].to_broadcast([128, 4, 128]), etab.toks), ALU.mult)
                yield
                if j == 0:
                    cp("act", Spb[s % 2][1], Spe)
                    yield

        def ev_Y(e, s, WO):
            sl = slice(s * 128, (s + 1) * 128)
            at_ = ATb[s % 2]
            po = bank(2 + (s % 2), (4, 128), F32)
            for h in range(4):
                mm(po[:, h, :], at_[:, h, :], Vtok[:, s, h * 128:(h + 1) * 128], True, False)
                mm(po[:, h, :], Qz[0][:, h, sl], Spb[s % 2][0][:, h, :], False, False)
                mm(po[:, h, :], Qz[1][:, h, sl], Spb[s % 2][1][:, h, :], False, True)
            yield
            act(sqe, po, AF.Square)
            yield
            P.op("dve", lambda en: en.tensor_reduce(out=est.ap[:, 0:4], in_=sqe.ap, axis=AX.X, op=ALU.add), reads=[sqe], writes=[est])
            yield
            rstd_of(est[:, 8:12], est[:, 4:8], est[:, 0:4], 1.0 / 128.0)
            yield
            tt("dve", tno, po, View(est.ap[:, 8:12].unsqueeze(2).to_broadcast([128, 4, 128]), est.toks), ALU.mult)
            yield
            tno2 = View(tno.ap.rearrange("p h d -> p (h d)"), tno.toks)
            tt("pool", tno2, tno2, gainA[e], ALU.mult)
            yield
            tt("dve", yat, tno2, Gs[:, s, :], ALU.mult)
            yield
            pT = bank(7, (4, 128), BF16)
            for h in range(4):
                tr(pT[:, h, :], yat[:, h * 128:(h + 1) * 128])
            yield
            cp("act", yT[:, 0:4, sl], pT)
            yield
            for hf in range(2):
                for c in range(8):
                    mm(bank(4 + hf), yT[:, c, sl], WO[:, c, hf * 512:(hf + 1) * 512], c == 0, c == 7)
                yield
            for hf in range(2):
                tt("dve", xs[s][:, hf * 512:(hf + 1) * 512], xs[s][:, hf * 512:(hf + 1) * 512], bank(4 + hf), ALU.add)
            yield

        def even_layer(l, it):
            e = l // 2
            norm_to_hT(nmix_d[l:l + 1, :])
            W1 = wload(wie_b, "wie", e, 0, 1024)
            W2 = wload(wie_b, "wie", e, 1024, 1024)
            W3 = wload(wie_b, "wie", e, 2048, 1024)
            for s in range(NSUB):
                sl = slice(s * 128, (s + 1) * 128)
                for n in range(2):
                    for c in range(8):
                        mm(bank(4 + n), hT[:, c, sl], W2[:, c, n * 512:(n + 1) * 512], c == 0, c == 7)
                cp("act", Vtok[:, s, :], bank(4))
                sgt = View(tno.ap.rearrange("p h d -> p (h d)"), tno.toks)
                sigm(sgt, bank(5))
                tt("dve", Gs[:, s, :], sgt, bank(5), ALU.mult)
            for j in range(2):
                for z_ in (Qz[j], Kz[j]):
                    zv = View(z_.ap.rearrange("p h (s j t) -> p h s j t", s=4, j=2)[:, :, :, 1 - j, :], z_.toks)
                    memset("pool", zv, 0.0)
            for h in range(4):
                if it == 0:
                    pump(n=8)
                interleave(ev_gates(e, h, W1), ev_rglru(e, h, W3))
            WO = wload(woe_b, "woe", e, 0, 1024)
            interleave(ev_X(e, 0))
            for s in range(NSUB):
                if it == 0:
                    pump(n=8)
                interleave(ev_Y(e, s, WO), ev_X(e, s + 1) if s + 1 < NSUB else None)

        print("SBUF bytes used", cur[0], "of", SB_BYTES)
        if n_layers > 0:
            pump(upto_group=0)
        for it in range(n_tiles):
            for s in range(NSUB):
                r0 = it * T + s * 128
                dma("sp", xs[s].ap, x_d[r0:r0 + 128, :], [], [xs[s]])
            for li, l in enumerate(layers):
                gi = 2 * li
                if it == 0:
                    pump(upto_group=gi)
                if do_mixer:
                    if l % 2 == 1:
                        odd_layer(l, it)
                    else:
                        even_layer(l, it)
                if do_mlp:
                    if it == 0:
                        pump(upto_group=gi + 1)
                    mlp(l, it == 0)
            for s in range(NSUB):
                r0 = it * T + s * 128
                dma("sp", out_d[r0:r0 + 128, :], xs[s].ap, [xs[s]], [("D", "out", it, s)])
        P.emit()
    return nc


def _small_pack(inp):
    sm = np.zeros((128, NSM), np.float32)

    def fm(v):
        return np.ascontiguousarray(v.reshape(4, 128).T)

    for e in range(2):
        sm[:, SM_LBL + e * 4:SM_LBL + e * 4 + 4] = fm(inp["hgrn_lb_logits"][e])
        for j in range(4):
            sm[:, SM_CONVW + e * 16 + j * 4:SM_CONVW + e * 16 + j * 4 + 4] = fm(inp["conv_w"][e, j])
        sm[:, SM_CONVB + e * 4:SM_CONVB + e * 4 + 4] = fm(inp["conv_b"][e])
        sm[:, SM_BA + e * 4:SM_BA + e * 4 + 4] = fm(inp["rg_ba"][e])
        sm[:, SM_BX + e * 4:SM_BX + e * 4 + 4] = fm(inp["rg_bx"][e])
        sm[:, SM_LAM + e * 4:SM_LAM + e * 4 + 4] = fm(inp["rg_lambda"][e])
    return sm


_NC_CACHE = {}


def make_in_maps(inputs, n_cores=8):
    inp = {k: np.asarray(v) for k, v in inputs.items()}
    shared = {
        "norm_mix": inp["norm_mix"], "norm_mlp": inp["norm_mlp"],
        "w_mlp_in": inp["w_mlp_in"], "w_mlp_out": inp["w_mlp_out"],
        "w_in_even": inp["w_in_even"], "w_out_even": inp["w_out_even"],
        "w_in_odd": inp["w_in_odd"], "w_out_odd": inp["w_out_odd"],
        "small": _small_pack(inp), "hgrn_out_norm": inp["hgrn_out_norm"],
        "rg_wa": inp["rg_wa"], "rg_wx": inp["rg_wx"],
        "q_norm": inp["q_norm"], "k_norm": inp["k_norm"], "sinks": inp["sinks"],
    }
    shared = {k: np.ascontiguousarray(v, dtype=np.float32) for k, v in shared.items()}
    maps = []
    for b in range(n_cores):
        m = dict(shared)
        m["x"] = np.ascontiguousarray(inp["x"][b], dtype=np.float32)
        m["pos"] = np.ascontiguousarray(inp["positions"][b].reshape(SEQ // 128, 128).T.astype(np.int32))
        maps.append(m)
    return maps


def kernel(**inputs):
    key = "full"
    if key not in _NC_CACHE:
        _NC_CACHE[key] = build_nc()
    nc = _NC_CACHE[key]
    in_maps = make_in_maps(inputs, 8)
    res = run_bass_kernel_spmd(nc, in_maps, core_ids=list(range(8)))
    out = np.stack([np.asarray(r["out"]).reshape(SEQ, D) for r in res.results], axis=0)
    return out.astype(np.float32)
```

```python
import math
from contextlib import ExitStack

import numpy as np
import concourse.bass as bass
import concourse.mybir as mybir
from concourse.bass_utils import run_bass_kernel_spmd

F32 = mybir.dt.float32
BF16 = mybir.dt.bfloat16
I32 = mybir.dt.int32
AF = mybir.ActivationFunctionType
ALU = mybir.AluOpType
AX = mybir.AxisListType

GRAN = 256
ENGS = ("pe", "act", "dve", "pool", "sp")


class View:
    __slots__ = ("ap", "toks")

    def __init__(self, ap, toks):
        self.ap = ap
        self.toks = toks

    def __getitem__(self, key):
        return View(self.ap[key], self.toks)


class Arena:
    def __init__(self, tensor, space, nbytes):
        self.t = tensor
        self.space = space
        self.nbytes = nbytes

    def view(self, off, shape, dtype, p0=0, p1=128):
        esz = mybir.dt.size(dtype)
        n = 1
        for s in shape:
            n *= s
        nb = n * esz
        assert off % 4 == 0 and nb % 4 == 0, (off, nb)
        assert off + nb <= self.nbytes, (off, nb, self.nbytes)
        ap = self.t[p0:p1, off // 4:(off + nb) // 4]
        if dtype != F32:
            ap = ap.bitcast(dtype)
        if len(shape) == 2:
            ap = ap.rearrange("p (a b) -> p a b", a=shape[0])
        elif len(shape) == 3:
            ap = ap.rearrange("p (a b c) -> p a b c", a=shape[0], b=shape[1])
        toks = tuple((self.space, g) for g in range(off // GRAN, (off + nb - 1) // GRAN + 1))
        return View(ap, toks)


class Op:
    __slots__ = ("eng", "fn", "reads", "writes", "dma", "idx", "waits", "sig", "sigval", "dsem", "dval", "prewait")

    def __init__(self, eng, fn, reads, writes, dma):
        self.eng = eng
        self.fn = fn
        self.reads = reads
        self.writes = writes
        self.dma = dma
        self.waits = []
        self.sig = False
        self.sigval = 0
        self.dsem = None
        self.dval = 0
        self.prewait = None


def _toks(xs):
    out = []
    for x in xs:
        if x is None:
            continue
        if isinstance(x, View):
            out.extend(x.toks)
        else:
            out.append(x)
    return out


class Prog:
    def __init__(self, nc, n_dma_sems=12):
        self.nc = nc
        self.ops = []
        self.n_dma_sems = n_dma_sems

    def op(self, eng, fn, reads=(), writes=(), dma=False):
        o = Op(eng, fn, _toks(reads), _toks(writes), dma)
        o.idx = len(self.ops)
        self.ops.append(o)
        return o

    def analyze(self):
        last_w = {}
        readers = {}
        per_eng = {e: [] for e in ENGS}
        need = []
        ops = self.ops
        for o in ops:
            deps = set()
            raw = set()
            for t in o.reads:
                w = last_w.get(t)
                if w is not None:
                    deps.add(w)
                    raw.add(w)
            for t in o.writes:
                w = last_w.get(t)
                if w is not None:
                    deps.add(w)
                r = readers.get(t)
                if r:
                    deps.update(r)
            for t in o.writes:
                last_w[t] = o.idx
                readers[t] = []
            for t in o.reads:
                readers.setdefault(t, []).append(o.idx)
            deps.discard(o.idx)
            per_eng[o.eng].append(o)
            nd = []
            for j in deps:
                p = ops[j]
                if p.dma or p.eng != o.eng or o.dma:
                    nd.append(j)
                elif o.eng == "pool" or (o.eng != "pe" and j in raw):
                    nd.append(j)
            for j in nd:
                if not ops[j].dma:
                    ops[j].sig = True
            need.append(nd)
        self.per_eng = per_eng
        self.dma_count = {}
        for e in ENGS:
            c = 0
            n = 0
            for o in per_eng[e]:
                if o.dma:
                    o.dsem = (e, n % self.n_dma_sems)
                    o.dval = 16 * (n // self.n_dma_sems + 1)
                    if n >= self.n_dma_sems:
                        o.prewait = (o.dsem, o.dval - 16)
                    n += 1
                elif o.sig:
                    c += 1
                    o.sigval = c
            self.dma_count[e] = n
        known = {e: {} for e in ENGS}
        for o, nd in zip(ops, need):
            k = known[o.eng]
            w = {}
            if o.prewait is not None:
                s, v = o.prewait
                if k.get(s, 0) < v:
                    w[s] = v
            for j in nd:
                p = ops[j]
                if p.dma:
                    s, v = p.dsem, p.dval
                else:
                    s, v = ("c", p.eng), p.sigval
                if k.get(s, 0) < v and w.get(s, 0) < v:
                    w[s] = v
            for s, v in w.items():
                k[s] = v
            o.waits = list(w.items())

    def emit(self, final_waits_eng="sp"):
        nc = self.nc
        self.analyze()
        with ExitStack() as es:
            sems = {}
            for e in ENGS:
                sems[("c", e)] = es.enter_context(nc.semaphore("c_" + e))
                for i in range(min(self.n_dma_sems, self.dma_count[e])):
                    sems[(e, i)] = es.enter_context(nc.semaphore("d_%s_%d" % (e, i)))
            block = es.enter_context(nc.Block())

            def run(e):
                def body(eng):
                    for o in self.per_eng[e]:
                        for s, v in o.waits:
                            eng.wait_ge(sems[s], v)
                        ins = o.fn(eng)
                        if o.dma:
                            ins.then_inc(sems[o.dsem], 16)
                        elif o.sig:
                            ins.then_inc(sems[("c", e)], 1)
                    if e == final_waits_eng:
                        for e2 in ENGS:
                            n = self.dma_count[e2]
                            for i in range(min(self.n_dma_sems, n)):
                                cnt = (n - i + self.n_dma_sems - 1) // self.n_dma_sems
                                eng.wait_ge(sems[(e2, i)], 16 * cnt)
                return body

            block.tensor(run("pe"))
            block.scalar(run("act"))
            block.vector(run("dve"))
            block.gpsimd(run("pool"))
            block.sync(run("sp"))


D = 1024
SEQ = 4096
DEPTH = 4
DFF = 4096
T = 512
NT = SEQ // T
NSUB = T // 128
EPS = 1e-6
TWO_PI = 2.0 * math.pi

SM_LBL = 0
SM_CONVW = 8
SM_CONVB = 40
SM_BA = 48
SM_BX = 56
SM_LAM = 64
NSM = 72


def build_nc(n_tiles=NT, layers=(0, 1, 2, 3), do_mixer=True, do_mlp=True):
    n_layers = len(layers)
    nc = bass.Bass("TRN2", target_bir_lowering=False)
    dt_in = lambda name, shape, dt=F32: nc.dram_tensor(name, shape, dt, kind="ExternalInput").ap()
    x_d = dt_in("x", [SEQ, D])
    pos_d = dt_in("pos", [128, SEQ // 128], I32)
    nmix_d = dt_in("norm_mix", [DEPTH, D])
    nmlp_d = dt_in("norm_mlp", [DEPTH, D])
    wmi_d = dt_in("w_mlp_in", [DEPTH, D, DFF])
    wmo_d = dt_in("w_mlp_out", [DEPTH, DFF, D])
    wie_d = dt_in("w_in_even", [2, D, 3072])
    woe_d = dt_in("w_out_even", [2, D, D])
    wio_d = dt_in("w_in_odd", [2, D, 1536])
    woo_d = dt_in("w_out_odd", [2, D, D])
    small_d = dt_in("small", [128, NSM])
    hon_d = dt_in("hgrn_out_norm", [2, 512])
    rgwa_d = dt_in("rg_wa", [2, 8, 64, 64])
    rgwx_d = dt_in("rg_wx", [2, 8, 64, 64])
    qn_d = dt_in("q_norm", [2, 64])
    kn_d = dt_in("k_norm", [2, 64])
    sinks_d = dt_in("sinks", [2, 16])
    out_d = nc.dram_tensor("out", [SEQ, D], F32, kind="ExternalOutput").ap()

    def scratch(name, shape):
        return nc.dram_tensor(name, shape, BF16, kind="Internal").ap()

    wmi_b = scratch("wmi_b", [DEPTH, D, DFF])
    wmo_b = scratch("wmo_b", [DEPTH, DFF, D])
    wie_b = scratch("wie_b", [2, D, 3072])
    woe_b = scratch("woe_b", [2, D, D])
    wio_b = scratch("wio_b", [2, D, 1536])
    woo_b = scratch("woo_b", [2, D, D])

    P = Prog(nc)
    with ExitStack() as es:
        SB_BYTES = 206 * 1024
        sb_t = es.enter_context(nc.sbuf_tensor("arena", [128, SB_BYTES // 4], F32))
        ps_t = es.enter_context(nc.psum_tensor("psarena", [128, 4096], F32))
        SB = Arena(sb_t, "S", SB_BYTES)
        PS = Arena(ps_t, "P", 16384)
        cur = [0]

        def alloc(shape, dt, p0=0, p1=128):
            n = 1
            for s_ in shape:
                n *= s_
            nb = (n * mybir.dt.size(dt) + GRAN - 1) // GRAN * GRAN
            v = SB.view(cur[0], shape, dt, p0, p1)
            cur[0] += nb
            return v

        def bank(i, shape=(512,), dt=F32, off=0):
            return PS.view(i * 2048 + off, shape, dt)

        def mm(out, lhsT, rhs, start, stop):
            P.op("pe", lambda e: e.matmul(out.ap, lhsT=lhsT.ap, rhs=rhs.ap, start=start, stop=stop),
                 reads=[lhsT, rhs], writes=[out])

        def tr(out, in_):
            P.op("pe", lambda e: e.transpose(out=out.ap, in_=in_.ap, identity=ident.ap),
                 reads=[in_, ident], writes=[out])

        def act(out, in_, func, bias=None, scale=None, accum=None, rd=()):
            kw = {}
            rds = [in_] + list(rd)
            if bias is not None:
                if isinstance(bias, View):
                    kw["bias"] = bias.ap
                    rds.append(bias)
                else:
                    kw["bias"] = bias
            if scale is not None:
                if isinstance(scale, View):
                    kw["scale"] = scale.ap
                    rds.append(scale)
                else:
                    kw["scale"] = scale
            wr = [out]
            if accum is not None:
                kw["accum_out"] = accum.ap
                wr.append(accum)
            P.op("act", lambda e: e.activation(out=out.ap, in_=in_.ap, func=func, **kw), reads=rds, writes=wr)

        def tt(eng, out, in0, in1, op):
            P.op(eng, lambda e: e.tensor_tensor(out=out.ap, in0=in0.ap, in1=in1.ap, op=op), reads=[in0, in1], writes=[out])

        def ts(eng, out, in0, s1, s2, op0, op1=None):
            rds = [in0]
            a1 = s1
            a2 = s2
            if isinstance(s1, View):
                rds.append(s1)
                a1 = s1.ap
            if isinstance(s2, View):
                rds.append(s2)
                a2 = s2.ap
            if op1 is None:
                P.op(eng, lambda e: e.tensor_scalar(out=out.ap, in0=in0.ap, scalar1=a1, scalar2=None, op0=op0), reads=rds, writes=[out])
            else:
                P.op(eng, lambda e: e.tensor_scalar(out=out.ap, in0=in0.ap, scalar1=a1, scalar2=a2, op0=op0, op1=op1), reads=rds, writes=[out])

        def stt(eng, out, in0, sc, in1, op0, op1):
            rds = [in0, in1]
            a = sc
            if isinstance(sc, View):
                rds.append(sc)
                a = sc.ap
            P.op(eng, lambda e: e.scalar_tensor_tensor(out=out.ap, in0=in0.ap, scalar=a, in1=in1.ap, op0=op0, op1=op1), reads=rds, writes=[out])

        def cp(eng, out, in_):
            if eng == "act":
                P.op("act", lambda e: e.copy(out=out.ap, in_=in_.ap), reads=[in_], writes=[out])
            else:
                P.op(eng, lambda e: e.tensor_copy(out=out.ap, in_=in_.ap), reads=[in_], writes=[out])

        def memset(eng, out, val):
            P.op(eng, lambda e: e.memset(out.ap, val), writes=[out])

        def recip(out, in_):
            P.op("dve", lambda e: e.reciprocal(out=out.ap, in_=in_.ap), reads=[in_], writes=[out])

        def dma(q, out_ap, in_ap, reads, writes):
            P.op(q, lambda e: e.dma_start(out=out_ap, in_=in_ap), reads=reads, writes=writes, dma=True)

        def sigm(out, in_, bias=None, scale=None, neg_bias=None):
            nscale = -1.0 if scale is None else -scale
            act(out, in_, AF.Exp, bias=neg_bias, scale=nscale)
            act(out, out, AF.Ln, bias=1.0)
            act(out, out, AF.Exp, scale=-1.0)

        def rstd_of(out, tmp, ss, scale):
            act(tmp, ss, AF.Ln, bias=EPS, scale=scale)
            act(out, tmp, AF.Exp, scale=-0.5)

        def bc(v, shape):
            return View(v.ap.to_broadcast(shape), v.toks)

        def bcm(v, shape):
            return View(v.ap.unsqueeze(1).to_broadcast(shape), v.toks)

        xs = [alloc((D,), F32) for _ in range(NSUB)]
        hT = alloc((8, T), BF16)
        yT = alloc((8, T), BF16)
        NRING = 3
        ring = [alloc((8, 1024), BF16) for _ in range(NRING)]
        gbc = [alloc((D,), F32) for _ in range(1)]
        ident = alloc((128,), BF16)
        identf = alloc((128,), F32)
        small = alloc((NSM,), F32)
        stat = alloc((16,), F32)
        hb = [alloc((D,), BF16) for _ in range(2)]
        stats = [alloc((64,), F32) for _ in range(NSUB)]

        memset("pool", identf, 0.0)
        P.op("pool", lambda e: e.affine_select(out=identf.ap, in_=identf.ap, compare_op=ALU.not_equal, fill=1.0,
                                               base=0, pattern=[[-1, 128]], channel_multiplier=1),
             reads=[identf], writes=[identf])
        cp("dve", ident, identf)
        dma("sp", small.ap, small_d, [], [small])

        CW = 1024
        NSTG = 2
        stg = [alloc((CW,), F32) for _ in range(NSTG)]
        stgb = [alloc((CW,), BF16) for _ in range(NSTG)]
        conv_state = {"n": 0}

        def conv_chunks(src, dst, name, idx, R, C):
            out = []
            cw = C if C <= CW else (CW if C % CW == 0 else 512)
            for r0 in range(0, R, 128):
                for c0 in range(0, C, cw):
                    def f(r0=r0, c0=c0):
                        k = conv_state["n"] % NSTG
                        ce = ("act", "dve")[conv_state["n"] % 2]
                        conv_state["n"] += 1
                        s_, b_ = stg[k], stgb[k]
                        dma("sp", s_.ap[:, 0:cw], src[idx, r0:r0 + 128, c0:c0 + cw], [], [s_])
                        cp(ce, b_[:, 0:cw], s_[:, 0:cw])
                        dma("sp", dst[idx, r0:r0 + 128, c0:c0 + cw], b_.ap[:, 0:cw], [b_], [("D", name, idx, r0 // 128)])
                    out.append(f)
            return out

        conv_groups = []
        for l in layers:
            g = []
            if do_mixer:
                if l % 2 == 0:
                    g += conv_chunks(wie_d, wie_b, "wie", l // 2, D, 3072)
                    g += conv_chunks(woe_d, woe_b, "woe", l // 2, D, D)
                else:
                    g += conv_chunks(wio_d, wio_b, "wio", l // 2, D, 1536)
                    g += conv_chunks(woo_d, woo_b, "woo", l // 2, D, D)
            conv_groups.append(g)
            g = []
            if do_mlp:
                ci = conv_chunks(wmi_d, wmi_b, "wmi", l, D, DFF)
                co = conv_chunks(wmo_d, wmo_b, "wmo", l, DFF, D)
                g += ci + co
            conv_groups.append(g)
        conv_flat = [f for g in conv_groups for f in g]
        conv_pos = [0]

        def pump(upto_group=None, n=None):
            if upto_group is not None:
                tgt = sum(len(g) for g in conv_groups[:upto_group + 1])
            else:
                tgt = min(len(conv_flat), conv_pos[0] + n)
            while conv_pos[0] < tgt:
                conv_flat[conv_pos[0]]()
                conv_pos[0] += 1

        def wtoks(name, idx, rows):
            return [("D", name, idx, r) for r in rows]

        ring_n = [0]

        def wload(src_b, name, idx, c0, cw, rows=None):
            slot = ring[ring_n[0] % NRING]
            ring_n[0] += 1
            v = slot if cw == 1024 else slot[:, :, 0:cw]
            dma("sp", v.ap, src_b[idx].rearrange("(c p) f -> p c f", p=128)[:, :, c0:c0 + cw],
                wtoks(name, idx, range(8)), [slot])
            return v

        def wload_rows(src_b, name, idx, r0):
            slot = ring[ring_n[0] % NRING]
            ring_n[0] += 1
            dma("sp", slot.ap, src_b[idx, r0 * 128:(r0 + 8) * 128, :].rearrange("(j p) d -> p j d", p=128),
                wtoks(name, idx, range(r0, r0 + 8)), [slot])
            return slot

        gb_n = [0]

        def load_gain(gain_row_ap):
            dma("sp", gbc[0].ap, gain_row_ap.partition_broadcast(128), [], [gbc[0]])

        def norm_gen(s):
            g = gbc[0]
            st_ = stats[s]
            ss, rs, rr = st_[:, 0:1], st_[:, 1:2], st_[:, 2:3]
            h_ = hb[s % 2]
            act(h_, xs[s], AF.Square, scale=1.0 / 32.0, accum=ss)
            yield
            rstd_of(rr, rs, ss, 1.0)
            yield
            stt("dve", h_, xs[s], rr, g, ALU.mult, ALU.mult)
            yield
            pT = bank(7, (8, 128), BF16)
            for c in range(8):
                tr(pT[:, c, :], h_[:, c * 128:(c + 1) * 128])
            cp("act", hT[:, :, s * 128:(s + 1) * 128], pT)
            yield

        def norm_to_hT(gain_row_ap):
            load_gain(gain_row_ap)
            for s in range(NSUB):
                for _ in norm_gen(s):
                    pass

        def v3(v, a):
            return View(v.ap.rearrange("p (a b) -> p a b", a=a), v.toks)

        NBLK = SEQ // 128
        ctab = alloc((NBLK, 32), F32)
        stab = alloc((NBLK, 32), F32)
        gqk = [[alloc((64,), F32) for _ in range(2)] for _ in range(2)]
        esink = [alloc((16,), F32) for _ in range(2)]
        swam = alloc((2, 128), BF16)
        Vb = [[alloc((4, 65), BF16) for _ in range(2)] for _ in range(2)]
        KTr = [[[alloc((4, 128), BF16) for _ in range(2)] for _ in range(2)] for _ in range(2)]
        Sp = [alloc((4, 128), F32) for _ in range(2)]
        gainA = [alloc((512,), F32) for _ in range(2)]
        WBD = alloc((16, 128), BF16)
        ones1 = alloc((1,), F32)
        onesT = View(ones1.ap.to_broadcast([128, T]), ones1.toks)
        bmask = alloc((128,), BF16)
        ecst = alloc((64,), F32)
        Lcar = alloc((8,), F32)
        hcar = alloc((8,), F32)
        halo = alloc((8, 3), F32)
        SCR = cur[0]

        has_even = do_mixer and any(l % 2 == 0 for l in layers)
        if has_even:
            LB, OML, NOML, CL, CL2, NBA, NBX = 0, 8, 16, 24, 32, 40, 48
            tmpc = alloc((64,), F32)
            l0 = small[:, SM_LBL:SM_LBL + 4]
            l1 = small[:, SM_LBL + 4:SM_LBL + 8]
            mx, e0, e1, sd, sm0, sm1 = (tmpc[:, 4 * i:4 * i + 4] for i in range(6))
            tt("dve", mx, l0, l1, ALU.max)
            tt("dve", e0, l0, mx, ALU.subtract)
            tt("dve", e1, l1, mx, ALU.subtract)
            act(e0, e0, AF.Exp)
            act(e1, e1, AF.Exp)
            tt("dve", sd, e0, e1, ALU.add)
            recip(sd, sd)
            tt("dve", sm0, e0, sd, ALU.mult)
            tt("dve", sm1, e1, sd, ALU.mult)
            tt("dve", ecst[:, LB:LB + 4], sm0, sm0, ALU.subtract)
            tt("dve", mx, sm0, sm1, ALU.add)
            tt("dve", ecst[:, LB + 4:LB + 8], mx, sm0, ALU.subtract)
            ts("dve", ecst[:, OML:OML + 8], ecst[:, LB:LB + 8], -1.0, 1.0, ALU.mult, ALU.add)
            ts("dve", ecst[:, NOML:NOML + 8], ecst[:, LB:LB + 8], -1.0, None, ALU.add)
            sigm(tmpc[:, 32:40], small[:, SM_LAM:SM_LAM + 8])
            act(tmpc[:, 32:40], tmpc[:, 32:40], AF.Ln)
            ts("dve", ecst[:, NBA:NBA + 8], small[:, SM_BA:SM_BA + 8], -1.0, None, ALU.mult)
            ts("dve", ecst[:, NBX:NBX + 8], small[:, SM_BX:SM_BX + 8], -1.0, None, ALU.mult)
            ts("dve", ecst[:, CL:CL + 8], tmpc[:, 32:40], 8.0, None, ALU.mult)
            ts("dve", ecst[:, CL2:CL2 + 8], tmpc[:, 32:40], 16.0, None, ALU.mult)
            wst = alloc((16, 128), F32)
            memset("pool", wst, 0.0)
            for e in range(2):
                for gi_, wsrc in enumerate((rgwa_d, rgwx_d)):
                    for m in range(4):
                        k_ = e * 8 + gi_ * 4 + m
                        dma("sp", wst.ap[0:64, k_, 0:64], wsrc[e, 2 * m], [], [wst])
                        dma("sp", wst.ap[64:128, k_, 64:128], wsrc[e, 2 * m + 1], [], [wst])
            cp("pool", WBD, wst)
            for e in range(2):
                dma("sp", gainA[e].ap, hon_d[e:e + 1, :].partition_broadcast(128), [], [gainA[e]])
                memset("pool", Sp[e], 0.0)
            memset("pool", ones1, 1.0)
            memset("pool", Lcar, 0.0)
            memset("pool", hcar, 0.0)
            memset("pool", halo, 0.0)
            bmf = alloc((128,), F32)
            memset("pool", bmf, 1.0)
            P.op("pool", lambda e: e.affine_select(out=bmf.ap, in_=bmf.ap, compare_op=ALU.is_ge, fill=0.0,
                                                   base=0, pattern=[[1, 128]], channel_multiplier=-1),
                 reads=[bmf], writes=[bmf])
            memset("pool", bmf[0:64, 64:128], 0.0)
            cp("dve", bmask, bmf)
        cur[0] = SCR

        has_odd = do_mixer and any(l % 2 == 1 for l in layers)
        if has_odd:
            posi = alloc((NBLK,), I32)
            posf = alloc((NBLK,), F32)
            invf = alloc((32,), F32)
            ang = alloc((NBLK, 32), F32)
            ang2 = alloc((NBLK, 32), F32)
            ki = alloc((NBLK, 32), I32)
            kf = alloc((NBLK, 32), F32)
            dma("sp", posi.ap, pos_d, [], [posi])
            cp("dve", posf, posi)
            for i in range(32):
                memset("pool", invf[:, i:i + 1], float(10000.0 ** (-i / 32.0)))
            tt("dve", ang, View(posf.ap.unsqueeze(2).to_broadcast([128, NBLK, 32]), posf.toks), bcm(invf, [128, NBLK, 32]), ALU.mult)

            def sin_of(dst, src, shift):
                ts("dve", ang2, src, shift, None, ALU.add)
                ts("dve", ki, ang2, 1.0 / TWO_PI, None, ALU.mult)
                cp("dve", kf, ki)
                stt("dve", ang2, kf, -TWO_PI, ang2, ALU.mult, ALU.add)
                ts("dve", kf, ang2, math.pi, -TWO_PI, ALU.is_gt, ALU.mult)
                tt("dve", ang2, ang2, kf, ALU.add)
                ts("dve", kf, ang2, -math.pi, TWO_PI, ALU.is_lt, ALU.mult)
                tt("dve", ang2, ang2, kf, ALU.add)
                act(dst, ang2, AF.Sin)

            sin_of(stab, ang, 0.0)
            sin_of(ctab, ang, math.pi / 2.0)
            onesf = alloc((128,), F32)
            mtmp = alloc((2, 128), F32)
            memset("pool", onesf, 1.0)
            P.op("pool", lambda e: e.affine_select(out=mtmp.ap[:, 0, :], in_=onesf.ap, compare_op=ALU.is_gt, fill=0.0,
                                                   base=0, pattern=[[-1, 128]], channel_multiplier=1),
                 reads=[onesf], writes=[mtmp])
            P.op("pool", lambda e: e.affine_select(out=mtmp.ap[:, 1, :], in_=onesf.ap, compare_op=ALU.is_ge, fill=0.0,
                                                   base=0, pattern=[[1, 128]], channel_multiplier=-1),
                 reads=[onesf], writes=[mtmp])
            cp("dve", swam, mtmp)
            for o in range(2):
                dma("sp", gqk[o][0].ap, qn_d[o:o + 1, :].partition_broadcast(128), [], [gqk[o][0]])
                dma("sp", gqk[o][1].ap, kn_d[o:o + 1, :].partition_broadcast(128), [], [gqk[o][1]])
                dma("sp", esink[o].ap, sinks_d[o:o + 1, :].partition_broadcast(128), [], [esink[o]])
                act(esink[o], esink[o], AF.Exp)
                for k_ in range(2):
                    memset("pool", Vb[o][k_], 1.0)
                    memset("pool", KTr[o][k_][0], 0.0)
                    memset("pool", KTr[o][k_][1], 0.0)

        cur[0] = SCR
        aT = [alloc((8, T), BF16) for _ in range(1)]
        rl = [alloc((T,), F32) for _ in range(2)]
        mlp_n = [0]
        mlp_r = [0]

        def mlp(l, first_tile, prenormed, next_gain):
            if not prenormed:
                norm_to_hT(nmlp_d[l:l + 1, :])
            pend = None
            for r in range(4):
                Wi = wload(wmi_b, "wmi", l, r * 1024, 1024)
                Wo = wload_rows(wmo_b, "wmo", l, r * 8)
                a_ = aT[0]
                if r == 3 and next_gain is not None:
                    load_gain(next_gain)
                for j in range(8):
                    pu = bank(mlp_n[0] % 2)
                    r_ = rl[mlp_n[0] % 2]
                    mlp_n[0] += 1
                    for c in range(8):
                        mm(pu, Wi[:, c, j * 128:(j + 1) * 128], hT[:, c, :], c == 0, c == 7)
                    act(r_, pu, AF.Relu)
                    tt("pool", a_[:, j, :], r_, r_, ALU.mult)
                    if first_tile:
                        pump(n=1)
                for s in range(NSUB):
                    py = [bank(2 + 2 * (s % 2)), bank(3 + 2 * (s % 2))]
                    for h in range(2):
                        for j in range(8):
                            mm(py[h], a_[:, j, s * 128:(s + 1) * 128], Wo[:, j, h * 512:(h + 1) * 512], j == 0, j == 7)
                    for h in range(2):
                        tt("dve", xs[s][:, h * 512:(h + 1) * 512], xs[s][:, h * 512:(h + 1) * 512], py[h], ALU.add)
                    if r == 3 and next_gain is not None:
                        if pend is not None:
                            for _ in pend:
                                pass
                        pend = norm_gen(s)
                        for _ in range(3):
                            next(pend)
            if pend is not None:
                for _ in pend:
                    pass

        def interleave(*gens):
            gens = [g for g in gens if g is not None]
            while gens:
                for g in list(gens):
                    try:
                        next(g)
                    except StopIteration:
                        gens.remove(g)

        cur[0] = SCR
        sqb = alloc((1280,), F32)
        zn = alloc((1280,), F32)
        rt = [alloc((20, 32), F32) for _ in range(4)]
        qkr = alloc((1280,), BF16)
        kdup = alloc((512,), BF16)
        QT = [alloc((8, 128), BF16) for _ in range(2)]
        PT = [alloc((4, 2, 128), BF16) for _ in range(2)]
        atok = alloc((1024,), BF16)
        ostA = alloc((64,), F32)
        ostB = alloc((64,), F32)
        kst = alloc((4, 128), BF16)
        vst = alloc((4, 64), BF16)

        def odd_A(o, it, s, WA, WB):
            gb = it * NSUB + s
            sl = slice(s * 128, (s + 1) * 128)
            own = gb % 2
            for n in range(2):
                for c in range(8):
                    mm(bank(n), hT[:, c, sl], WA[:, c, n * 512:(n + 1) * 512], c == 0, c == 7)
                yield
            for c in range(8):
                mm(bank(2), hT[:, c, sl], WB[:, c, :], c == 0, c == 7)
            yield
            cp("act", vst, PS.view(2 * 2048 + 1024, (4, 64), F32))
            for (b_, h0, h1) in ((0, 0, 8), (1, 8, 16), (2, 16, 20)):
                act(sqb[:, h0 * 64:h1 * 64], PS.view(b_ * 2048, ((h1 - h0) * 64,), F32), AF.Square)
                yield
            ssq = ostA[:, 0:20]
            rq_ = ostA[:, 20:40]
            rq = ostA[:, 40:60]
            P.op("dve", lambda e: e.tensor_reduce(out=ssq.ap, in_=v3(sqb, 20).ap, axis=AX.X, op=ALU.add), reads=[sqb], writes=[ssq])
            yield
            rstd_of(rq, rq_, ssq, 1.0 / 64.0)
            yield
            zn3 = v3(zn, 20)
            for (b_, h0, h1) in ((0, 0, 8), (1, 8, 16), (2, 16, 20)):
                tt("dve", zn3[:, h0:h1, :], PS.view(b_ * 2048, (h1 - h0, 64), F32),
                   View(rq.ap[:, h0:h1].unsqueeze(2).to_broadcast([128, h1 - h0, 64]), rq.toks), ALU.mult)
                yield
            tt("pool", zn3[:, 0:16, :], zn3[:, 0:16, :], bcm(gqk[o][0], [128, 16, 64]), ALU.mult)
            tt("dve", zn3[:, 16:20, :], zn3[:, 16:20, :], bcm(gqk[o][1], [128, 4, 64]), ALU.mult)
            yield
            cosb = bcm(ctab[:, gb, :], [128, 20, 32])
            sinb = bcm(stab[:, gb, :], [128, 20, 32])
            x1 = zn3[:, :, 0:32]
            x2 = zn3[:, :, 32:64]
            q3 = v3(qkr, 20)
            tt("pool", rt[0], x1, cosb, ALU.mult)
            tt("dve", rt[1], x2, sinb, ALU.mult)
            yield
            tt("pool", rt[2], x2, cosb, ALU.mult)
            tt("dve", rt[3], x1, sinb, ALU.mult)
            yield
            tt("pool", q3[:, :, 0:32], rt[0], rt[1], ALU.subtract)
            tt("dve", q3[:, :, 32:64], rt[2], rt[3], ALU.add)
            yield
            pT = bank(7, (8, 128), BF16)
            for c in range(8):
                tr(pT[:, c, :], qkr[:, c * 128:(c + 1) * 128])
            cp("act", QT[s % 2], pT)
            kd4 = View(kdup.ap.rearrange("p (g r d) -> p g r d", g=4, r=2), kdup.toks)
            ksrc = View(q3.ap[:, 16:20, :].unsqueeze(2).to_broadcast([128, 4, 2, 64]), qkr.toks)
            cp("pool", kd4, ksrc)
            yield
            pTk = bank(6, (4, 128), BF16)
            for g in range(4):
                tr(pTk[:, g, :], kdup[:, g * 128:(g + 1) * 128])
            cp("dve", kst, pTk)
            yield

        def odd_commit(o, it, s):
            own = (it * NSUB + s) % 2
            cp("dve", KTr[o][own][0][0:64], kst[0:64])
            cp("pool", KTr[o][own][1][64:128], kst[64:128])
            cp("dve", Vb[o][own][:, :, 0:64], vst)

        def odd_B(o, it, s, WO):
            gb = it * NSUB + s
            sl = slice(s * 128, (s + 1) * 128)
            own, prv = gb % 2, (gb + 1) % 2
            qt_ = QT[s % 2]
            a3 = v3(atok, 16)
            kbs = [1] if gb == 0 else [0, 1]
            for g in range(4):
                pS = PS.view(3 * 2048, (4, 2, 128), F32)
                for hh in range(4):
                    h = 4 * g + hh
                    c, half = h // 2, h % 2
                    for kb in kbs:
                        slot = prv if kb == 0 else own
                        mm(pS[:, hh, kb, :], KTr[o][slot][half][:, g, :], qt_[:, c, :], True, True)
                yield
                pt_ = PT[g % 2]
                if gb == 0:
                    act(pt_[:, :, 1, :], pS[:, :, 1, :], AF.Exp, scale=0.125)
                    yield
                    tt("pool", pt_[:, :, 1, :], pt_[:, :, 1, :], bcm(swam[:, 1, :], [128, 4, 128]), ALU.mult)
                else:
                    act(pt_, pS, AF.Exp, scale=0.125)
                    yield
                    tt("pool", pt_, pt_, View(swam.ap.unsqueeze(1).to_broadcast([128, 4, 2, 128]), swam.toks), ALU.mult)
                yield
                pO = bank(5 + (g % 2), (4, 65), F32)
                for hh in range(4):
                    for kb in kbs:
                        slot = prv if kb == 0 else own
                        mm(pO[:, hh, :], pt_[:, hh, kb, :], Vb[o][slot][:, g, :], kb == kbs[0], kb == kbs[-1])
                dn = ostB[:, 4 * g:4 * g + 4]
                tt("dve", dn, pO[:, :, 64], esink[o][:, 4 * g:4 * g + 4], ALU.add)
                recip(dn, dn)
                tt("dve", a3[:, 4 * g:4 * g + 4, :], pO[:, :, 0:64],
                   View(dn.ap.unsqueeze(2).to_broadcast([128, 4, 64]), dn.toks), ALU.mult)
                yield
            pT = bank(5, (8, 128), BF16)
            for c in range(8):
                tr(pT[:, c, :], atok[:, c * 128:(c + 1) * 128])
            yield
            cp("act", yT[:, :, sl], pT)
            yield
            for hf in range(2):
                for c in range(8):
                    mm(bank(3 + hf), yT[:, c, sl], WO[:, c, hf * 512:(hf + 1) * 512], c == 0, c == 7)
                yield
            for hf in range(2):
                tt("dve", xs[s][:, hf * 512:(hf + 1) * 512], xs[s][:, hf * 512:(hf + 1) * 512], bank(3 + hf), ALU.add)
            yield

        def chain(*gens):
            for g in gens:
                for _ in g:
                    yield

        def odd_layer(l, it, prenormed):
            o = l // 2
            if not prenormed:
                norm_to_hT(nmix_d[l:l + 1, :])
            WA = wload(wio_b, "wio", o, 0, 1024)
            WB = wload(wio_b, "wio", o, 1024, 512)
            WO = wload(woo_b, "woo", o, 0, 1024)
            if it == 0:
                pump(n=16)
            interleave(odd_A(o, it, 0, WA, WB))
            odd_commit(o, it, 0)
            for s in range(NSUB):
                if it == 0:
                    pump(n=16)
                if s + 1 < NSUB:
                    interleave(odd_B(o, it, s, WO), odd_A(o, it, s + 1, WA, WB))
                    odd_commit(o, it, s + 1)
                else:
                    load_gain(nmlp_d[l:l + 1, :])
                    interleave(odd_B(o, it, s, WO), chain(norm_gen(0), norm_gen(1), norm_gen(2)))
                    for _ in norm_gen(3):
                        pass

        cur[0] = SCR
        hA, hB, hK, hC, hD = (alloc((T,), F32) for _ in range(5))
        rA, rB, rC, rD, rE = (alloc((T,), F32) for _ in range(5))
        Qz = [alloc((4, T), BF16) for _ in range(2)]
        Kz = [alloc((4, T), BF16) for _ in range(2)]
        Ktok = [alloc((NSUB, 512), BF16) for _ in range(2)]
        Vtok = alloc((NSUB, 512), BF16)
        Gs = alloc((NSUB, 512), BF16)
        ATb = [alloc((4, 128), BF16) for _ in range(2)]
        T2s = alloc((4, 128), F32)
        Spb = [[alloc((4, 128), BF16) for _ in range(2)] for _ in range(2)]
        sqe = alloc((4, 128), F32)
        tno = alloc((4, 128), F32)
        yat = alloc((512,), BF16)
        etab = alloc((4, 4, 8), F32)
        est = alloc((16,), F32)
        XB = alloc((516,), F32)
        xcb = alloc((T,), BF16)
        LB, OML, NOML, CL, CL2, NBA, NBX = 0, 8, 16, 24, 32, 40, 48

        def ev_gates(e, h, W1):
            Lprev, Emu, Elast, Gt = (etab[:, i] for i in range(4))
            pq = bank(0)
            pf = bank(1)
            for c in range(8):
                mm(pf, W1[:, c, 512 + h * 128:512 + (h + 1) * 128], hT[:, c, :], c == 0, c == 7)
            yield
            for c in range(8):
                mm(pq, W1[:, c, h * 128:(h + 1) * 128], hT[:, c, :], c == 0, c == 7)
            yield
            col = e * 4 + h
            act(hA, pf, AF.Exp, scale=-1.0)
            yield
            act(hA, hA, AF.Ln, bias=1.0)
            yield
            act(hA, hA, AF.Exp, scale=-1.0)
            yield
            ts("dve", hB, hA, ecst[:, OML + col:OML + col + 1], ecst[:, LB + col:LB + col + 1], ALU.mult, ALU.add)
            ts("pool", hK, hA, ecst[:, NOML + col:NOML + col + 1], ecst[:, OML + col:OML + col + 1], ALU.mult, ALU.add)
            yield
            ts("dve", hB, hB, 1e-30, None, ALU.max)
            yield
            act(hB, hB, AF.Ln)
            yield
            P.op("dve", lambda en: en.tensor_tensor_scan(out=hC.ap, data0=onesT.ap, data1=hB.ap,
                                                         initial=Lcar.ap[:, col:col + 1],
                                                         op0=ALU.mult, op1=ALU.add),
                 reads=[onesT, hB, Lcar], writes=[hC])
            yield
            C3 = v3(hC, 8)
            D3 = v3(hD, 8)
            cp("pool", Lprev[:, h, 0:1], Lcar[:, col:col + 1])
            cp("pool", Lprev[:, h, 1:8], C3[:, 0:7, 63])
            cp("pool", Lcar[:, col:col + 1], hC[:, T - 1:T])
            tt("dve", D3, C3, View(C3.ap[:, :, 31:32].to_broadcast([128, 8, 64]), hC.toks), ALU.subtract)
            yield
            act(hA, hD, AF.Exp)
            yield
            act(hB, hD, AF.Exp, scale=-1.0)
            tt("dve", Emu[:, h, :], C3[:, :, 31], Lprev[:, h, :], ALU.subtract)
            yield
            for j in range(2):
                def par(v, j=j):
                    return View(v.ap.rearrange("p (s j t) -> p s j t", s=4, j=2)[:, :, j, :], v.toks)
                tt("dve", par(Qz[j][:, h, :]), par(pq), par(hA), ALU.mult)
                tt("pool", par(Kz[j][:, h, :]), par(hK), par(hB), ALU.mult)
                yield
            act(Emu[:, h, :], Emu[:, h, :], AF.Exp)
            act(Elast[:, h, :], D3[:, :, 63], AF.Exp)
            yield
            tt("dve", Gt[:, h, 0:7], Elast[:, h, 0:7], Emu[:, h, 1:8], ALU.mult)
            cp("dve", Gt[:, h, 7:8], Elast[:, h, 7:8])
            yield
            for j in range(2):
                pT = bank(2 + j, (4, 128), BF16)
                for s in range(NSUB):
                    tr(pT[:, s, :], Kz[j][:, h, s * 128:(s + 1) * 128])
                yield
                cp("act" if j == 0 else "dve", Ktok[j][:, :, h * 128:(h + 1) * 128], pT)
                yield

        def ev_rglru(e, m, W3):
            R_, I_, T1, Gg, T2r = rA, rB, rC, rD, rE
            col = e * 4 + m
            pg, px = bank(4), bank(5)
            for c in range(8):
                mm(px, W3[:, c, 512 + m * 128:512 + (m + 1) * 128], hT[:, c, :], c == 0, c == 7)
            yield
            for c in range(8):
                mm(pg, W3[:, c, m * 128:(m + 1) * 128], hT[:, c, :], c == 0, c == 7)
            yield
            cp("pool", XB[:, 0:3], halo[:, col, :])
            cp("act", XB[:, 3:515], px)
            yield
            xc = T2r

            def cwc(j):
                k_ = SM_CONVW + e * 16 + j * 4 + m
                return small[:, k_:k_ + 1]
            ts("dve", xc, XB[:, 3:515], cwc(3), small[:, SM_CONVB + col:SM_CONVB + col + 1], ALU.mult, ALU.add)
            yield
            for k_ in (1, 2, 3):
                stt("dve", xc, XB[:, 3 - k_:515 - k_], cwc(3 - k_), xc, ALU.mult, ALU.add)
                yield
            cp("pool", halo[:, col, :], XB[:, 512:515])
            cp("act", xcb, xc)
            yield
            pr, pi = bank(6), bank(7)
            mm(pr, WBD[:, e * 8 + m, :], xcb, True, True)
            mm(pi, WBD[:, e * 8 + 4 + m, :], xcb, True, True)
            yield
            act(R_, pr, AF.Exp, bias=ecst[:, NBA + col:NBA + col + 1], scale=-1.0)
            yield
            act(I_, pi, AF.Exp, bias=ecst[:, NBX + col:NBX + col + 1], scale=-1.0)
            yield
            act(R_, R_, AF.Ln, bias=1.0)
            yield
            act(I_, I_, AF.Ln, bias=1.0)
            yield
            act(R_, R_, AF.Exp, scale=-1.0)
            yield
            act(I_, I_, AF.Exp, scale=-1.0)
            yield
            act(T1, R_, AF.Exp, scale=ecst[:, CL2 + col:CL2 + col + 1])
            tt("pool", I_, I_, xc, ALU.mult)
            yield
            ts("dve", T1, T1, -1.0, 1.0, ALU.mult, ALU.add)
            yield
            ts("dve", T1, T1, 1e-30, None, ALU.max)
            act(R_, R_, AF.Exp, scale=ecst[:, CL + col:CL + col + 1])
            yield
            act(T1, T1, AF.Ln)
            yield
            act(T1, T1, AF.Exp, scale=0.5)
            yield
            tt("dve", I_, I_, T1, ALU.mult)
            yield
            P.op("dve", lambda en: en.tensor_tensor_scan(out=T1.ap, data0=R_.ap, data1=I_.ap,
                                                         initial=hcar.ap[:, col:col + 1],
                                                         op0=ALU.mult, op1=ALU.add),
                 reads=[R_, I_, hcar], writes=[T1])
            yield
            cp("pool", hcar[:, col:col + 1], T1[:, T - 1:T])
            cp("act", Gg, pg)
            yield
            act(T2r, pg, AF.Square)
            yield
            ts("dve", T2r, T2r, 0.044715, 1.0, ALU.mult, ALU.add)
            yield
            tt("dve", T2r, T2r, Gg, ALU.mult)
            yield
            act(T2r, T2r, AF.Exp, scale=-2.0 * math.sqrt(2.0 / math.pi))
            yield
            act(T2r, T2r, AF.Ln, bias=1.0)
            yield
            act(T2r, T2r, AF.Exp, scale=-1.0)
            tt("pool", Gg, Gg, T1, ALU.mult)
            yield
            tt("dve", yT[:, 4 + m, :], Gg, T2r, ALU.mult)
            yield

        def ev_X(e, s):
            Lprev, Emu, Elast, Gt = (etab[:, i] for i in range(4))
            sl = slice(s * 128, (s + 1) * 128)
            Spe = Sp[e]
            pS = bank(6, (4, 128), F32)
            for h in range(4):
                for j in range(2):
                    mm(pS[:, h, :], Kz[j][:, h, sl], Qz[j][:, h, sl], j == 0, j == 1)
            yield
            pkv = [bank(0, (4, 128), F32), bank(1, (4, 128), F32)]
            for j in range(2):
                for h in range(4):
                    mm(pkv[j][:, h, :], Ktok[j][:, s, h * 128:(h + 1) * 128], Vtok[:, s, h * 128:(h + 1) * 128], True, True)
                yield
            tt("dve", ATb[s % 2], pS, bcm(bmask, [128, 4, 128]), ALU.mult)
            yield
            if s == 0:
                tt("dve", Spe, Spe, View(Emu.ap[:, :, 0:1].to_broadcast([128, 4, 128]), etab.toks), ALU.mult)
                yield
            cp("act", Spb[s % 2][0], Spe)
            yield
            for j in range(2):
                cch = 2 * s + j
                tt("dve", T2s, pkv[j], Spe, ALU.add)
                yield
                tt("dve", Spe, T2s, View(Gt.ap[:, :, cch:cch + 1].to_broadcast([128, 4, 128]), etab.toks), ALU.mult)
                yield
                if j == 0:
                    cp("act", Spb[s % 2][1], Spe)
                    yield

        def ev_Y(e, s, WO):
            sl = slice(s * 128, (s + 1) * 128)
            at_ = ATb[s % 2]
            po = bank(2 + (s % 2), (4, 128), F32)
            for h in range(4):
                mm(po[:, h, :], at_[:, h, :], Vtok[:, s, h * 128:(h + 1) * 128], True, False)
                mm(po[:, h, :], Qz[0][:, h, sl], Spb[s % 2][0][:, h, :], False, False)
                mm(po[:, h, :], Qz[1][:, h, sl], Spb[s % 2][1][:, h, :], False, True)
            yield
            act(sqe, po, AF.Square)
            yield
            P.op("dve", lambda en: en.tensor_reduce(out=est.ap[:, 0:4], in_=sqe.ap, axis=AX.X, op=ALU.add), reads=[sqe], writes=[est])
            yield
            rstd_of(est[:, 8:12], est[:, 4:8], est[:, 0:4], 1.0 / 128.0)
            yield
            tt("dve", tno, po, View(est.ap[:, 8:12].unsqueeze(2).to_broadcast([128, 4, 128]), est.toks), ALU.mult)
            yield
            tno2 = View(tno.ap.rearrange("p h d -> p (h d)"), tno.toks)
            tt("pool", tno2, tno2, gainA[e], ALU.mult)
            yield
            tt("dve", yat, tno2, Gs[:, s, :], ALU.mult)
            yield
            pT = bank(7, (4, 128), BF16)
            for h in range(4):
                tr(pT[:, h, :], yat[:, h * 128:(h + 1) * 128])
            cp("act", yT[:, 0:4, sl], pT)
            yield
            for hf in range(2):
                for c in range(8):
                    mm(bank(4 + hf), yT[:, c, sl], WO[:, c, hf * 512:(hf + 1) * 512], c == 0, c == 7)
                yield
            for hf in range(2):
                tt("dve", xs[s][:, hf * 512:(hf + 1) * 512], xs[s][:, hf * 512:(hf + 1) * 512], bank(4 + hf), ALU.add)
            yield

        def even_layer(l, it, prenormed):
            e = l // 2
            if not prenormed:
                norm_to_hT(nmix_d[l:l + 1, :])
            W1 = wload(wie_b, "wie", e, 0, 1024)
            W2 = wload(wie_b, "wie", e, 1024, 1024)
            W3 = wload(wie_b, "wie", e, 2048, 1024)
            for s in range(NSUB):
                sl = slice(s * 128, (s + 1) * 128)
                for n in range(2):
                    for c in range(8):
                        mm(bank(4 + n), hT[:, c, sl], W2[:, c, n * 512:(n + 1) * 512], c == 0, c == 7)
                cp("act", Vtok[:, s, :], bank(4))
                sgt = View(tno.ap.rearrange("p h d -> p (h d)"), tno.toks)
                sigm(sgt, bank(5))
                tt("dve", Gs[:, s, :], sgt, bank(5), ALU.mult)
            for j in range(2):
                for z_ in (Qz[j], Kz[j]):
                    zv = View(z_.ap.rearrange("p h (s j t) -> p h s j t", s=4, j=2)[:, :, :, 1 - j, :], z_.toks)
                    memset("pool", zv, 0.0)
            for h in range(4):
                if it == 0:
                    pump(n=8)
                interleave(ev_gates(e, h, W1), ev_rglru(e, h, W3))
            WO = wload(woe_b, "woe", e, 0, 1024)
            interleave(ev_X(e, 0))
            load_gain(nmlp_d[l:l + 1, :])
            for s in range(NSUB):
                if it == 0:
                    pump(n=8)
                interleave(ev_Y(e, s, WO), ev_X(e, s + 1) if s + 1 < NSUB else None,
                           norm_gen(s - 1) if s >= 1 else None)
            for _ in norm_gen(NSUB - 1):
                pass

        print("SBUF bytes used", cur[0], "of", SB_BYTES)
        if n_layers > 0:
            pump(upto_group=0)
        for it in range(n_tiles):
            for s in range(NSUB):
                r0 = it * T + s * 128
                dma("sp", xs[s].ap, x_d[r0:r0 + 128, :], [], [xs[s]])
            for li, l in enumerate(layers):
                gi = 2 * li
                if it == 0:
                    pump(upto_group=gi)
                pren = do_mixer and do_mlp and li > 0
                if do_mixer:
                    if l % 2 == 1:
                        odd_layer(l, it, pren)
                    else:
                        even_layer(l, it, pren)
                if do_mlp:
                    if it == 0:
                        pump(upto_group=gi + 1)
                    nxt = None
                    if do_mixer and li + 1 < len(layers):
                        nxt = nmix_d[layers[li + 1]:layers[li + 1] + 1, :]
                    mlp(l, it == 0, do_mixer, nxt)
            for s in range(NSUB):
                r0 = it * T + s * 128
                dma("sp", out_d[r0:r0 + 128, :], xs[s].ap, [xs[s]], [("D", "out", it, s)])
        P.emit()
    return nc


def _small_pack(inp):
    sm = np.zeros((128, NSM), np.float32)

    def fm(v):
        return np.ascontiguousarray(v.reshape(4, 128).T)

    for e in range(2):
        sm[:, SM_LBL + e * 4:SM_LBL + e * 4 + 4] = fm(inp["hgrn_lb_logits"][e])
        for j in range(4):
            sm[:, SM_CONVW + e * 16 + j * 4:SM_CONVW + e * 16 + j * 4 + 4] = fm(inp["conv_w"][e, j])
        sm[:, SM_CONVB + e * 4:SM_CONVB + e * 4 + 4] = fm(inp["conv_b"][e])
        sm[:, SM_BA + e * 4:SM_BA + e * 4 + 4] = fm(inp["rg_ba"][e])
        sm[:, SM_BX + e * 4:SM_BX + e * 4 + 4] = fm(inp["rg_bx"][e])
        sm[:, SM_LAM + e * 4:SM_LAM + e * 4 + 4] = fm(inp["rg_lambda"][e])
    return sm


_NC_CACHE = {}


def make_in_maps(inputs, n_cores=8):
    inp = {k: np.asarray(v) for k, v in inputs.items()}
    shared = {
        "norm_mix": inp["norm_mix"], "norm_mlp": inp["norm_mlp"],
        "w_mlp_in": inp["w_mlp_in"], "w_mlp_out": inp["w_mlp_out"],
        "w_in_even": inp["w_in_even"], "w_out_even": inp["w_out_even"],
        "w_in_odd": inp["w_in_odd"], "w_out_odd": inp["w_out_odd"],
        "small": _small_pack(inp), "hgrn_out_norm": inp["hgrn_out_norm"],
        "rg_wa": inp["rg_wa"], "rg_wx": inp["rg_wx"],
        "q_norm": inp["q_norm"], "k_norm": inp["k_norm"], "sinks": inp["sinks"],
    }
    shared = {k: np.ascontiguousarray(v, dtype=np.float32) for k, v in shared.items()}
    maps = []
    for b in range(n_cores):
        m = dict(shared)
        m["x"] = np.ascontiguousarray(inp["x"][b], dtype=np.float32)
        m["pos"] = np.ascontiguousarray(inp["positions"][b].reshape(SEQ // 128, 128).T.astype(np.int32))
        maps.append(m)
    return maps


def kernel(**inputs):
    key = "full"
    if key not in _NC_CACHE:
        _NC_CACHE[key] = build_nc()
    nc = _NC_CACHE[key]
    in_maps = make_in_maps(inputs, 8)
    res = run_bass_kernel_spmd(nc, in_maps, core_ids=list(range(8)))
    out = np.stack([np.asarray(r["out"]).reshape(SEQ, D) for r in res.results], axis=0)
    return out.astype(np.float32)
```

```python
import math
from contextlib import ExitStack

import numpy as np
import concourse.bass as bass
import concourse.mybir as mybir
from concourse.bass_utils import run_bass_kernel_spmd

F32 = mybir.dt.float32
BF16 = mybir.dt.bfloat16
I32 = mybir.dt.int32
AF = mybir.ActivationFunctionType
ALU = mybir.AluOpType
AX = mybir.AxisListType

GRAN = 256
ENGS = ("pe", "act", "dve", "pool", "sp")


class View:
    __slots__ = ("ap", "toks")

    def __init__(self, ap, toks):
        self.ap = ap
        self.toks = toks

    def __getitem__(self, key):
        return View(self.ap[key], self.toks)


class Arena:
    def __init__(self, tensor, space, nbytes):
        self.t = tensor
        self.space = space
        self.nbytes = nbytes

    def view(self, off, shape, dtype, p0=0, p1=128):
        esz = mybir.dt.size(dtype)
        n = 1
        for s in shape:
            n *= s
        nb = n * esz
        assert off % 4 == 0 and nb % 4 == 0, (off, nb)
        assert off + nb <= self.nbytes, (off, nb, self.nbytes)
        ap = self.t[p0:p1, off // 4:(off + nb) // 4]
        if dtype != F32:
            ap = ap.bitcast(dtype)
        if len(shape) == 2:
            ap = ap.rearrange("p (a b) -> p a b", a=shape[0])
        elif len(shape) == 3:
            ap = ap.rearrange("p (a b c) -> p a b c", a=shape[0], b=shape[1])
        toks = tuple((self.space, g) for g in range(off // GRAN, (off + nb - 1) // GRAN + 1))
        return View(ap, toks)


class Op:
    __slots__ = ("eng", "fn", "reads", "writes", "dma", "idx", "waits", "sig", "sigval", "dsem", "dval", "prewait")

    def __init__(self, eng, fn, reads, writes, dma):
        self.eng = eng
        self.fn = fn
        self.reads = reads
        self.writes = writes
        self.dma = dma
        self.waits = []
        self.sig = False
        self.sigval = 0
        self.dsem = None
        self.dval = 0
        self.prewait = None


def _toks(xs):
    out = []
    for x in xs:
        if x is None:
            continue
        if isinstance(x, View):
            out.extend(x.toks)
        else:
            out.append(x)
    return out


class Prog:
    def __init__(self, nc, n_dma_sems=12):
        self.nc = nc
        self.ops = []
        self.n_dma_sems = n_dma_sems

    def op(self, eng, fn, reads=(), writes=(), dma=False):
        o = Op(eng, fn, _toks(reads), _toks(writes), dma)
        o.idx = len(self.ops)
        self.ops.append(o)
        return o

    def analyze(self):
        last_w = {}
        readers = {}
        per_eng = {e: [] for e in ENGS}
        need = []
        ops = self.ops
        for o in ops:
            deps = set()
            raw = set()
            for t in o.reads:
                w = last_w.get(t)
                if w is not None:
                    deps.add(w)
                    raw.add(w)
            for t in o.writes:
                w = last_w.get(t)
                if w is not None:
                    deps.add(w)
                r = readers.get(t)
                if r:
                    deps.update(r)
            for t in o.writes:
                last_w[t] = o.idx
                readers[t] = []
            for t in o.reads:
                readers.setdefault(t, []).append(o.idx)
            deps.discard(o.idx)
            per_eng[o.eng].append(o)
            nd = []
            for j in deps:
                p = ops[j]
                if p.dma or p.eng != o.eng or o.dma:
                    nd.append(j)
                elif o.eng == "pool" or (o.eng != "pe" and j in raw):
                    nd.append(j)
            for j in nd:
                if not ops[j].dma:
                    ops[j].sig = True
            need.append(nd)
        self.per_eng = per_eng
        self.dma_count = {}
        for e in ENGS:
            c = 0
            n = 0
            for o in per_eng[e]:
                if o.dma:
                    o.dsem = (e, n % self.n_dma_sems)
                    o.dval = 16 * (n // self.n_dma_sems + 1)
                    if n >= self.n_dma_sems:
                        o.prewait = (o.dsem, o.dval - 16)
                    n += 1
                elif o.sig:
                    c += 1
                    o.sigval = c
            self.dma_count[e] = n
        known = {e: {} for e in ENGS}
        for o, nd in zip(ops, need):
            k = known[o.eng]
            w = {}
            if o.prewait is not None:
                s, v = o.prewait
                if k.get(s, 0) < v:
                    w[s] = v
            for j in nd:
                p = ops[j]
                if p.dma:
                    s, v = p.dsem, p.dval
                else:
                    s, v = ("c", p.eng), p.sigval
                if k.get(s, 0) < v and w.get(s, 0) < v:
                    w[s] = v
            for s, v in w.items():
                k[s] = v
            o.waits = list(w.items())

    def emit(self, final_waits_eng="sp"):
        nc = self.nc
        self.analyze()
        with ExitStack() as es:
            sems = {}
            for e in ENGS:
                sems[("c", e)] = es.enter_context(nc.semaphore("c_" + e))
                for i in range(min(self.n_dma_sems, self.dma_count[e])):
                    sems[(e, i)] = es.enter_context(nc.semaphore("d_%s_%d" % (e, i)))
            block = es.enter_context(nc.Block())

            def run(e):
                def body(eng):
                    for o in self.per_eng[e]:
                        for s, v in o.waits:
                            eng.wait_ge(sems[s], v)
                        ins = o.fn(eng)
                        if o.dma:
                            ins.then_inc(sems[o.dsem], 16)
                        elif o.sig:
                            ins.then_inc(sems[("c", e)], 1)
                    if e == final_waits_eng:
                        for e2 in ENGS:
                            n = self.dma_count[e2]
                            for i in range(min(self.n_dma_sems, n)):
                                cnt = (n - i + self.n_dma_sems - 1) // self.n_dma_sems
                                eng.wait_ge(sems[(e2, i)], 16 * cnt)
                return body

            block.tensor(run("pe"))
            block.scalar(run("act"))
            block.vector(run("dve"))
            block.gpsimd(run("pool"))
            block.sync(run("sp"))


D = 1024
SEQ = 4096
DEPTH = 4
DFF = 4096
T = 512
NT = SEQ // T
NSUB = T // 128
EPS = 1e-6
TWO_PI = 2.0 * math.pi

SM_LBL = 0
SM_CONVW = 8
SM_CONVB = 40
SM_BA = 48
SM_BX = 56
SM_LAM = 64
NSM = 72


def build_nc(n_tiles=NT, layers=(0, 1, 2, 3), do_mixer=True, do_mlp=True):
    n_layers = len(layers)
    nc = bass.Bass("TRN2", target_bir_lowering=False)
    dt_in = lambda name, shape, dt=F32: nc.dram_tensor(name, shape, dt, kind="ExternalInput").ap()
    x_d = dt_in("x", [SEQ, D])
    pos_d = dt_in("pos", [128, SEQ // 128], I32)
    nmix_d = dt_in("norm_mix", [DEPTH, D])
    nmlp_d = dt_in("norm_mlp", [DEPTH, D])
    wmi_d = dt_in("w_mlp_in", [DEPTH, D, DFF])
    wmo_d = dt_in("w_mlp_out", [DEPTH, DFF, D])
    wie_d = dt_in("w_in_even", [2, D, 3072])
    woe_d = dt_in("w_out_even", [2, D, D])
    wio_d = dt_in("w_in_odd", [2, D, 1536])
    woo_d = dt_in("w_out_odd", [2, D, D])
    small_d = dt_in("small", [128, NSM])
    hon_d = dt_in("hgrn_out_norm", [2, 512])
    rgwa_d = dt_in("rg_wa", [2, 8, 64, 64])
    rgwx_d = dt_in("rg_wx", [2, 8, 64, 64])
    qn_d = dt_in("q_norm", [2, 64])
    kn_d = dt_in("k_norm", [2, 64])
    sinks_d = dt_in("sinks", [2, 16])
    out_d = nc.dram_tensor("out", [SEQ, D], F32, kind="ExternalOutput").ap()

    def scratch(name, shape):
        return nc.dram_tensor(name, shape, BF16, kind="Internal").ap()

    wmi_b = scratch("wmi_b", [DEPTH, D, DFF])
    wmo_b = scratch("wmo_b", [DEPTH, DFF, D])
    wie_b = scratch("wie_b", [2, D, 3072])
    woe_b = scratch("woe_b", [2, D, D])
    wio_b = scratch("wio_b", [2, D, 1536])
    woo_b = scratch("woo_b", [2, D, D])

    P = Prog(nc)
    with ExitStack() as es:
        SB_BYTES = 206 * 1024
        sb_t = es.enter_context(nc.sbuf_tensor("arena", [128, SB_BYTES // 4], F32))
        ps_t = es.enter_context(nc.psum_tensor("psarena", [128, 4096], F32))
        SB = Arena(sb_t, "S", SB_BYTES)
        PS = Arena(ps_t, "P", 16384)
        cur = [0]

        def alloc(shape, dt, p0=0, p1=128):
            n = 1
            for s_ in shape:
                n *= s_
            nb = (n * mybir.dt.size(dt) + GRAN - 1) // GRAN * GRAN
            v = SB.view(cur[0], shape, dt, p0, p1)
            cur[0] += nb
            return v

        def bank(i, shape=(512,), dt=F32, off=0):
            return PS.view(i * 2048 + off, shape, dt)

        def mm(out, lhsT, rhs, start, stop):
            P.op("pe", lambda e: e.matmul(out.ap, lhsT=lhsT.ap, rhs=rhs.ap, start=start, stop=stop),
                 reads=[lhsT, rhs], writes=[out])

        def tr(out, in_):
            P.op("pe", lambda e: e.transpose(out=out.ap, in_=in_.ap, identity=ident.ap),
                 reads=[in_, ident], writes=[out])

        def act(out, in_, func, bias=None, scale=None, accum=None, rd=()):
            kw = {}
            rds = [in_] + list(rd)
            if bias is not None:
                if isinstance(bias, View):
                    kw["bias"] = bias.ap
                    rds.append(bias)
                else:
                    kw["bias"] = bias
            if scale is not None:
                if isinstance(scale, View):
                    kw["scale"] = scale.ap
                    rds.append(scale)
                else:
                    kw["scale"] = scale
            wr = [out]
            if accum is not None:
                kw["accum_out"] = accum.ap
                wr.append(accum)
            P.op("act", lambda e: e.activation(out=out.ap, in_=in_.ap, func=func, **kw), reads=rds, writes=wr)

        def tt(eng, out, in0, in1, op):
            P.op(eng, lambda e: e.tensor_tensor(out=out.ap, in0=in0.ap, in1=in1.ap, op=op), reads=[in0, in1], writes=[out])

        def ts(eng, out, in0, s1, s2, op0, op1=None):
            rds = [in0]
            a1 = s1
            a2 = s2
            if isinstance(s1, View):
                rds.append(s1)
                a1 = s1.ap
            if isinstance(s2, View):
                rds.append(s2)
                a2 = s2.ap
            if op1 is None:
                P.op(eng, lambda e: e.tensor_scalar(out=out.ap, in0=in0.ap, scalar1=a1, scalar2=None, op0=op0), reads=rds, writes=[out])
            else:
                P.op(eng, lambda e: e.tensor_scalar(out=out.ap, in0=in0.ap, scalar1=a1, scalar2=a2, op0=op0, op1=op1), reads=rds, writes=[out])

        def stt(eng, out, in0, sc, in1, op0, op1):
            rds = [in0, in1]
            a = sc
            if isinstance(sc, View):
                rds.append(sc)
                a = sc.ap
            P.op(eng, lambda e: e.scalar_tensor_tensor(out=out.ap, in0=in0.ap, scalar=a, in1=in1.ap, op0=op0, op1=op1), reads=rds, writes=[out])

        def cp(eng, out, in_):
            if eng == "act":
                P.op("act", lambda e: e.copy(out=out.ap, in_=in_.ap), reads=[in_], writes=[out])
            else:
                P.op(eng, lambda e: e.tensor_copy(out=out.ap, in_=in_.ap), reads=[in_], writes=[out])

        def memset(eng, out, val):
            P.op(eng, lambda e: e.memset(out.ap, val), writes=[out])

        def recip(out, in_):
            P.op("dve", lambda e: e.reciprocal(out=out.ap, in_=in_.ap), reads=[in_], writes=[out])

        def dma(q, out_ap, in_ap, reads, writes):
            P.op(q, lambda e: e.dma_start(out=out_ap, in_=in_ap), reads=reads, writes=writes, dma=True)

        def sigm(out, in_, bias=None, scale=None, neg_bias=None):
            nscale = -1.0 if scale is None else -scale
            act(out, in_, AF.Exp, bias=neg_bias, scale=nscale)
            act(out, out, AF.Ln, bias=1.0)
            act(out, out, AF.Exp, scale=-1.0)

        def rstd_of(out, tmp, ss, scale):
            act(tmp, ss, AF.Ln, bias=EPS, scale=scale)
            act(out, tmp, AF.Exp, scale=-0.5)

        def bc(v, shape):
            return View(v.ap.to_broadcast(shape), v.toks)

        def bcm(v, shape):
            return View(v.ap.unsqueeze(1).to_broadcast(shape), v.toks)

        xs = [alloc((D,), F32) for _ in range(NSUB)]
        hT = alloc((8, T), BF16)
        yT = alloc((8, T), BF16)
        NRING = 3
        ring = [alloc((8, 1024), BF16) for _ in range(NRING)]
        gbc = [alloc((D,), F32) for _ in range(1)]
        ident = alloc((128,), BF16)
        identf = alloc((128,), F32)
        small = alloc((NSM,), F32)
        stat = alloc((16,), F32)
        hb = [alloc((D,), BF16) for _ in range(2)]
        stats = [alloc((64,), F32) for _ in range(NSUB)]

        memset("pool", identf, 0.0)
        P.op("pool", lambda e: e.affine_select(out=identf.ap, in_=identf.ap, compare_op=ALU.not_equal, fill=1.0,
                                               base=0, pattern=[[-1, 128]], channel_multiplier=1),
             reads=[identf], writes=[identf])
        cp("dve", ident, identf)
        dma("sp", small.ap, small_d, [], [small])

        def conv_chunks(src, dst, name, idx, R, C):
            out = []
            rows = max(128, (512 * 1024) // C // 128 * 128)
            for r0 in range(0, R, rows):
                def f(r0=r0):
                    dma("pool", dst[idx, r0:r0 + rows, :], src[idx, r0:r0 + rows, :], [],
                        [("D", name, idx, r) for r in range(r0 // 128, (r0 + rows) // 128)])
                out.append(f)
            return out

        conv_groups = []
        for l in layers:
            g = []
            if do_mixer:
                if l % 2 == 0:
                    g += conv_chunks(wie_d, wie_b, "wie", l // 2, D, 3072)
                    g += conv_chunks(woe_d, woe_b, "woe", l // 2, D, D)
                else:
                    g += conv_chunks(wio_d, wio_b, "wio", l // 2, D, 1536)
                    g += conv_chunks(woo_d, woo_b, "woo", l // 2, D, D)
            conv_groups.append(g)
            g = []
            if do_mlp:
                ci = conv_chunks(wmi_d, wmi_b, "wmi", l, D, DFF)
                co = conv_chunks(wmo_d, wmo_b, "wmo", l, DFF, D)
                g += ci + co
            conv_groups.append(g)
        conv_flat = [f for g in conv_groups for f in g]
        conv_pos = [0]

        def pump(upto_group=None, n=None):
            if upto_group is not None:
                tgt = sum(len(g) for g in conv_groups[:upto_group + 1])
            else:
                tgt = min(len(conv_flat), conv_pos[0] + n)
            while conv_pos[0] < tgt:
                conv_flat[conv_pos[0]]()
                conv_pos[0] += 1

        def wtoks(name, idx, rows):
            return [("D", name, idx, r) for r in rows]

        ring_n = [0]

        def wload(src_b, name, idx, c0, cw, rows=None):
            slot = ring[ring_n[0] % NRING]
            ring_n[0] += 1
            v = slot if cw == 1024 else slot[:, :, 0:cw]
            dma("sp", v.ap, src_b[idx].rearrange("(c p) f -> p c f", p=128)[:, :, c0:c0 + cw],
                wtoks(name, idx, range(8)), [slot])
            return v

        def wload_rows(src_b, name, idx, r0):
            slot = ring[ring_n[0] % NRING]
            ring_n[0] += 1
            dma("sp", slot.ap, src_b[idx, r0 * 128:(r0 + 8) * 128, :].rearrange("(j p) d -> p j d", p=128),
                wtoks(name, idx, range(r0, r0 + 8)), [slot])
            return slot

        gb_n = [0]

        def load_gain(gain_row_ap):
            dma("sp", gbc[0].ap, gain_row_ap.partition_broadcast(128), [], [gbc[0]])

        def norm_gen(s):
            g = gbc[0]
            st_ = stats[s]
            ss, rs, rr = st_[:, 0:1], st_[:, 1:2], st_[:, 2:3]
            h_ = hb[s % 2]
            act(h_, xs[s], AF.Square, scale=1.0 / 32.0, accum=ss)
            yield
            rstd_of(rr, rs, ss, 1.0)
            yield
            stt("dve", h_, xs[s], rr, g, ALU.mult, ALU.mult)
            yield
            pT = bank(7, (8, 128), BF16)
            for c in range(8):
                tr(pT[:, c, :], h_[:, c * 128:(c + 1) * 128])
            cp("act", hT[:, :, s * 128:(s + 1) * 128], pT)
            yield

        def norm_to_hT(gain_row_ap):
            load_gain(gain_row_ap)
            for s in range(NSUB):
                for _ in norm_gen(s):
                    pass

        def v3(v, a):
            return View(v.ap.rearrange("p (a b) -> p a b", a=a), v.toks)

        NBLK = SEQ // 128
        ctab = alloc((NBLK, 32), F32)
        stab = alloc((NBLK, 32), F32)
        gqk = [[alloc((64,), F32) for _ in range(2)] for _ in range(2)]
        esink = [alloc((16,), F32) for _ in range(2)]
        swam = alloc((2, 128), BF16)
        Vb = [[alloc((4, 65), BF16) for _ in range(2)] for _ in range(2)]
        KTr = [[[alloc((4, 128), BF16) for _ in range(2)] for _ in range(2)] for _ in range(2)]
        Sp = [alloc((4, 128), F32) for _ in range(2)]
        gainA = [alloc((512,), F32) for _ in range(2)]
        WBD = alloc((16, 128), BF16)
        ones1 = alloc((1,), F32)
        onesT = View(ones1.ap.to_broadcast([128, T]), ones1.toks)
        bmask = alloc((128,), BF16)
        ecst = alloc((64,), F32)
        Lcar = alloc((8,), F32)
        hcar = alloc((8,), F32)
        halo = alloc((8, 3), F32)
        SCR = cur[0]

        has_even = do_mixer and any(l % 2 == 0 for l in layers)
        if has_even:
            LB, OML, NOML, CL, CL2, NBA, NBX = 0, 8, 16, 24, 32, 40, 48
            tmpc = alloc((64,), F32)
            l0 = small[:, SM_LBL:SM_LBL + 4]
            l1 = small[:, SM_LBL + 4:SM_LBL + 8]
            mx, e0, e1, sd, sm0, sm1 = (tmpc[:, 4 * i:4 * i + 4] for i in range(6))
            tt("dve", mx, l0, l1, ALU.max)
            tt("dve", e0, l0, mx, ALU.subtract)
            tt("dve", e1, l1, mx, ALU.subtract)
            act(e0, e0, AF.Exp)
            act(e1, e1, AF.Exp)
            tt("dve", sd, e0, e1, ALU.add)
            recip(sd, sd)
            tt("dve", sm0, e0, sd, ALU.mult)
            tt("dve", sm1, e1, sd, ALU.mult)
            tt("dve", ecst[:, LB:LB + 4], sm0, sm0, ALU.subtract)
            tt("dve", mx, sm0, sm1, ALU.add)
            tt("dve", ecst[:, LB + 4:LB + 8], mx, sm0, ALU.subtract)
            ts("dve", ecst[:, OML:OML + 8], ecst[:, LB:LB + 8], -1.0, 1.0, ALU.mult, ALU.add)
            ts("dve", ecst[:, NOML:NOML + 8], ecst[:, LB:LB + 8], -1.0, None, ALU.add)
            sigm(tmpc[:, 32:40], small[:, SM_LAM:SM_LAM + 8])
            act(tmpc[:, 32:40], tmpc[:, 32:40], AF.Ln)
            ts("dve", ecst[:, NBA:NBA + 8], small[:, SM_BA:SM_BA + 8], -1.0, None, ALU.mult)
            ts("dve", ecst[:, NBX:NBX + 8], small[:, SM_BX:SM_BX + 8], -1.0, None, ALU.mult)
            ts("dve", ecst[:, CL:CL + 8], tmpc[:, 32:40], 8.0, None, ALU.mult)
            ts("dve", ecst[:, CL2:CL2 + 8], tmpc[:, 32:40], 16.0, None, ALU.mult)
            wst = alloc((16, 128), F32)
            memset("pool", wst, 0.0)
            for e in range(2):
                for gi_, wsrc in enumerate((rgwa_d, rgwx_d)):
                    for m in range(4):
                        k_ = e * 8 + gi_ * 4 + m
                        dma("sp", wst.ap[0:64, k_, 0:64], wsrc[e, 2 * m], [], [wst])
                        dma("sp", wst.ap[64:128, k_, 64:128], wsrc[e, 2 * m + 1], [], [wst])
            cp("pool", WBD, wst)
            for e in range(2):
                dma("sp", gainA[e].ap, hon_d[e:e + 1, :].partition_broadcast(128), [], [gainA[e]])
                memset("pool", Sp[e], 0.0)
            memset("pool", ones1, 1.0)
            memset("pool", Lcar, 0.0)
            memset("pool", hcar, 0.0)
            memset("pool", halo, 0.0)
            bmf = alloc((128,), F32)
            memset("pool", bmf, 1.0)
            P.op("pool", lambda e: e.affine_select(out=bmf.ap, in_=bmf.ap, compare_op=ALU.is_ge, fill=0.0,
                                                   base=0, pattern=[[1, 128]], channel_multiplier=-1),
                 reads=[bmf], writes=[bmf])
            memset("pool", bmf[0:64, 64:128], 0.0)
            cp("dve", bmask, bmf)
        cur[0] = SCR

        has_odd = do_mixer and any(l % 2 == 1 for l in layers)
        if has_odd:
            posi = alloc((NBLK,), I32)
            posf = alloc((NBLK,), F32)
            invf = alloc((32,), F32)
            ang = alloc((NBLK, 32), F32)
            ang2 = alloc((NBLK, 32), F32)
            ki = alloc((NBLK, 32), I32)
            kf = alloc((NBLK, 32), F32)
            dma("sp", posi.ap, pos_d, [], [posi])
            cp("dve", posf, posi)
            for i in range(32):
                memset("pool", invf[:, i:i + 1], float(10000.0 ** (-i / 32.0)))
            tt("dve", ang, View(posf.ap.unsqueeze(2).to_broadcast([128, NBLK, 32]), posf.toks), bcm(invf, [128, NBLK, 32]), ALU.mult)

            def sin_of(dst, src, shift):
                ts("dve", ang2, src, shift, None, ALU.add)
                ts("dve", ki, ang2, 1.0 / TWO_PI, None, ALU.mult)
                cp("dve", kf, ki)
                stt("dve", ang2, kf, -TWO_PI, ang2, ALU.mult, ALU.add)
                ts("dve", kf, ang2, math.pi, -TWO_PI, ALU.is_gt, ALU.mult)
                tt("dve", ang2, ang2, kf, ALU.add)
                ts("dve", kf, ang2, -math.pi, TWO_PI, ALU.is_lt, ALU.mult)
                tt("dve", ang2, ang2, kf, ALU.add)
                act(dst, ang2, AF.Sin)

            sin_of(stab, ang, 0.0)
            sin_of(ctab, ang, math.pi / 2.0)
            onesf = alloc((128,), F32)
            mtmp = alloc((2, 128), F32)
            memset("pool", onesf, 1.0)
            P.op("pool", lambda e: e.affine_select(out=mtmp.ap[:, 0, :], in_=onesf.ap, compare_op=ALU.is_gt, fill=0.0,
                                                   base=0, pattern=[[-1, 128]], channel_multiplier=1),
                 reads=[onesf], writes=[mtmp])
            P.op("pool", lambda e: e.affine_select(out=mtmp.ap[:, 1, :], in_=onesf.ap, compare_op=ALU.is_ge, fill=0.0,
                                                   base=0, pattern=[[1, 128]], channel_multiplier=-1),
                 reads=[onesf], writes=[mtmp])
            ts("dve", mtmp, mtmp, 30000.0, -30000.0, ALU.mult, ALU.add)
            cp("dve", swam, mtmp)
            for o in range(2):
                dma("sp", gqk[o][0].ap, qn_d[o:o + 1, :].partition_broadcast(128), [], [gqk[o][0]])
                dma("sp", gqk[o][1].ap, kn_d[o:o + 1, :].partition_broadcast(128), [], [gqk[o][1]])
                dma("sp", esink[o].ap, sinks_d[o:o + 1, :].partition_broadcast(128), [], [esink[o]])
                act(esink[o], esink[o], AF.Exp)
                for k_ in range(2):
                    memset("pool", Vb[o][k_], 1.0)
                    memset("pool", KTr[o][k_][0], 0.0)
                    memset("pool", KTr[o][k_][1], 0.0)

        cur[0] = SCR
        aT = [alloc((8, T), BF16) for _ in range(1)]
        rl = [alloc((T,), F32) for _ in range(2)]
        mlp_n = [0]
        mlp_r = [0]

        def mlp(l, first_tile, prenormed, next_gain):
            if not prenormed:
                norm_to_hT(nmlp_d[l:l + 1, :])
            pend = None
            for r in range(4):
                Wi = wload(wmi_b, "wmi", l, r * 1024, 1024)
                Wo = wload_rows(wmo_b, "wmo", l, r * 8)
                a_ = aT[0]
                if r == 3 and next_gain is not None:
                    load_gain(next_gain)
                for j in range(8):
                    pu = bank(mlp_n[0] % 2)
                    r_ = rl[mlp_n[0] % 2]
                    mlp_n[0] += 1
                    for c in range(8):
                        mm(pu, Wi[:, c, j * 128:(j + 1) * 128], hT[:, c, :], c == 0, c == 7)
                    act(r_, pu, AF.Relu)
                    tt("pool", a_[:, j, :], r_, r_, ALU.mult)
                    if first_tile and j % 2 == 1:
                        pump(n=1)
                for s in range(NSUB):
                    py = [bank(2 + 2 * (s % 2)), bank(3 + 2 * (s % 2))]
                    for h in range(2):
                        for j in range(8):
                            mm(py[h], a_[:, j, s * 128:(s + 1) * 128], Wo[:, j, h * 512:(h + 1) * 512], j == 0, j == 7)
                    for h in range(2):
                        tt("dve", xs[s][:, h * 512:(h + 1) * 512], xs[s][:, h * 512:(h + 1) * 512], py[h], ALU.add)
                    if r == 3 and next_gain is not None:
                        if pend is not None:
                            for _ in pend:
                                pass
                        pend = norm_gen(s)
                        for _ in range(3):
                            next(pend)
            if pend is not None:
                for _ in pend:
                    pass

        def interleave(*gens):
            gens = [g for g in gens if g is not None]
            while gens:
                for g in list(gens):
                    try:
                        next(g)
                    except StopIteration:
                        gens.remove(g)

        cur[0] = SCR
        sqb = alloc((1280,), F32)
        zn = alloc((1280,), F32)
        rt = [alloc((20, 32), F32) for _ in range(4)]
        qkr = alloc((1280,), BF16)
        kdup = alloc((512,), BF16)
        QT = [alloc((8, 128), BF16) for _ in range(2)]
        PT = [alloc((4, 2, 128), BF16) for _ in range(2)]
        atok = alloc((1024,), BF16)
        ostA = alloc((64,), F32)
        ostB = alloc((64,), F32)
        kst = alloc((4, 128), BF16)
        vst = alloc((4, 64), BF16)

        def odd_A(o, it, s, WA, WB):
            gb = it * NSUB + s
            sl = slice(s * 128, (s + 1) * 128)
            own = gb % 2
            for n in range(2):
                for c in range(8):
                    mm(bank(n), hT[:, c, sl], WA[:, c, n * 512:(n + 1) * 512], c == 0, c == 7)
                yield
            for c in range(8):
                mm(bank(2), hT[:, c, sl], WB[:, c, :], c == 0, c == 7)
            yield
            cp("act", vst, PS.view(2 * 2048 + 1024, (4, 64), F32))
            for (b_, h0, h1) in ((0, 0, 8), (1, 8, 16), (2, 16, 20)):
                act(sqb[:, h0 * 64:h1 * 64], PS.view(b_ * 2048, ((h1 - h0) * 64,), F32), AF.Square)
                yield
            ssq = ostA[:, 0:20]
            rq_ = ostA[:, 20:40]
            rq = ostA[:, 40:60]
            P.op("dve", lambda e: e.tensor_reduce(out=ssq.ap, in_=v3(sqb, 20).ap, axis=AX.X, op=ALU.add), reads=[sqb], writes=[ssq])
            yield
            rstd_of(rq, rq_, ssq, 1.0 / 64.0)
            yield
            zn3 = v3(zn, 20)
            for (b_, h0, h1) in ((0, 0, 8), (1, 8, 16), (2, 16, 20)):
                tt("dve", zn3[:, h0:h1, :], PS.view(b_ * 2048, (h1 - h0, 64), F32),
                   View(rq.ap[:, h0:h1].unsqueeze(2).to_broadcast([128, h1 - h0, 64]), rq.toks), ALU.mult)
                yield
            tt("pool", zn3[:, 0:16, :], zn3[:, 0:16, :], bcm(gqk[o][0], [128, 16, 64]), ALU.mult)
            tt("dve", zn3[:, 16:20, :], zn3[:, 16:20, :], bcm(gqk[o][1], [128, 4, 64]), ALU.mult)
            yield
            cosb = bcm(ctab[:, gb, :], [128, 20, 32])
            sinb = bcm(stab[:, gb, :], [128, 20, 32])
            x1 = zn3[:, :, 0:32]
            x2 = zn3[:, :, 32:64]
            q3 = v3(qkr, 20)
            tt("pool", rt[0], x1, cosb, ALU.mult)
            tt("dve", rt[1], x2, sinb, ALU.mult)
            yield
            tt("pool", rt[2], x2, cosb, ALU.mult)
            tt("dve", rt[3], x1, sinb, ALU.mult)
            yield
            tt("pool", q3[:, :, 0:32], rt[0], rt[1], ALU.subtract)
            tt("dve", q3[:, :, 32:64], rt[2], rt[3], ALU.add)
            yield
            pT = bank(7, (8, 128), BF16)
            for c in range(8):
                tr(pT[:, c, :], qkr[:, c * 128:(c + 1) * 128])
            cp("act", QT[s % 2], pT)
            kd4 = View(kdup.ap.rearrange("p (g r d) -> p g r d", g=4, r=2), kdup.toks)
            ksrc = View(q3.ap[:, 16:20, :].unsqueeze(2).to_broadcast([128, 4, 2, 64]), qkr.toks)
            cp("pool", kd4, ksrc)
            yield
            pTk = bank(6, (4, 128), BF16)
            for g in range(4):
                tr(pTk[:, g, :], kdup[:, g * 128:(g + 1) * 128])
            cp("dve", kst, pTk)
            yield

        def odd_commit(o, it, s):
            own = (it * NSUB + s) % 2
            cp("dve", KTr[o][own][0][0:64], kst[0:64])
            cp("pool", KTr[o][own][1][64:128], kst[64:128])
            cp("dve", Vb[o][own][:, :, 0:64], vst)

        def odd_B(o, it, s, WO):
            gb = it * NSUB + s
            sl = slice(s * 128, (s + 1) * 128)
            own, prv = gb % 2, (gb + 1) % 2
            qt_ = QT[s % 2]
            a3 = v3(atok, 16)
            kbs = [1] if gb == 0 else [0, 1]
            for g in range(4):
                pS = PS.view(3 * 2048, (4, 2, 128), F32)
                for hh in range(4):
                    h = 4 * g + hh
                    c, half = h // 2, h % 2
                    for kb in kbs:
                        slot = prv if kb == 0 else own
                        mm(pS[:, hh, kb, :], KTr[o][slot][half][:, g, :], qt_[:, c, :], True, False)
                        mm(pS[:, hh, kb, :], ident, swam[:, kb, :], False, True)
                yield
                pt_ = PT[g % 2]
                if gb == 0:
                    act(pt_[:, :, 1, :], pS[:, :, 1, :], AF.Exp, scale=0.125)
                else:
                    act(pt_, pS, AF.Exp, scale=0.125)
                yield
                pO = bank(5 + (g % 2), (4, 65), F32)
                for hh in range(4):
                    for kb in kbs:
                        slot = prv if kb == 0 else own
                        mm(pO[:, hh, :], pt_[:, hh, kb, :], Vb[o][slot][:, g, :], kb == kbs[0], kb == kbs[-1])
                dn = ostB[:, 4 * g:4 * g + 4]
                tt("dve", dn, pO[:, :, 64], esink[o][:, 4 * g:4 * g + 4], ALU.add)
                recip(dn, dn)
                tt("dve", a3[:, 4 * g:4 * g + 4, :], pO[:, :, 0:64],
                   View(dn.ap.unsqueeze(2).to_broadcast([128, 4, 64]), dn.toks), ALU.mult)
                yield
            pT = bank(5, (8, 128), BF16)
            for c in range(8):
                tr(pT[:, c, :], atok[:, c * 128:(c + 1) * 128])
            yield
            cp("act", yT[:, :, sl], pT)
            yield
            for hf in range(2):
                for c in range(8):
                    mm(bank(3 + hf), yT[:, c, sl], WO[:, c, hf * 512:(hf + 1) * 512], c == 0, c == 7)
                yield
            for hf in range(2):
                tt("dve", xs[s][:, hf * 512:(hf + 1) * 512], xs[s][:, hf * 512:(hf + 1) * 512], bank(3 + hf), ALU.add)
            yield

        def chain(*gens):
            for g in gens:
                for _ in g:
                    yield

        def odd_layer(l, it, prenormed):
            o = l // 2
            if not prenormed:
                norm_to_hT(nmix_d[l:l + 1, :])
            WA = wload(wio_b, "wio", o, 0, 1024)
            WB = wload(wio_b, "wio", o, 1024, 512)
            WO = wload(woo_b, "woo", o, 0, 1024)
            if it == 0:
                pump(n=4)
            interleave(odd_A(o, it, 0, WA, WB))
            odd_commit(o, it, 0)
            for s in range(NSUB):
                if it == 0:
                    pump(n=4)
                if s + 1 < NSUB:
                    interleave(odd_B(o, it, s, WO), odd_A(o, it, s + 1, WA, WB))
                    odd_commit(o, it, s + 1)
                else:
                    load_gain(nmlp_d[l:l + 1, :])
                    interleave(odd_B(o, it, s, WO), chain(norm_gen(0), norm_gen(1), norm_gen(2)))
                    for _ in norm_gen(3):
                        pass

        cur[0] = SCR
        hA, hB, hK, hC, hD = (alloc((T,), F32) for _ in range(5))
        rA, rB, rC, rD, rE = (alloc((T,), F32) for _ in range(5))
        Qz = [alloc((4, T), BF16) for _ in range(2)]
        Kz = [alloc((4, T), BF16) for _ in range(2)]
        Ktok = [alloc((NSUB, 512), BF16) for _ in range(2)]
        Vtok = alloc((NSUB, 512), BF16)
        Gs = alloc((NSUB, 512), BF16)
        ATb = [alloc((4, 128), BF16) for _ in range(2)]
        T2s = alloc((4, 128), F32)
        Spb = [[alloc((4, 128), BF16) for _ in range(2)] for _ in range(2)]
        sqe = alloc((4, 128), F32)
        tno = alloc((4, 128), F32)
        yat = alloc((512,), BF16)
        etab = alloc((4, 4, 8), F32)
        est = alloc((16,), F32)
        XB = alloc((516,), F32)
        xcb = alloc((T,), BF16)
        LB, OML, NOML, CL, CL2, NBA, NBX = 0, 8, 16, 24, 32, 40, 48

        def ev_gates(e, h, W1):
            Lprev, Emu, Elast, Gt = (etab[:, i] for i in range(4))
            pq = bank(0)
            pf = bank(1)
            for c in range(8):
                mm(pf, W1[:, c, 512 + h * 128:512 + (h + 1) * 128], hT[:, c, :], c == 0, c == 7)
            yield
            for c in range(8):
                mm(pq, W1[:, c, h * 128:(h + 1) * 128], hT[:, c, :], c == 0, c == 7)
            yield
            col = e * 4 + h
            act(hA, pf, AF.Exp, scale=-1.0)
            yield
            act(hA, hA, AF.Ln, bias=1.0)
            yield
            act(hA, hA, AF.Exp, scale=-1.0)
            yield
            ts("dve", hB, hA, ecst[:, OML + col:OML + col + 1], ecst[:, LB + col:LB + col + 1], ALU.mult, ALU.add)
            ts("pool", hK, hA, ecst[:, NOML + col:NOML + col + 1], ecst[:, OML + col:OML + col + 1], ALU.mult, ALU.add)
            yield
            ts("dve", hB, hB, 1e-30, None, ALU.max)
            yield
            act(hB, hB, AF.Ln)
            yield
            P.op("dve", lambda en: en.tensor_tensor_scan(out=hC.ap, data0=onesT.ap, data1=hB.ap,
                                                         initial=Lcar.ap[:, col:col + 1],
                                                         op0=ALU.mult, op1=ALU.add),
                 reads=[onesT, hB, Lcar], writes=[hC])
            yield
            C3 = v3(hC, 8)
            D3 = v3(hD, 8)
            cp("pool", Lprev[:, h, 0:1], Lcar[:, col:col + 1])
            cp("pool", Lprev[:, h, 1:8], C3[:, 0:7, 63])
            cp("pool", Lcar[:, col:col + 1], hC[:, T - 1:T])
            tt("dve", D3, C3, View(C3.ap[:, :, 31:32].to_broadcast([128, 8, 64]), hC.toks), ALU.subtract)
            yield
            act(hA, hD, AF.Exp)
            yield
            act(hB, hD, AF.Exp, scale=-1.0)
            tt("dve", Emu[:, h, :], C3[:, :, 31], Lprev[:, h, :], ALU.subtract)
            yield
            for j in range(2):
                def par(v, j=j):
                    return View(v.ap.rearrange("p (s j t) -> p s j t", s=4, j=2)[:, :, j, :], v.toks)
                tt("dve", par(Qz[j][:, h, :]), par(pq), par(hA), ALU.mult)
                tt("pool", par(Kz[j][:, h, :]), par(hK), par(hB), ALU.mult)
                yield
            act(Emu[:, h, :], Emu[:, h, :], AF.Exp)
            act(Elast[:, h, :], D3[:, :, 63], AF.Exp)
            yield
            tt("dve", Gt[:, h, 0:7], Elast[:, h, 0:7], Emu[:, h, 1:8], ALU.mult)
            cp("dve", Gt[:, h, 7:8], Elast[:, h, 7:8])
            yield
            for j in range(2):
                pT = bank(2 + j, (4, 128), BF16)
                for s in range(NSUB):
                    tr(pT[:, s, :], Kz[j][:, h, s * 128:(s + 1) * 128])
                yield
                cp("act" if j == 0 else "dve", Ktok[j][:, :, h * 128:(h + 1) * 128], pT)
                yield

        def ev_rglru(e, m, W3):
            R_, I_, T1, Gg, T2r = rA, rB, rC, rD, rE
            col = e * 4 + m
            pg, px = bank(4), bank(5)
            for c in range(8):
                mm(px, W3[:, c, 512 + m * 128:512 + (m + 1) * 128], hT[:, c, :], c == 0, c == 7)
            yield
            for c in range(8):
                mm(pg, W3[:, c, m * 128:(m + 1) * 128], hT[:, c, :], c == 0, c == 7)
            yield
            cp("pool", XB[:, 0:3], halo[:, col, :])
            cp("act", XB[:, 3:515], px)
            yield
            xc = T2r

            def cwc(j):
                k_ = SM_CONVW + e * 16 + j * 4 + m
                return small[:, k_:k_ + 1]
            ts("dve", xc, XB[:, 3:515], cwc(3), small[:, SM_CONVB + col:SM_CONVB + col + 1], ALU.mult, ALU.add)
            yield
            for k_ in (1, 2, 3):
                stt("dve", xc, XB[:, 3 - k_:515 - k_], cwc(3 - k_), xc, ALU.mult, ALU.add)
                yield
            cp("pool", halo[:, col, :], XB[:, 512:515])
            cp("act", xcb, xc)
            yield
            pr, pi = bank(6), bank(7)
            mm(pr, WBD[:, e * 8 + m, :], xcb, True, True)
            mm(pi, WBD[:, e * 8 + 4 + m, :], xcb, True, True)
            yield
            act(R_, pr, AF.Exp, bias=ecst[:, NBA + col:NBA + col + 1], scale=-1.0)
            yield
            act(I_, pi, AF.Exp, bias=ecst[:, NBX + col:NBX + col + 1], scale=-1.0)
            yield
            act(R_, R_, AF.Ln, bias=1.0)
            yield
            act(I_, I_, AF.Ln, bias=1.0)
            yield
            act(R_, R_, AF.Exp, scale=-1.0)
            yield
            act(I_, I_, AF.Exp, scale=-1.0)
            yield
            act(T1, R_, AF.Exp, scale=ecst[:, CL2 + col:CL2 + col + 1])
            tt("pool", I_, I_, xc, ALU.mult)
            yield
            ts("dve", T1, T1, -1.0, 1.0, ALU.mult, ALU.add)
            yield
            ts("dve", T1, T1, 1e-30, None, ALU.max)
            act(R_, R_, AF.Exp, scale=ecst[:, CL + col:CL + col + 1])
            yield
            act(T1, T1, AF.Ln)
            yield
            act(T1, T1, AF.Exp, scale=0.5)
            yield
            tt("dve", I_, I_, T1, ALU.mult)
            yield
            P.op("dve", lambda en: en.tensor_tensor_scan(out=T1.ap, data0=R_.ap, data1=I_.ap,
                                                         initial=hcar.ap[:, col:col + 1],
                                                         op0=ALU.mult, op1=ALU.add),
                 reads=[R_, I_, hcar], writes=[T1])
            yield
            cp("pool", hcar[:, col:col + 1], T1[:, T - 1:T])
            cp("act", Gg, pg)
            yield
            act(T2r, pg, AF.Square)
            yield
            ts("dve", T2r, T2r, 0.044715, 1.0, ALU.mult, ALU.add)
            yield
            tt("dve", T2r, T2r, Gg, ALU.mult)
            yield
            act(T2r, T2r, AF.Exp, scale=-2.0 * math.sqrt(2.0 / math.pi))
            yield
            act(T2r, T2r, AF.Ln, bias=1.0)
            yield
            act(T2r, T2r, AF.Exp, scale=-1.0)
            tt("pool", Gg, Gg, T1, ALU.mult)
            yield
            tt("dve", yT[:, 4 + m, :], Gg, T2r, ALU.mult)
            yield

        def ev_X(e, s):
            Lprev, Emu, Elast, Gt = (etab[:, i] for i in range(4))
            sl = slice(s * 128, (s + 1) * 128)
            Spe = Sp[e]
            pS = bank(6, (4, 128), F32)
            for h in range(4):
                for j in range(2):
                    mm(pS[:, h, :], Kz[j][:, h, sl], Qz[j][:, h, sl], j == 0, j == 1)
            yield
            pkv = [bank(0, (4, 128), F32), bank(1, (4, 128), F32)]
            for j in range(2):
                for h in range(4):
                    mm(pkv[j][:, h, :], Ktok[j][:, s, h * 128:(h + 1) * 128], Vtok[:, s, h * 128:(h + 1) * 128], True, True)
                yield
            tt("dve", ATb[s % 2], pS, bcm(bmask, [128, 4, 128]), ALU.mult)
            yield
            if s == 0:
                tt("dve", Spe, Spe, View(Emu.ap[:, :, 0:1].to_broadcast([128, 4, 128]), etab.toks), ALU.mult)
                yield
            cp("act", Spb[s % 2][0], Spe)
            yield
            for j in range(2):
                cch = 2 * s + j
                tt("dve", T2s, pkv[j], Spe, ALU.add)
                yield
                tt("dve", Spe, T2s, View(Gt.ap[:, :, cch:cch + 1].to_broadcast([128, 4, 128]), etab.toks), ALU.mult)
                yield
                if j == 0:
                    cp("act", Spb[s % 2][1], Spe)
                    yield

        def ev_Y(e, s, WO):
            sl = slice(s * 128, (s + 1) * 128)
            at_ = ATb[s % 2]
            po = bank(2 + (s % 2), (4, 128), F32)
            for h in range(4):
                mm(po[:, h, :], at_[:, h, :], Vtok[:, s, h * 128:(h + 1) * 128], True, False)
                mm(po[:, h, :], Qz[0][:, h, sl], Spb[s % 2][0][:, h, :], False, False)
                mm(po[:, h, :], Qz[1][:, h, sl], Spb[s % 2][1][:, h, :], False, True)
            yield
            act(sqe, po, AF.Square)
            yield
            P.op("dve", lambda en: en.tensor_reduce(out=est.ap[:, 0:4], in_=sqe.ap, axis=AX.X, op=ALU.add), reads=[sqe], writes=[est])
            yield
            rstd_of(est[:, 8:12], est[:, 4:8], est[:, 0:4], 1.0 / 128.0)
            yield
            tt("dve", tno, po, View(est.ap[:, 8:12].unsqueeze(2).to_broadcast([128, 4, 128]), est.toks), ALU.mult)
            yield
            tno2 = View(tno.ap.rearrange("p h d -> p (h d)"), tno.toks)
            tt("pool", tno2, tno2, gainA[e], ALU.mult)
            yield
            tt("dve", yat, tno2, Gs[:, s, :], ALU.mult)
            yield
            pT = bank(7, (4, 128), BF16)
            for h in range(4):
                tr(pT[:, h, :], yat[:, h * 128:(h + 1) * 128])
            cp("act", yT[:, 0:4, sl], pT)
            yield
            for hf in range(2):
                for c in range(8):
                    mm(bank(4 + hf), yT[:, c, sl], WO[:, c, hf * 512:(hf + 1) * 512], c == 0, c == 7)
                yield
            for hf in range(2):
                tt("dve", xs[s][:, hf * 512:(hf + 1) * 512], xs[s][:, hf * 512:(hf + 1) * 512], bank(4 + hf), ALU.add)
            yield

        def even_layer(l, it, prenormed):
            e = l // 2
            if not prenormed:
                norm_to_hT(nmix_d[l:l + 1, :])
            W1 = wload(wie_b, "wie", e, 0, 1024)
            W2 = wload(wie_b, "wie", e, 1024, 1024)
            W3 = wload(wie_b, "wie", e, 2048, 1024)
            for s in range(NSUB):
                sl = slice(s * 128, (s + 1) * 128)
                for n in range(2):
                    for c in range(8):
                        mm(bank(4 + n), hT[:, c, sl], W2[:, c, n * 512:(n + 1) * 512], c == 0, c == 7)
                cp("act", Vtok[:, s, :], bank(4))
                sgt = View(tno.ap.rearrange("p h d -> p (h d)"), tno.toks)
                sigm(sgt, bank(5))
                tt("dve", Gs[:, s, :], sgt, bank(5), ALU.mult)
            for j in range(2):
                for z_ in (Qz[j], Kz[j]):
                    zv = View(z_.ap.rearrange("p h (s j t) -> p h s j t", s=4, j=2)[:, :, :, 1 - j, :], z_.toks)
                    memset("pool", zv, 0.0)
            for h in range(4):
                if it == 0:
                    pump(n=2)
                interleave(ev_gates(e, h, W1), ev_rglru(e, h, W3))
            WO = wload(woe_b, "woe", e, 0, 1024)
            interleave(ev_X(e, 0))
            load_gain(nmlp_d[l:l + 1, :])
            for s in range(NSUB):
                if it == 0:
                    pump(n=2)
                interleave(ev_Y(e, s, WO), ev_X(e, s + 1) if s + 1 < NSUB else None,
                           norm_gen(s - 1) if s >= 1 else None)
            for _ in norm_gen(NSUB - 1):
                pass

        print("SBUF bytes used", cur[0], "of", SB_BYTES)
        if n_layers > 0:
            pump(upto_group=1)
        for it in range(n_tiles):
            for s in range(NSUB):
                r0 = it * T + s * 128
                dma("sp", xs[s].ap, x_d[r0:r0 + 128, :], [], [xs[s]])
            for li, l in enumerate(layers):
                gi = 2 * li
                if it == 0:
                    pump(upto_group=gi)
                pren = do_mixer and do_mlp and li > 0
                if do_mixer:
                    if l % 2 == 1:
                        odd_layer(l, it, pren)
                    else:
                        even_layer(l, it, pren)
                if do_mlp:
                    if it == 0:
                        pump(upto_group=gi + 1)
                    nxt = None
                    if do_mixer and li + 1 < len(layers):
                        nxt = nmix_d[layers[li + 1]:layers[li + 1] + 1, :]
                    mlp(l, it == 0, do_mixer, nxt)
            for s in range(NSUB):
                r0 = it * T + s * 128
                dma("sp", out_d[r0:r0 + 128, :], xs[s].ap, [xs[s]], [("D", "out", it, s)])
        P.emit()
    return nc


def _small_pack(inp):
    sm = np.zeros((128, NSM), np.float32)

    def fm(v):
        return np.ascontiguousarray(v.reshape(4, 128).T)

    for e in range(2):
        sm[:, SM_LBL + e * 4:SM_LBL + e * 4 + 4] = fm(inp["hgrn_lb_logits"][e])
        for j in range(4):
            sm[:, SM_CONVW + e * 16 + j * 4:SM_CONVW + e * 16 + j * 4 + 4] = fm(inp["conv_w"][e, j])
        sm[:, SM_CONVB + e * 4:SM_CONVB + e * 4 + 4] = fm(inp["conv_b"][e])
        sm[:, SM_BA + e * 4:SM_BA + e * 4 + 4] = fm(inp["rg_ba"][e])
        sm[:, SM_BX + e * 4:SM_BX + e * 4 + 4] = fm(inp["rg_bx"][e])
        sm[:, SM_LAM + e * 4:SM_LAM + e * 4 + 4] = fm(inp["rg_lambda"][e])
    return sm


_NC_CACHE = {}


def make_in_maps(inputs, n_cores=8):
    inp = {k: np.asarray(v) for k, v in inputs.items()}
    shared = {
        "norm_mix": inp["norm_mix"], "norm_mlp": inp["norm_mlp"],
        "w_mlp_in": inp["w_mlp_in"], "w_mlp_out": inp["w_mlp_out"],
        "w_in_even": inp["w_in_even"], "w_out_even": inp["w_out_even"],
        "w_in_odd": inp["w_in_odd"], "w_out_odd": inp["w_out_odd"],
        "small": _small_pack(inp), "hgrn_out_norm": inp["hgrn_out_norm"],
        "rg_wa": inp["rg_wa"], "rg_wx": inp["rg_wx"],
        "q_norm": inp["q_norm"], "k_norm": inp["k_norm"], "sinks": inp["sinks"],
    }
    shared = {k: np.ascontiguousarray(v, dtype=np.float32) for k, v in shared.items()}
    maps = []
    for b in range(n_cores):
        m = dict(shared)
        m["x"] = np.ascontiguousarray(inp["x"][b], dtype=np.float32)
        m["pos"] = np.ascontiguousarray(inp["positions"][b].reshape(SEQ // 128, 128).T.astype(np.int32))
        maps.append(m)
    return maps


def kernel(**inputs):
    key = "full"
    if key not in _NC_CACHE:
        _NC_CACHE[key] = build_nc()
    nc = _NC_CACHE[key]
    in_maps = make_in_maps(inputs, 8)
    res = run_bass_kernel_spmd(nc, in_maps, core_ids=list(range(8)))
    out = np.stack([np.asarray(r["out"]).reshape(SEQ, D) for r in res.results], axis=0)
    return out.astype(np.float32)
```
